# Optimizing a Trainium2 kernel written in Bass

```python
import jax, jax.numpy as jnp
from jax import lax
import numpy as np

D_MODEL = 1024
BATCH = 8
SEQ = 4096
DEPTH = 2

CHUNK = 64
Q_BLOCK = 128
EPS = 1e-6
HEAD_DIM = 64
FOX_HEADS = 8
FOX_W = FOX_HEADS * HEAD_DIM
SB_HEADS = 8
SB_W = SB_HEADS * HEAD_DIM
MLA_HEADS = 8
MLA_Q_RANK = 384
MLA_KV_RANK = 256
MLA_NOPE = 64
MLA_ROPE = 32
MLA_V = 64
MLA_W = MLA_HEADS * MLA_V
ROPE_BASE = 10000.0
N_BRANCH = 3
D_FF = -(-8 * D_MODEL // (3 * 256)) * 256

SPLIT_WIDTHS = (3 * FOX_W, FOX_HEADS, MLA_Q_RANK, MLA_KV_RANK, MLA_ROPE, 3 * SB_W, N_BRANCH * D_MODEL)
IN_WIDTH = int(sum(SPLIT_WIDTHS))
SPLIT_POINTS = [int(p) for p in np.cumsum(SPLIT_WIDTHS)[:-1]]

kernel_name = "hybrid_fox_mla_stickbreak_gated_encoder"


def rms_norm(x, g):
    xf = x.astype(jnp.float32)
    y = xf * lax.rsqrt(jnp.mean(xf * xf, axis=-1, keepdims=True) + EPS)
    return (y * g.astype(jnp.float32)).astype(x.dtype)


def rope_tables(positions, dim):
    half = dim // 2
    inv = ROPE_BASE ** (-jnp.arange(half, dtype=jnp.float32) / half)
    ang = positions.astype(jnp.float32)[..., None] * inv
    return jnp.cos(ang), jnp.sin(ang)


def apply_rope(x, cos, sin):
    x1, x2 = jnp.split(x.astype(jnp.float32), 2, axis=-1)
    out = jnp.concatenate([x1 * cos - x2 * sin, x1 * sin + x2 * cos], axis=-1)
    return out.astype(x.dtype)


def to_heads(t, n_heads):
    b, s, w = t.shape
    return t.reshape(b, s, n_heads, w // n_heads).transpose(0, 2, 1, 3)


def merge_heads(t):
    b, h, s, d = t.shape
    return t.transpose(0, 2, 1, 3).reshape(b, s, h * d)


def to_blocks(t):
    b, h, s = t.shape[:3]
    t = t.reshape((b, h, s // Q_BLOCK, Q_BLOCK) + t.shape[3:])
    return jnp.moveaxis(t, 2, 0)


def from_blocks(t):
    nb, b, h, qb = t.shape[:4]
    return jnp.moveaxis(t, 0, 2).reshape((b, h, nb * qb) + t.shape[4:])


def fox_attention(q, k, v, log_f):
    s = q.shape[2]
    scale = q.shape[-1] ** -0.5
    cum = jnp.cumsum(log_f, axis=-1)
    kpos = jnp.arange(s)
    starts = jnp.arange(s // Q_BLOCK) * Q_BLOCK

    def block(args):
        qb, fq, start = args
        qpos = start + jnp.arange(Q_BLOCK)
        logits = jnp.einsum('bhqd,bhkd->bhqk', qb, k, preferred_element_type=jnp.float32) * scale
        logits = logits + fq[..., None] - cum[..., None, :]
        logits = jnp.where(kpos[None, :] <= qpos[:, None], logits, -jnp.inf)
        p = jax.nn.softmax(logits, axis=-1)
        return jnp.einsum('bhqk,bhkd->bhqd', p.astype(v.dtype), v)

    return from_blocks(lax.map(block, (to_blocks(q), to_blocks(cum), starts)))


def mla_attention(q_nope, q_rope, k_nope, k_rope, v):
    s = q_nope.shape[2]
    scale = (MLA_NOPE + MLA_ROPE) ** -0.5
    kchunk = jnp.arange(s) // CHUNK
    starts = jnp.arange(s // Q_BLOCK) * Q_BLOCK

    def block(args):
        qn, qr, start = args
        qchunk = (start + jnp.arange(Q_BLOCK)) // CHUNK
        logits = (jnp.einsum('bhqn,bhkn->bhqk', qn, k_nope, preferred_element_type=jnp.float32)
                  + jnp.einsum('bhqr,bkr->bhqk', qr, k_rope, preferred_element_type=jnp.float32)) * scale
        logits = jnp.where(kchunk[None, :] <= qchunk[:, None], logits, -jnp.inf)
        p = jax.nn.softmax(logits, axis=-1)
        return jnp.einsum('bhqk,bhkv->bhqv', p.astype(v.dtype), v)

    return from_blocks(lax.map(block, (to_blocks(q_nope), to_blocks(q_rope), starts)))


def stick_breaking_attention(q, k, v):
    s = q.shape[2]
    scale = q.shape[-1] ** -0.5
    kpos = jnp.arange(s)
    starts = jnp.arange(s // Q_BLOCK) * Q_BLOCK

    def block(args):
        qb, start = args
        qpos = start + jnp.arange(Q_BLOCK)
        mask = kpos[None, :] < qpos[:, None]
        z = jnp.einsum('bhqd,bhkd->bhqk', qb, k, preferred_element_type=jnp.float32) * scale
        log_beta = jax.nn.log_sigmoid(z)
        log_keep = jnp.where(mask, jax.nn.log_sigmoid(-z), 0.0)
        suffix = lax.cumsum(log_keep, axis=3, reverse=True) - log_keep
        a = jnp.where(mask, jnp.exp(log_beta + suffix), 0.0)
        return jnp.einsum('bhqk,bhkd->bhqd', a.astype(v.dtype), v)

    return from_blocks(lax.map(block, (to_blocks(q), starts)))


def setup_inputs(seed: int = 0) -> dict:
    key = jax.random.key(seed)
    ks = jax.random.split(key, 24)

    def nrm(k, shape, fan_in, gain=1.0):
        return jax.random.normal(k, shape, jnp.float32) * (gain * fan_in ** -0.5)

    def gain(k, shape):
        return 1.0 + 0.1 * jax.random.normal(k, shape, jnp.float32)

    x = jax.random.normal(ks[0], (BATCH, SEQ, D_MODEL), jnp.float32)
    c = jax.random.normal(ks[1], (BATCH, D_MODEL), jnp.float32)
    offs = jax.random.randint(ks[2], (BATCH, 1), 0, 10000, dtype=jnp.int32)
    positions = jnp.arange(SEQ, dtype=jnp.int32)[None, :] + offs
    return {
        "x": x,
        "c": c,
        "positions": positions,
        "g_mix": gain(ks[3], (DEPTH, D_MODEL)),
        "w_ada": nrm(ks[4], (DEPTH, D_MODEL, 6 * D_MODEL), D_MODEL, 0.5),
        "b_ada": 0.01 * jax.random.normal(ks[5], (DEPTH, 6 * D_MODEL), jnp.float32),
        "w_in": nrm(ks[6], (DEPTH, D_MODEL, IN_WIDTH), D_MODEL),
        "b_fox_f": jax.random.uniform(ks[7], (DEPTH, FOX_HEADS), jnp.float32, 1.0, 5.0),
        "g_mla_q": gain(ks[8], (DEPTH, MLA_Q_RANK)),
        "w_mla_uq": nrm(ks[9], (DEPTH, MLA_Q_RANK, MLA_HEADS * (MLA_NOPE + MLA_ROPE)), MLA_Q_RANK),
        "g_mla_kv": gain(ks[10], (DEPTH, MLA_KV_RANK)),
        "w_mla_ukv": nrm(ks[11], (DEPTH, MLA_KV_RANK, MLA_HEADS * (MLA_NOPE + MLA_V)), MLA_KV_RANK),
        "w_o_fox": nrm(ks[12], (DEPTH, FOX_W, D_MODEL), FOX_W),
        "w_o_mla": nrm(ks[13], (DEPTH, MLA_W, D_MODEL), MLA_W),
        "w_o_sb": nrm(ks[14], (DEPTH, SB_W, D_MODEL), SB_W),
        "w_out": nrm(ks[15], (DEPTH, D_MODEL, D_MODEL), D_MODEL),
        "g_ffn": gain(ks[16], (DEPTH, D_MODEL)),
        "w_ffn_gate": nrm(ks[17], (DEPTH, D_MODEL, D_FF), D_MODEL),
        "w_ffn_up": nrm(ks[18], (DEPTH, D_MODEL, D_FF), D_MODEL),
        "w_ffn_down": nrm(ks[19], (DEPTH, D_FF, D_MODEL), D_FF),
        "g_final": gain(ks[20], (D_MODEL,)),
    }


def reference(x, c, positions, g_mix, w_ada, b_ada, w_in, b_fox_f, g_mla_q, w_mla_uq, g_mla_kv,
              w_mla_ukv, w_o_fox, w_o_mla, w_o_sb, w_out, g_ffn, w_ffn_gate, w_ffn_up, w_ffn_down,
              g_final):
    b, s, d = x.shape
    cos, sin = rope_tables(positions, MLA_ROPE)
    cond = jax.nn.silu(c)
    for l in range(DEPTH):
        mod = (cond @ w_ada[l] + b_ada[l]).reshape(b, 6, d)
        sh_a, sc_a, gt_a, sh_f, sc_f, gt_f = [mod[:, i, None, :] for i in range(6)]

        u = rms_norm(x, g_mix[l]) * (1 + sc_a) + sh_a
        proj = u @ w_in[l]
        fox_qkv, fox_f, mla_ql, mla_kvl, mla_kr, sb_qkv, gate_logit = jnp.split(proj, SPLIT_POINTS, axis=-1)

        fq, fk, fv = [to_heads(t, FOX_HEADS) for t in jnp.split(fox_qkv, 3, axis=-1)]
        log_f = jax.nn.log_sigmoid((fox_f + b_fox_f[l]).astype(jnp.float32)).transpose(0, 2, 1)
        y_fox = merge_heads(fox_attention(fq, fk, fv, log_f))

        cq = rms_norm(mla_ql, g_mla_q[l])
        q = to_heads(cq @ w_mla_uq[l], MLA_HEADS)
        q_nope = q[..., :MLA_NOPE]
        q_rope = apply_rope(q[..., MLA_NOPE:], cos[:, None], sin[:, None])
        ckv = rms_norm(mla_kvl, g_mla_kv[l])
        kv = to_heads(ckv @ w_mla_ukv[l], MLA_HEADS)
        k_nope, mla_v = kv[..., :MLA_NOPE], kv[..., MLA_NOPE:]
        k_rope = apply_rope(mla_kr, cos, sin)
        y_mla = merge_heads(mla_attention(q_nope, q_rope, k_nope, k_rope, mla_v))

        sq, sk, sv = [to_heads(t, SB_HEADS) for t in jnp.split(sb_qkv, 3, axis=-1)]
        y_sb = merge_heads(stick_breaking_attention(sq, sk, sv))

        g = jax.nn.sigmoid(gate_logit.astype(jnp.float32)).astype(x.dtype).reshape(b, s, N_BRANCH, d)
        merged = (g[:, :, 0] * (y_fox @ w_o_fox[l])
                  + g[:, :, 1] * (y_mla @ w_o_mla[l])
                  + g[:, :, 2] * (y_sb @ w_o_sb[l]))
        x = x + gt_a * (merged @ w_out[l])

        u = rms_norm(x, g_ffn[l]) * (1 + sc_f) + sh_f
        h = jax.nn.silu(u @ w_ffn_gate[l]) * (u @ w_ffn_up[l])
        x = x + gt_f * (h @ w_ffn_down[l])
    return rms_norm(x, g_final)
```

```python
import numpy as np
import ml_dtypes
import concourse.bass as bass
import concourse.mybir as mybir
from concourse.bass_utils import run_bass_kernel_spmd

F32, BF16, I32, U8 = mybir.dt.float32, mybir.dt.bfloat16, mybir.dt.int32, mybir.dt.uint8
AF = mybir.ActivationFunctionType
ALU = mybir.AluOpType

D = 1024
KC = 8
DFF = 2816
FC = 22
DEPTH = 2
EPS = 1e-6
OFF_FOXQ, OFF_FOXF, OFF_QL, OFF_KVL, OFF_KR, OFF_SB, OFF_GATE, INW = 0, 1536, 1544, 1928, 2184, 2216, 3752, 6824
NEG = -30000.0
TWO_PI = float(2.0 * np.pi)
PI = float(np.pi)
CW1 = 6.28125
CW2 = float(2.0 * np.pi - 6.28125)

C_IDENT, C_NTRI, C_NU, C_NTRI2, C_NONES, C_MFOX, C_MMLA, C_MSB, C_INV, C_SGN, C_END = (
    0, 128, 256, 384, 512, 640, 768, 896, 1024, 1025, 1026)


def make_consts():
    c = np.zeros((128, C_END), np.float32)
    i = np.arange(128)
    c[:, C_IDENT:C_IDENT + 128] = np.eye(128)
    j, s = i[:, None], i[None, :]
    c[:, C_NTRI:C_NTRI + 128] = -1.0 * (j >= s)
    c[:, C_NU:C_NU + 128] = -1.0 * (j < s)
    c[:, C_NTRI2:C_NTRI2 + 128] = -1.0 * (j <= s)
    c[:, C_NONES:C_NONES + 128] = -1.0
    k, q = i[:, None], i[None, :]
    c[:, C_MFOX:C_MFOX + 128] = NEG * (k > q)
    c[:, C_MMLA:C_MMLA + 128] = NEG * ((k >= 64) & (q < 64))
    c[:, C_MSB:C_MSB + 128] = NEG * (k >= q)
    inv = (np.float32(10000.0) ** (-(np.arange(16, dtype=np.float32) / np.float32(16)))).astype(np.float32)
    c[64:80, C_INV] = inv
    c[80:96, C_INV] = inv
    c[64:80, C_SGN] = -1.0
    c[80:96, C_SGN] = 1.0
    return c


class Res:
    __slots__ = ("name", "w", "r", "sem", "semv", "excl")

    def __init__(self, name, excl=False):
        self.name = name
        self.excl = excl
        self.w = None
        self.r = {}
        self.sem = None
        self.semv = 0


class Sched:
    def __init__(self, nc):
        self.nc = nc
        self.eng = {"pe": nc.tensor, "act": nc.scalar, "dve": nc.vector, "pool": nc.gpsimd, "sp": nc.sync}
        self.sem = {k: nc.alloc_semaphore("s_" + k) for k in ("pe", "act", "dve", "pool")}
        self.cnt = {k: 0 for k in self.sem}
        self.seen = {k: {} for k in self.eng}
        self.dma_res = []
        self.nwait = 0
        self.sem_pool = []
        self.nsem = 0

    def _wait(self, e, tok):
        if tok is None:
            return
        sem, val, src = tok
        if src == "pe" and e == "pe":
            return
        sid = id(sem)
        if self.seen[e].get(sid, 0) >= val:
            return
        self.eng[e].wait_ge(sem, val)
        self.nwait += 1
        self.seen[e][sid] = val

    def _deps(self, e, reads, writes):
        for r in reads:
            self._wait(e, r.w)
            if r.excl:
                for tok in r.r.values():
                    if tok[2] != e:
                        self._wait(e, tok)
        for w in writes:
            self._wait(e, w.w)
            for tok in w.r.values():
                if tok[2] == e and e != "sp":
                    continue
                self._wait(e, tok)

    def _post(self, tok, reads, writes):
        sid = id(tok[0])
        for r in reads:
            r.r[sid] = tok
        for w in writes:
            w.w = tok
            w.r = {}

    def op(self, e, reads, writes, fn):
        self._deps(e, reads, writes)
        ins = fn(self.eng[e])
        self.cnt[e] += 1
        ins.then_inc(self.sem[e], 1)
        tok = (self.sem[e], self.cnt[e], e)
        self._post(tok, reads, writes)

    def dma(self, out_ap, in_ap, sb, reads, writes, q="sp", ser=True):
        wr = list(writes)
        if sb not in wr and ser:
            wr_dep = wr + [sb]
        else:
            wr_dep = wr
        self._deps(q, reads, wr_dep)
        if sb.sem is None:
            if self.sem_pool:
                sb.sem, sb.semv = self.sem_pool.pop()
            else:
                sb.sem = self.nc.alloc_semaphore(f"d{self.nsem}")
                sb.semv = 0
                self.nsem += 1
            self.dma_res.append(sb)
        ins = self.eng[q].dma_start(out=out_ap, in_=in_ap)
        sb.semv += 16
        ins.then_inc(sb.sem, 16)
        tok = (sb.sem, sb.semv, "dma")
        self._post(tok, reads, wr)

    def barrier(self, engines=("pe", "act", "dve", "pool", "sp")):
        toks = [(self.sem[k], self.cnt[k], k) for k in self.sem if self.cnt[k] > 0]
        toks += [(r.sem, r.semv, "dma") for r in self.dma_res if r.semv > 0]
        for e in engines:
            for t in toks:
                self._wait(e, t)
        if len(engines) == 5:
            for r in self.dma_res:
                self.sem_pool.append((r.sem, r.semv))
                r.sem = None
            self.dma_res = []


class Arena:
    def __init__(self, nc, nbytes):
        self.t = nc.alloc_sbuf_tensor("arena", [128, nbytes], U8)
        self.n = nbytes
        self.top = 0
        self.peak = 0
        self.htop = nbytes

    def alloc(self, shape, dt, parts=128):
        esz = {F32: 4, BF16: 2, I32: 4}[dt]
        free = int(np.prod(shape[1:]))
        nb = (free * esz + 31) // 32 * 32
        off = self.top
        assert off + nb <= self.htop, f"arena overflow {off}+{nb}>{self.htop}"
        self.top += nb
        self.peak = max(self.peak, self.top)
        v = self.t[:, off:off + free * esz].bitcast(dt)
        if len(shape) == 3:
            v = v.rearrange("p (a b) -> p a b", b=shape[2])
        elif len(shape) == 4:
            v = v.rearrange("p (a b c) -> p a b c", b=shape[2], c=shape[3])
        if shape[0] < 128:
            v = v[0:shape[0]]
        return v

    def alloc_top(self, shape, dt):
        esz = {F32: 4, BF16: 2, I32: 4}[dt]
        free = int(np.prod(shape[1:]))
        nb = (free * esz + 31) // 32 * 32
        self.htop -= nb
        off = self.htop
        assert off >= self.top, "arena top/bottom collision"
        v = self.t[:, off:off + free * esz].bitcast(dt)
        if len(shape) == 3:
            v = v.rearrange("p (a b) -> p a b", b=shape[2])
        return v

    def mark(self):
        return self.top

    def release(self, m):
        self.top = m


def build(S, depth=DEPTH, branches=(0, 1, 2)):
    NT = S // 128
    NQ = S // 512
    nc = bass.Bass("TRN2", target_bir_lowering=False)

    def din(name, shape, dt=F32):
        return nc.dram_tensor(name, list(shape), dt, kind="ExternalInput").ap()

    x_in = din("x", [S, D])
    c_t = din("c_t", [128, 8])
    pos = din("pos", [1, S], I32)
    g_mix = din("g_mix", [DEPTH, D])
    w_ada = din("w_ada", [DEPTH, D, 6 * D])
    b_ada = din("b_ada", [DEPTH, 6 * D])
    w_in = din("w_in", [DEPTH, D, INW])
    b_fox_f = din("b_fox_f", [DEPTH, 8])
    g_mla_q = din("g_mla_q", [DEPTH, 384])
    w_mla_uq = din("w_mla_uq", [DEPTH, 384, 768])
    g_mla_kv = din("g_mla_kv", [DEPTH, 256])
    w_mla_ukv = din("w_mla_ukv", [DEPTH, 256, 1024])
    w_o = [din("w_o_fox", [DEPTH, 512, D]), din("w_o_mla", [DEPTH, 512, D]), din("w_o_sb", [DEPTH, 512, D])]
    w_out = din("w_out", [DEPTH, D, D])
    g_ffn = din("g_ffn", [DEPTH, D])
    w_gate = din("w_ffn_gate", [DEPTH, D, DFF])
    w_up = din("w_ffn_up", [DEPTH, D, DFF])
    w_down = din("w_ffn_down", [DEPTH, DFF, D])
    g_final = din("g_final", [1, D])
    consts = din("consts", [128, C_END])
    y = nc.dram_tensor("y", [S, D], F32, kind="ExternalOutput").ap()

    KQ = [70, 96, 64]
    modrow = nc.dram_tensor("modrow", [DEPTH, 6 * D], F32).ap()
    qs = [nc.dram_tensor(f"qs{b}", [8, KQ[b], S], BF16).ap() for b in range(3)]
    ks = [nc.dram_tensor(f"ks{b}", [8, KQ[b], S], BF16).ap() for b in range(3)]
    vs = [nc.dram_tensor(f"vs{b}", [8, 128, NT * 64], BF16).ap() for b in range(3)]
    ybr = nc.dram_tensor("ybr", [3, 512, S], BF16).ap()
    uts = nc.dram_tensor("uts", [128, KC, S], BF16).ap()

    sc = Sched(nc)
    A = Arena(nc, 207 * 1024)
    ps = [nc.alloc_psum_tensor(f"ps{i}", [128, 512], F32) for i in range(8)]
    PB = [Res(f"pb{i}", excl=True) for i in range(8)]

    def psf(i):
        return ps[i][:]

    def psb(i):
        return ps[i][:].bitcast(BF16)

    R_mod = [[Res(f"mod{l}_{n}") for n in range(12)] for l in range(DEPTH)]
    R_y = [Res(f"y{t}") for t in range(NT)]
    R_qs = [[Res(f"qs{b}_{h}") for h in range(8)] for b in range(3)]
    R_ks = [[Res(f"ks{b}_{h}") for h in range(8)] for b in range(3)]
    R_vs = [[Res(f"vs{b}_{h}") for h in range(8)] for b in range(3)]
    R_qF = Res("qF")
    R_kF = Res("kF")
    R_ybr = [[[Res(f"ybr{b}_{h}_{q}") for q in range(NQ)] for h in range(8)] for b in range(3)]
    R_none = Res("ext")
    R_uts = [Res(f"uts{q}") for q in range(NQ)]

    cst = A.alloc([128, C_END], F32)
    R_cst = Res("cst")
    cbf = A.alloc([128, 1024], BF16)
    R_cbf = Res("cbf")
    sc.dma(cst, consts, R_cst, [R_none], [R_cst])
    sc.op("dve", [R_cst], [R_cbf], lambda e: e.tensor_copy(out=cbf, in_=cst[:, 0:1024]))
    ident = cbf[:, C_IDENT:C_IDENT + 128]
    ntri = cbf[:, C_NTRI:C_NTRI + 128]
    nu = cbf[:, C_NU:C_NU + 128]
    masks = [cbf[:, C_MFOX:C_MFOX + 128], cbf[:, C_MMLA:C_MMLA + 128], cbf[:, C_MSB:C_MSB + 128]]
    ntri2_f = cst[:, C_NTRI2:C_NTRI2 + 128]
    nones_f = cst[:, C_NONES:C_NONES + 128]

    uT = A.alloc([128, KC, S], BF16)
    R_uT = [Res(f"uT{q}") for q in range(NQ)]
    persist_mark = A.mark()

    cast_ctr = [0]

    def cast(dst, src, reads, writes):
        cast_ctr[0] += 1
        if cast_ctr[0] % 2 == 0:
            sc.op("dve", reads, writes, lambda e: e.tensor_copy(out=dst, in_=src))
        else:
            sc.op("act", reads, writes, lambda e: e.copy(out=dst, in_=src))

    def bcast_load(dst, row_ap, Rdst, reads):
        sc.dma(dst, row_ap.partition_broadcast(128), Rdst, reads, [Rdst])

    def phase_mod(l):
        if l in mod_done:
            return
        m = A.mark()
        ct = A.alloc([128, 8], F32)
        cond = A.alloc([128, 8], F32)
        R_ct, R_cond = Res("ct"), Res("cond")
        wst = [A.alloc([128, KC, 512], F32) for _ in range(2)]
        R_wst = [Res(f"wst{i}") for i in range(2)]
        brow = [A.alloc([1, 512], F32) for _ in range(2)]
        R_brow = [Res(f"brow{i}") for i in range(2)]
        mrow = [A.alloc([1, 512], F32) for _ in range(2)]
        R_mrow = [Res(f"mrow{i}") for i in range(2)]
        sc.dma(ct, c_t, R_ct, [R_none], [R_ct])
        sc.op("act", [R_ct], [R_cond], lambda e: e.activation(out=cond, in_=ct, func=AF.Silu))
        wv = w_ada[l].rearrange("(kc p) n -> p kc n", p=128)
        for n in range(12):
            s_ = n % 2
            sc.dma(wst[s_], wv[:, :, n * 512:(n + 1) * 512], R_wst[s_], [R_none], [R_wst[s_]])
            sc.dma(brow[s_][0:1, :], b_ada[l:l + 1, n * 512:(n + 1) * 512], R_brow[s_], [R_none], [R_brow[s_]])
            pb = n % 2

            def mm(e, s_=s_, pb=pb):
                for kc in range(KC):
                    ins = e.matmul(psf(pb)[0:1, :], cond[:, kc:kc + 1], wst[s_][:, kc, :],
                                   start=(kc == 0), stop=(kc == KC - 1))
                return ins
            sc.op("pe", [R_cond, R_wst[s_]], [PB[pb]], mm)
            sc.op("dve", [PB[pb], R_brow[s_]], [R_mrow[s_]],
                  lambda e, s_=s_, pb=pb: e.tensor_tensor(out=mrow[s_][0:1, :], in0=psf(pb)[0:1, :],
                                                          in1=brow[s_][0:1, :], op=ALU.add))
            sc.dma(modrow[l:l + 1, n * 512:(n + 1) * 512], mrow[s_][0:1, :], R_mrow[s_], [R_mrow[s_]], [R_mod[l][n]])
        A.release(m)
        sc.barrier()

    def load_mod_tiles(l, i_sh, i_sc, grow):
        G = A.alloc([128, D], F32)
        SH = A.alloc([128, D], F32)
        GM = A.alloc([128, D], F32)
        R_G, R_SH, R_GM = Res("G"), Res("SH"), Res("GM")
        bcast_load(G, modrow[l:l + 1, i_sc * D:(i_sc + 1) * D], R_G, [R_mod[l][2 * i_sc], R_mod[l][2 * i_sc + 1]])
        bcast_load(SH, modrow[l:l + 1, i_sh * D:(i_sh + 1) * D], R_SH, [R_mod[l][2 * i_sh], R_mod[l][2 * i_sh + 1]])
        bcast_load(GM, grow, R_GM, [R_none])
        sc.op("dve", [R_G, R_GM], [R_G],
              lambda e: e.scalar_tensor_tensor(out=G, in0=G, scalar=1.0, in1=GM, op0=ALU.add, op1=ALU.mult))
        return G, SH, R_G, R_SH

    class NormBufs:
        def __init__(self, n=2, ntmp=2):
            self.n = n
            self.xt = [A.alloc([128, D], F32) for _ in range(n)]
            self.R_xt = [Res(f"xt{i}") for i in range(n)]
            self.tmp = [A.alloc([128, D], F32) for _ in range(ntmp)]
            self.R_tmp = [Res(f"ntmp{i}") for i in range(ntmp)]
            self.u = [A.alloc([128, D], BF16) for _ in range(n)]
            self.R_u = [Res(f"u{i}") for i in range(n)]
            self.junk = A.alloc([128, D], BF16)
            self.R_junk = Res("junk")
            self.ss = [A.alloc([128, 2], F32) for _ in range(n)]
            self.R_ss = [Res(f"ss{i}") for i in range(n)]

    def norm_tile(nb, s_, G, SH, R_G, R_SH):
        ti = s_ % len(nb.tmp)
        xt, ss, tmp, u = nb.xt[s_], nb.ss[s_], nb.tmp[ti], nb.u[s_]
        Rx, Rss, Rt, Ru = nb.R_xt[s_], nb.R_ss[s_], nb.R_tmp[ti], nb.R_u[s_]
        sc.op("act", [Rx], [nb.R_junk, Rss],
              lambda e: e.activation(out=nb.junk, in_=xt, func=AF.Square, accum_out=ss[:, 0:1]))
        sc.op("dve", [Rss], [Rss], lambda e: e.tensor_scalar(out=ss[:, 0:1], in0=ss[:, 0:1], scalar1=1.0 / D,
                                                             scalar2=EPS, op0=ALU.mult, op1=ALU.add))
        sc.op("act", [Rss], [Rss], lambda e: e.activation(out=ss[:, 0:1], in_=ss[:, 0:1], func=AF.Ln))
        sc.op("act", [Rss], [Rss], lambda e: e.activation(out=ss[:, 0:1], in_=ss[:, 0:1], func=AF.Exp, scale=-0.5))
        sc.op("dve", [Rx, Rss, R_G], [Rt],
              lambda e: e.scalar_tensor_tensor(out=tmp, in0=xt, scalar=ss[:, 0:1], in1=G, op0=ALU.mult, op1=ALU.mult))
        sc.op("dve", [Rt, R_SH], [Ru], lambda e: e.tensor_tensor(out=u, in0=tmp, in1=SH, op=ALU.add))

    def transpose_tile(u, Ru, pb, dst3, Rdst, eng):
        def tr(e):
            for kc in range(KC):
                ins = e.transpose(psb(pb)[:, kc * 128:(kc + 1) * 128], u[:, kc * 128:(kc + 1) * 128], ident)
            return ins
        sc.op("pe", [Ru, R_cbf], [PB[pb]], tr)
        src = psb(pb).rearrange("p (a b) -> p a b", b=128)
        if eng == "act":
            sc.op("act", [PB[pb]], [Rdst], lambda e: e.copy(out=dst3, in_=src))
        else:
            sc.op("dve", [PB[pb]], [Rdst], lambda e: e.tensor_copy(out=dst3, in_=src))

    def phase_norm_attn(l):
        m = A.mark()
        G, SH, R_G, R_SH = load_mod_tiles(l, 0, 1, g_mix[l:l + 1, :])
        nb = NormBufs(2)
        for t in range(NT):
            s_ = t % 2
            src = x_in if l == 0 else y
            sc.dma(nb.xt[s_], src[t * 128:(t + 1) * 128, :], nb.R_xt[s_], [R_none if l == 0 else R_y[t]], [nb.R_xt[s_]])
            norm_tile(nb, s_, G, SH, R_G, R_SH)
            transpose_tile(nb.u[s_], nb.R_u[s_], t % 2, uT[:, :, t * 128:(t + 1) * 128], R_uT[t // 4],
                           "act" if t % 2 == 0 else "dve")
        for q in range(NQ):
            sc.dma(uts[:, :, q * 512:(q + 1) * 512], uT[:, :, q * 512:(q + 1) * 512], R_uT[q], [R_uT[q]], [R_uts[q]])
        A.release(m)
        sc.barrier()

    def load_w_cols(l, wsrc_view, c0, ncols, stg, R_stg, dst, R_dst, kcs=KC):
        sc.dma(stg[:, 0:kcs, 0:ncols], wsrc_view[:, :, c0:c0 + ncols], R_stg, [R_none], [R_stg])
        sc.op("pool", [R_stg], [R_dst], lambda e: e.tensor_copy(out=dst[:, 0:kcs, 0:ncols], in_=stg[:, 0:kcs, 0:ncols]))

    def proj_featmajor(w3, Rw, c0, M, kcs, rhs_fn, R_rhs, pb, nkc_rows=None):
        def mm(e):
            for kc in range(kcs):
                ins = e.matmul(psf(pb)[0:M, :], w3[:, kc, c0:c0 + M], rhs_fn(kc), start=(kc == 0), stop=(kc == kcs - 1))
            return ins
        sc.op("pe", [Rw] + R_rhs, [PB[pb]], mm)

    def phase_proj_qkv(l, b, colbase):
        m = A.mark()
        wv = w_in[l].rearrange("(kc p) n -> p kc n", p=128)
        stg = [A.alloc([128, KC, 384], F32) for _ in range(2)]
        R_stg = [Res(f"stg{i}") for i in range(2)]
        wb = [A.alloc([128, KC, 384], BF16) for _ in range(2)]
        R_wb = [Res(f"wb{i}") for i in range(2)]
        R_stg3 = [[Res(f"stg{i}_{j}") for j in range(3)] for i in range(2)]
        qT2 = [A.alloc([128, S], BF16) for _ in range(2)]
        kT2 = [A.alloc([128, S], BF16) for _ in range(2)]
        vsb2 = [A.alloc([128, NT, 128], BF16) for _ in range(2)]
        R_qT2 = [Res(f"qT{i}") for i in range(2)]
        R_kT2 = [Res(f"kT{i}") for i in range(2)]
        R_vsb2 = [Res(f"vsb{i}") for i in range(2)]
        for hp in range(4):
            s_ = hp % 2
            qT, kT, vsb = qT2[s_], kT2[s_], vsb2[s_]
            R_qT, R_kT, R_vsb = R_qT2[s_], R_kT2[s_], R_vsb2[s_]
            for i in range(3):
                c0 = colbase + i * 512 + hp * 128
                sc.dma(stg[s_][:, :, i * 128:(i + 1) * 128], wv[:, :, c0:c0 + 128], R_stg3[s_][i], [R_none], [R_stg3[s_][i]])
            cast(wb[s_], stg[s_], R_stg3[s_], [R_wb[s_]] + R_stg3[s_])
            cnt = 0
            for i, (dstT, R_d) in enumerate(((qT, R_qT), (kT, R_kT))):
                for tc in range(NQ):
                    pb = cnt % 2
                    cnt += 1
                    proj_featmajor(wb[s_], R_wb[s_], i * 128, 128, KC,
                                   lambda kc, tc=tc: uT[:, kc, tc * 512:(tc + 1) * 512], [R_uT[tc]], pb)
                    if cnt % 2 == 0:
                        sc.op("act", [PB[pb]], [R_d], lambda e, pb=pb, dstT=dstT, tc=tc: e.copy(
                            out=dstT[:, tc * 512:(tc + 1) * 512], in_=psf(pb)))
                    else:
                        sc.op("dve", [PB[pb]], [R_d], lambda e, pb=pb, dstT=dstT, tc=tc: e.tensor_copy(
                            out=dstT[:, tc * 512:(tc + 1) * 512], in_=psf(pb)))
            for tg in range(NT // 4):
                pb = 2 + tg % 2

                def mmv(e, tg=tg, pb=pb, s_=s_):
                    for j in range(4):
                        t = tg * 4 + j
                        for kc in range(KC):
                            ins = e.matmul(psf(pb)[:, j * 128:(j + 1) * 128], uT[:, kc, t * 128:(t + 1) * 128],
                                           wb[s_][:, kc, 256:384], start=(kc == 0), stop=(kc == KC - 1))
                    return ins
                sc.op("pe", [R_wb[s_], R_uT[tg]], [PB[pb]], mmv)
                sc.op("dve" if tg % 2 == 0 else "act", [PB[pb]], [R_vsb],
                      (lambda e, tg=tg, pb=pb: e.tensor_copy(
                          out=vsb[:, tg * 4:(tg + 1) * 4, :], in_=psf(pb).rearrange("p (a b) -> p a b", b=128)))
                      if tg % 2 == 0 else
                      (lambda e, tg=tg, pb=pb: e.copy(
                          out=vsb[:, tg * 4:(tg + 1) * 4, :], in_=psf(pb).rearrange("p (a b) -> p a b", b=128))))
            for hh in range(2):
                h = 2 * hp + hh
                sc.dma(qs[b][h, 0:64, :], qT[hh * 64:(hh + 1) * 64, :], R_qT, [R_qT], [R_qs[b][h]], ser=False)
                sc.dma(ks[b][h, 0:64, :], kT[hh * 64:(hh + 1) * 64, :], R_kT, [R_kT], [R_ks[b][h]], ser=False)
                sc.dma(vs[b][h].rearrange("p (t d) -> p t d", d=64), vsb[:, :, hh * 64:(hh + 1) * 64], R_vsb,
                       [R_vsb], [R_vs[b][h]], ser=False)
        A.release(m)
        sc.barrier()

    def phase_fox_F(l):
        m = A.mark()
        wv = w_in[l].rearrange("(kc p) n -> p kc n", p=128)
        stg = A.alloc([128, KC, 8], F32)
        wF = A.alloc([128, KC, 8], BF16)
        R_stg, R_wF = Res("stgF"), Res("wF")
        bF = A.alloc([128, 8], F32)
        R_bF = Res("bF")
        lf = A.alloc([128, NT, 8], F32)
        R_lf = Res("lf")
        sc.dma(stg, wv[:, :, OFF_FOXF:OFF_FOXF + 8], R_stg, [R_none], [R_stg])
        cast(wF, stg, [R_stg], [R_wF])
        bcast_load(bF, b_fox_f[l:l + 1, :], R_bF, [R_none])

        def mmf(e):
            for t in range(NT):
                for kc in range(KC):
                    ins = e.matmul(psf(0)[:, t * 8:(t + 1) * 8], uT[:, kc, t * 128:(t + 1) * 128], wF[:, kc, :],
                                   start=(kc == 0), stop=(kc == KC - 1))
            return ins
        sc.op("pe", [R_wF] + R_uT, [PB[0]], mmf)
        for t in range(NT):
            sc.op("dve", [PB[0], R_bF], [R_lf], lambda e, t=t: e.tensor_tensor(
                out=lf[:, t, :], in0=psf(0)[:, t * 8:(t + 1) * 8], in1=bF, op=ALU.add))
        lf2 = lf.rearrange("p t h -> p (t h)")
        sc.op("act", [R_lf], [R_lf], lambda e: e.activation(out=lf2, in_=lf2, func=AF.Exp, scale=-1.0))
        sc.op("act", [R_lf], [R_lf], lambda e: e.activation(out=lf2, in_=lf2, func=AF.Ln, bias=1.0, scale=1.0))
        f8 = [A.alloc([8, 512], F32) for _ in range(2)]
        r1 = [A.alloc([8, 512], F32) for _ in range(2)]
        fall = [A.alloc([8, 3, 512], BF16) for _ in range(2)]
        nfall = [A.alloc([8, 3, 512], BF16) for _ in range(2)]
        R_f8 = [Res(f"f8{i}") for i in range(2)]
        R_r1 = [Res(f"r1{i}") for i in range(2)]
        R_fall = [Res(f"fall{i}") for i in range(2)]
        R_nfall = [Res(f"nfall{i}") for i in range(2)]
        for qc in range(NQ):
            s_ = qc % 2
            pb = 1 + qc % 2

            def mmc(e, qc=qc, pb=pb):
                for j in range(4):
                    ti = qc * 4 + j
                    for tj in range(ti + 1):
                        ins = e.matmul(psf(pb)[0:8, j * 128:(j + 1) * 128], lf[:, tj, :],
                                       ntri2_f if tj == ti else nones_f, start=(tj == 0), stop=(tj == ti))
                return ins
            sc.op("pe", [R_lf, R_cst], [PB[pb]], mmc)
            f, r, fa, nfa = f8[s_], r1[s_], fall[s_], nfall[s_]
            Rf, Rr, Rfa, Rnfa = R_f8[s_], R_r1[s_], R_fall[s_], R_nfall[s_]
            sc.op("dve", [PB[pb]], [Rf], lambda e, f=f, pb=pb: e.tensor_scalar(
                out=f, in0=psf(pb)[0:8, :], scalar1=8.0, scalar2=None, op0=ALU.mult))
            sc.op("dve", [Rf], [Rfa], lambda e, f=f, fa=fa: e.tensor_copy(out=fa[:, 0, :], in_=f))
            sc.op("dve", [Rf, Rfa], [Rr], lambda e, f=f, fa=fa, r=r: e.tensor_tensor(
                out=r, in0=f, in1=fa[:, 0, :], op=ALU.subtract))
            sc.op("dve", [Rr], [Rfa], lambda e, fa=fa, r=r: e.tensor_copy(out=fa[:, 1, :], in_=r))
            sc.op("dve", [Rr, Rfa], [Rr], lambda e, fa=fa, r=r: e.tensor_tensor(
                out=r, in0=r, in1=fa[:, 1, :], op=ALU.subtract))
            sc.op("dve", [Rr], [Rfa], lambda e, fa=fa, r=r: e.tensor_copy(out=fa[:, 2, :], in_=r))
            sc.op("dve", [Rfa], [Rnfa], lambda e, fa=fa, nfa=nfa: e.tensor_scalar(
                out=nfa, in0=fa, scalar1=-1.0, scalar2=None, op0=ALU.mult))
            sc.dma(qs[0][:, 64:67, qc * 512:(qc + 1) * 512], fa, Rfa, [Rfa], [R_qF])
            sc.dma(ks[0][:, 67:70, qc * 512:(qc + 1) * 512], nfa, Rnfa, [Rnfa], [R_kF])
        A.release(m)
        sc.barrier()

    def phase_proj_mla(l):
        m = A.mark()
        b = 1
        wv = w_in[l].rearrange("(kc p) n -> p kc n", p=128)
        stg = A.alloc([128, KC, 672], F32)
        R_stg = Res("stgm")
        wl = A.alloc([128, KC, 640], BF16)
        wkr = A.alloc([128, KC, 96], BF16)
        wkrs = A.alloc([128, KC, 96], BF16)
        R_wl, R_wkr = Res("wl"), Res("wkr")
        sc.dma(stg, wv[:, :, OFF_QL:OFF_QL + 672], R_stg, [R_none], [R_stg])
        cast(wl, stg[:, :, 0:640], [R_stg], [R_wl])
        sc.op("pool", [R_stg], [R_wkr], lambda e: e.tensor_copy(out=wkr, in_=stg[:, :, 576:672]))
        sc.op("pool", [R_stg], [R_wkr], lambda e: e.tensor_copy(out=wkrs[:, :, 0:64], in_=stg[:, :, 576:640]))
        sc.op("pool", [R_stg], [R_wkr], lambda e: e.tensor_copy(out=wkrs[:, :, 64:80], in_=stg[:, :, 656:672]))
        sc.op("pool", [R_stg], [R_wkr], lambda e: e.tensor_copy(out=wkrs[:, :, 80:96], in_=stg[:, :, 640:656]))
        stq = A.alloc([128, 3, 768], F32)
        R_stq = Res("stq")
        wq = A.alloc([128, 3, 768], BF16)
        wqs = A.alloc([128, 3, 768], BF16)
        R_wq = Res("wq")
        sc.dma(stq, w_mla_uq[l].rearrange("(kc p) n -> p kc n", p=128), R_stq, [R_none], [R_stq])
        cast(wq, stq, [R_stq], [R_wq])
        stq4 = stq.rearrange("p k (h c) -> p k h c", c=96)
        wqs4 = wqs.rearrange("p k (h c) -> p k h c", c=96)
        for kc in range(3):
            sc.op("pool", [R_stq], [R_wq], lambda e, kc=kc: e.tensor_copy(out=wqs4[:, kc, :, 0:64], in_=stq4[:, kc, :, 0:64]))
            sc.op("pool", [R_stq], [R_wq], lambda e, kc=kc: e.tensor_copy(out=wqs4[:, kc, :, 64:80], in_=stq4[:, kc, :, 80:96]))
            sc.op("pool", [R_stq], [R_wq], lambda e, kc=kc: e.tensor_copy(out=wqs4[:, kc, :, 80:96], in_=stq4[:, kc, :, 64:80]))
        stkv = A.alloc([128, 2, 1024], F32)
        R_stkv = Res("stkv")
        wkv = A.alloc([128, 2, 1024], BF16)
        R_wkv = Res("wkv")
        sc.dma(stkv, w_mla_ukv[l].rearrange("(kc p) n -> p kc n", p=128), R_stkv, [R_none], [R_stkv])
        cast(wkv, stkv, [R_stkv], [R_wkv])
        wkk = A.alloc([128, 2, 512], BF16)
        wvv = A.alloc([128, 2, 512], BF16)
        stkv4 = stkv.rearrange("p k (h c) -> p k h c", c=128)
        for kc in range(2):
            sc.op("pool", [R_stkv], [R_wkv], lambda e, kc=kc: e.tensor_copy(
                out=wkk[:, kc, :].rearrange("p (h c) -> p h c", c=64), in_=stkv4[:, kc, :, 0:64]))
            sc.op("pool", [R_stkv], [R_wkv], lambda e, kc=kc: e.tensor_copy(
                out=wvv[:, kc, :].rearrange("p (h c) -> p h c", c=64), in_=stkv4[:, kc, :, 64:128]))
        gq = A.alloc([128, 384], F32)
        gkv = A.alloc([128, 256], F32)
        R_gq, R_gkv = Res("gq"), Res("gkv")
        bcast_load(gq, g_mla_q[l:l + 1, :], R_gq, [R_none])
        bcast_load(gkv, g_mla_kv[l:l + 1, :], R_gkv, [R_none])
        posi = A.alloc([96, 512], I32)
        ang = A.alloc([96, 512], F32)
        cs = A.alloc([96, 512], F32)
        ss_ = A.alloc([96, 512], F32)
        R_posi, R_ang, R_cs, R_ss = Res("posi"), Res("ang"), Res("cs"), Res("ssn")
        cqT = A.alloc([128, 3, 512], BF16)
        ckvT = A.alloc([128, 2, 512], BF16)
        R_cqT, R_ckvT = Res("cqT"), Res("ckvT")
        lat = [A.alloc([128, 640], F32) for _ in range(2)]
        R_lat = [Res(f"lat{i}") for i in range(2)]
        latb = [A.alloc([128, 640], BF16) for _ in range(2)]
        R_latb = [Res(f"latb{i}") for i in range(2)]
        st2 = [A.alloc([128, 2], F32) for _ in range(2)]
        R_st2 = [Res(f"st2{i}") for i in range(2)]
        junk = A.alloc([128, 384], BF16)
        R_junk = Res("junkm")
        krope = A.alloc([96, 512], BF16)
        R_krope = Res("krope")
        t1 = [A.alloc([96, 512], F32) for _ in range(2)]
        t2 = [A.alloc([96, 512], F32) for _ in range(2)]
        R_t1 = [Res(f"t1{i}") for i in range(2)]
        R_t2 = [Res(f"t2{i}") for i in range(2)]
        qh = [A.alloc([96, 512], BF16) for _ in range(2)]
        kh = [A.alloc([96, 512], BF16) for _ in range(2)]
        R_qh = [Res(f"qh{i}") for i in range(2)]
        R_kh = [Res(f"kh{i}") for i in range(2)]
        vsb = A.alloc([128, 4, 512], BF16)
        R_vsb = Res("vsbm")
        inv_c = cst[64:96, C_INV:C_INV + 1]
        sgn_c = cst[64:96, C_SGN:C_SGN + 1]
        P = slice(64, 96)
        import os as _os
        kmla = int(_os.environ.get("KMLA", "99"))

        def _bail():
            A.release(m)
            sc.barrier()
        if kmla <= 1:
            return _bail()
        for qc in range(NQ):
            cols = slice(qc * 512, (qc + 1) * 512)
            sc.dma(posi[P, :], pos[0:1, cols].partition_broadcast(32), R_posi, [R_none], [R_posi])
            sc.op("dve", [R_posi], [R_ang], lambda e: e.tensor_copy(out=ang[P, :], in_=posi[P, :]))
            sc.op("dve", [R_ang, R_cst], [R_ang], lambda e: e.tensor_scalar(
                out=ang[P, :], in0=ang[P, :], scalar1=inv_c, scalar2=None, op0=ALU.mult))
            sc.op("dve", [R_ang], [R_cs], lambda e: e.tensor_scalar(
                out=cs[P, :], in0=ang[P, :], scalar1=1.0 / TWO_PI, scalar2=None, op0=ALU.mult))
            sc.op("dve", [R_cs], [R_posi], lambda e: e.tensor_copy(out=posi[P, :], in_=cs[P, :]))
            sc.op("dve", [R_posi], [R_cs], lambda e: e.tensor_copy(out=cs[P, :], in_=posi[P, :]))
            sc.op("dve", [R_cs, R_ang], [R_ang], lambda e: e.scalar_tensor_tensor(
                out=ang[P, :], in0=cs[P, :], scalar=-CW1, in1=ang[P, :], op0=ALU.mult, op1=ALU.add))
            sc.op("dve", [R_cs, R_ang], [R_ang], lambda e: e.scalar_tensor_tensor(
                out=ang[P, :], in0=cs[P, :], scalar=-CW2, in1=ang[P, :], op0=ALU.mult, op1=ALU.add))
            sc.op("dve", [R_ang], [R_cs], lambda e: e.tensor_scalar(
                out=cs[P, :], in0=ang[P, :], scalar1=PI, scalar2=TWO_PI, op0=ALU.is_gt, op1=ALU.mult))
            sc.op("dve", [R_cs, R_ang], [R_ang], lambda e: e.tensor_tensor(
                out=ang[P, :], in0=ang[P, :], in1=cs[P, :], op=ALU.subtract))
            sc.op("dve", [R_ang], [R_cs], lambda e: e.tensor_scalar(
                out=cs[P, :], in0=ang[P, :], scalar1=-PI, scalar2=TWO_PI, op0=ALU.is_lt, op1=ALU.mult))
            sc.op("dve", [R_cs, R_ang], [R_ang], lambda e: e.tensor_tensor(
                out=ang[P, :], in0=ang[P, :], in1=cs[P, :], op=ALU.add))
            sc.op("dve", [R_ang], [R_cs], lambda e: e.tensor_scalar(
                out=cs[P, :], in0=ang[P, :], scalar1=PI / 2, scalar2=None, op0=ALU.add))
            sc.op("dve", [R_cs], [R_ss], lambda e: e.tensor_scalar(
                out=ss_[P, :], in0=cs[P, :], scalar1=PI, scalar2=TWO_PI, op0=ALU.is_gt, op1=ALU.mult))
            sc.op("dve", [R_cs, R_ss], [R_cs], lambda e: e.tensor_tensor(
                out=cs[P, :], in0=cs[P, :], in1=ss_[P, :], op=ALU.subtract))
            sc.op("act", [R_cs], [R_cs], lambda e: e.activation(out=cs[P, :], in_=cs[P, :], func=AF.Sin))
            sc.op("act", [R_ang], [R_ss], lambda e: e.activation(out=ss_[P, :], in_=ang[P, :], func=AF.Sin))
            sc.op("dve", [R_ss, R_cst], [R_ss], lambda e: e.tensor_scalar(
                out=ss_[P, :], in0=ss_[P, :], scalar1=sgn_c, scalar2=None, op0=ALU.mult))
            if kmla <= 2:
                return _bail()
            def lat_A(j):
                    t = qc * 4 + j
                    s_ = j % 2
                    tok = slice(t * 128, (t + 1) * 128)
                    pA, pB = (0, 1) if j % 2 == 0 else (6, 7)

                    def mml(e, tok=tok, pA=pA, pB=pB):
                        for kc in range(KC):
                            e.matmul(psf(pA)[:, 0:384], uT[:, kc, tok], wl[:, kc, 0:384], start=(kc == 0), stop=(kc == KC - 1))
                        for kc in range(KC):
                            ins = e.matmul(psf(pB)[:, 0:256], uT[:, kc, tok], wl[:, kc, 384:640], start=(kc == 0), stop=(kc == KC - 1))
                        return ins
                    sc.op("pe", [R_wl, R_uT[qc]], [PB[pA], PB[pB]], mml)

            def lat_B(j):
                    t = qc * 4 + j
                    s_ = j % 2
                    tok = slice(t * 128, (t + 1) * 128)
                    pA, pB = (0, 1) if j % 2 == 0 else (6, 7)
                    la, lb, st, Rla, Rlb, Rst = lat[s_], latb[s_], st2[s_], R_lat[s_], R_latb[s_], R_st2[s_]
                    sc.op("act", [PB[pA]], [R_junk, Rst], lambda e, st=st: e.activation(
                        out=junk[:, 0:384], in_=psf(pA)[:, 0:384], func=AF.Square, accum_out=st[:, 0:1]))
                    sc.op("act", [PB[pB]], [R_junk, Rst], lambda e, st=st: e.activation(
                        out=junk[:, 0:256], in_=psf(pB)[:, 0:256], func=AF.Square, accum_out=st[:, 1:2]))
                    sc.op("dve", [Rst], [Rst], lambda e, st=st: e.tensor_scalar(
                        out=st[:, 0:1], in0=st[:, 0:1], scalar1=1.0 / 384, scalar2=EPS, op0=ALU.mult, op1=ALU.add))
                    sc.op("dve", [Rst], [Rst], lambda e, st=st: e.tensor_scalar(
                        out=st[:, 1:2], in0=st[:, 1:2], scalar1=1.0 / 256, scalar2=EPS, op0=ALU.mult, op1=ALU.add))
                    sc.op("act", [Rst], [Rst], lambda e, st=st: e.activation(out=st, in_=st, func=AF.Ln))
                    sc.op("act", [Rst], [Rst], lambda e, st=st: e.activation(out=st, in_=st, func=AF.Exp, scale=-0.5))
                    sc.op("dve", [PB[pA], Rst, R_gq], [Rlb], lambda e, st=st, lb=lb: e.scalar_tensor_tensor(
                        out=lb[:, 0:384], in0=psf(pA)[:, 0:384], scalar=st[:, 0:1], in1=gq, op0=ALU.mult, op1=ALU.mult))
                    sc.op("dve", [PB[pB], Rst, R_gkv], [Rlb], lambda e, st=st, lb=lb: e.scalar_tensor_tensor(
                        out=lb[:, 384:640], in0=psf(pB)[:, 0:256], scalar=st[:, 1:2], in1=gkv, op0=ALU.mult, op1=ALU.mult))
                    pT = 2 if j % 2 == 0 else 5

                    def trl(e, lb=lb, pT=pT):
                        for kc in range(5):
                            ins = e.transpose(psb(pT)[:, kc * 128:(kc + 1) * 128], lb[:, kc * 128:(kc + 1) * 128], ident)
                        return ins
                    sc.op("pe", [Rlb, R_cbf], [PB[pT]], trl)
                    srcq = psb(pT)[:, 0:384].rearrange("p (a b) -> p a b", b=128)
                    srck = psb(pT)[:, 384:640].rearrange("p (a b) -> p a b", b=128)
                    sc.op("act", [PB[pT]], [R_cqT], lambda e, j=j, srcq=srcq: e.copy(out=cqT[:, :, j * 128:(j + 1) * 128], in_=srcq))
                    sc.op("dve", [PB[pT]], [R_ckvT], lambda e, j=j, srck=srck: e.tensor_copy(
                        out=ckvT[:, :, j * 128:(j + 1) * 128], in_=srck))

            lat_A(0)
            for j in range(4):
                if j + 1 < 4:
                    lat_A(j + 1)
                lat_B(j)
            if kmla <= 3:
                return _bail()
            pA, pB = 3, 4
            proj_featmajor(wkr, R_wkr, 0, 96, KC, lambda kc, cols=cols: uT[:, kc, cols], [R_uT[qc]], pA)
            proj_featmajor(wkrs, R_wkr, 0, 96, KC, lambda kc, cols=cols: uT[:, kc, cols], [R_uT[qc]], pB)
            sc.op("dve", [PB[pA], R_cs], [R_t1[0]], lambda e: e.tensor_tensor(
                out=t1[0][P, :], in0=psf(pA)[P, :], in1=cs[P, :], op=ALU.mult))
            sc.op("dve", [PB[pB], R_ss], [R_t2[0]], lambda e: e.tensor_tensor(
                out=t2[0][P, :], in0=psf(pB)[P, :], in1=ss_[P, :], op=ALU.mult))
            sc.op("pool", [R_t1[0], R_t2[0]], [R_krope], lambda e: e.tensor_tensor(
                out=krope[P, :], in0=t1[0][P, :], in1=t2[0][P, :], op=ALU.add))
            if kmla <= 4:
                return _bail()
            for h in range(8):
                s_ = h % 2
                pA, pB, pK = 3 + 3 * s_ - 3 * s_, 4, 5
                pA = 3 if s_ == 0 else 6
                pB = 4 if s_ == 0 else 7
                pK = 5 if s_ == 0 else 2
                proj_featmajor(wq, R_wq, h * 96, 96, 3, lambda kc: cqT[:, kc, :], [R_cqT], pA)
                proj_featmajor(wqs, R_wq, h * 96, 96, 3, lambda kc: cqT[:, kc, :], [R_cqT], pB)
                q_, k_, Rq, Rk = qh[s_], kh[s_], R_qh[s_], R_kh[s_]
                a1, a2, Ra1, Ra2 = t1[s_], t2[s_], R_t1[s_], R_t2[s_]
                sc.op("act", [PB[pA]], [Rq], lambda e, q_=q_, pA=pA: e.copy(out=q_[0:64, :], in_=psf(pA)[0:64, :]))
                sc.op("dve", [PB[pA], R_cs], [Ra1], lambda e, a1=a1, pA=pA: e.tensor_tensor(
                    out=a1[P, :], in0=psf(pA)[P, :], in1=cs[P, :], op=ALU.mult))
                sc.op("dve", [PB[pB], R_ss], [Ra2], lambda e, a2=a2, pB=pB: e.tensor_tensor(
                    out=a2[P, :], in0=psf(pB)[P, :], in1=ss_[P, :], op=ALU.mult))
                sc.op("pool", [Ra1, Ra2], [Rq], lambda e, a1=a1, a2=a2, q_=q_: e.tensor_tensor(
                    out=q_[P, :], in0=a1[P, :], in1=a2[P, :], op=ALU.add))
                sc.dma(qs[b][h, :, cols], q_[0:96, :], Rq, [Rq], [R_qs[b][h]])
                def mmk(e, h=h, pK=pK):
                    for kc in range(2):
                        ins = e.matmul(psf(pK)[0:64, :], wkk[:, kc, h * 64:(h + 1) * 64], ckvT[:, kc, :],
                                       start=(kc == 0), stop=(kc == 1))
                    return ins
                sc.op("pe", [R_wkv, R_ckvT], [PB[pK]], mmk)
                sc.op("act", [PB[pK]], [Rk], lambda e, k_=k_, pK=pK: e.copy(out=k_[0:64, :], in_=psf(pK)[0:64, :]))
                sc.dma(ks[b][h, 0:64, cols], k_[0:64, :], Rk, [Rk], [R_ks[b][h]])
                sc.dma(ks[b][h, 64:96, cols], krope[P, :], R_krope, [R_krope], [R_ks[b][h]], ser=False)
            if kmla <= 5:
                return _bail()
            for j in range(4):
                pV = j % 2

                def mmv(e, j=j, pV=pV):
                    for kc in range(2):
                        ins = e.matmul(psf(pV), ckvT[:, kc, j * 128:(j + 1) * 128],
                                       wvv[:, kc, :], start=(kc == 0), stop=(kc == 1))
                    return ins
                sc.op("pe", [R_wkv, R_ckvT], [PB[pV]], mmv)
                sc.op("dve", [PB[pV]], [R_vsb], lambda e, j=j, pV=pV: e.tensor_copy(out=vsb[:, j, :], in_=psf(pV)))
            for h in range(8):
                dst = vs[b][h].rearrange("p (t d) -> p t d", d=64)[:, qc * 4:(qc + 1) * 4, :]
                sc.dma(dst, vsb[:, :, h * 64:(h + 1) * 64], R_vsb, [R_vsb], [R_vs[b][h]], ser=False)
        A.release(m)
        sc.barrier()

    bg = [None]
    MW = {}

    def bg_step():
        if bg[0] is not None:
            try:
                next(bg[0])
            except StopIteration:
                bg[0] = None

    def bg_drain():
        while bg[0] is not None:
            bg_step()

    def gen_merge_weights(l):
        wv = w_in[l].rearrange("(kc p) n -> p kc n", p=128)
        wo = [A.alloc_top([128, 4, D], BF16) for _ in range(3)]
        wg = A.alloc_top([128, KC, 3 * D], BF16)
        wout = A.alloc_top([128, KC, D], BF16)
        stg = [A.alloc_top([128, KC, 256], F32) for _ in range(3)]
        R_stg = [Res(f"bstg{i}") for i in range(3)]
        R_wo = [Res(f"bwo{i}") for i in range(3)]
        R_wg, R_wout = Res("bwg"), Res("bwout")
        MW.update(wo=wo, wg=wg, wout=wout, R_wo=R_wo, R_wg=R_wg, R_wout=R_wout, stg=stg, R_stg=R_stg)
        units = []
        for b in range(3):
            wob = w_o[b][l].rearrange("(kc p) n -> p kc n", p=128)
            for half in range(2):
                units.append((wob[:, :, half * 512:(half + 1) * 512], 4, 512, wo[b][:, :, half * 512:(half + 1) * 512], R_wo[b]))
        for n in range(12):
            units.append((wv[:, :, OFF_GATE + n * 256:OFF_GATE + (n + 1) * 256], KC, 256, wg[:, :, n * 256:(n + 1) * 256], R_wg))
        wov = w_out[l].rearrange("(kc p) n -> p kc n", p=128)
        for n in range(4):
            units.append((wov[:, :, n * 256:(n + 1) * 256], KC, 256, wout[:, :, n * 256:(n + 1) * 256], R_wout))
        pend = []
        for i, (src, a_, c_, dst, Rd) in enumerate(units):
            s_ = i % 3
            stv = stg[s_].rearrange("p k c -> p (k c)").rearrange("p (k c) -> p k c", c=c_)
            if len(pend) == 2:
                pstv, pdst, pRd, ps_ = pend.pop(0)
                sc.op("dve", [R_stg[ps_]], [pRd], lambda e, pstv=pstv, pdst=pdst: e.tensor_copy(out=pdst, in_=pstv))
            sc.dma(stv, src, R_stg[s_], [R_none], [R_stg[s_]])
            pend.append((stv, dst, Rd, s_))
            yield
        for (pstv, pdst, pRd, ps_) in pend:
            sc.op("dve", [R_stg[ps_]], [pRd], lambda e, pstv=pstv, pdst=pdst: e.tensor_copy(out=pdst, in_=pstv))
            yield

    mod_done = set()

    def gen_mod(l2):
        stg, R_stg = MW["stg"], MW["R_stg"]
        ct = A.alloc_top([128, 8], F32)
        cond = A.alloc_top([128, 8], F32)
        R_ct, R_cond = Res("bct"), Res("bcond")
        brow = [A.alloc_top([128, 256], F32) for _ in range(3)]
        mrow = [A.alloc_top([128, 256], F32) for _ in range(3)]
        R_brow = [Res(f"bbrow{i}") for i in range(3)]
        R_mrow = [Res(f"bmrow{i}") for i in range(3)]
        sc.dma(ct, c_t, R_ct, [R_none], [R_ct])
        sc.op("act", [R_ct], [R_cond], lambda e: e.activation(out=cond, in_=ct, func=AF.Silu))
        wv = w_ada[l2].rearrange("(kc p) n -> p kc n", p=128)
        pb = 7
        pend = []

        def finish(n, s_):
            def mm(e):
                for kc in range(KC):
                    ins = e.matmul(psf(pb)[0:1, 0:256], cond[:, kc:kc + 1], stg[s_][:, kc, :],
                                   start=(kc == 0), stop=(kc == KC - 1))
                return ins
            sc.op("pe", [R_cond, R_stg[s_]], [PB[pb]], mm)
            sc.op("dve", [PB[pb], R_brow[s_]], [R_mrow[s_]], lambda e: e.tensor_tensor(
                out=mrow[s_][0:1, :], in0=psf(pb)[0:1, 0:256], in1=brow[s_][0:1, :], op=ALU.add))
            sc.dma(modrow[l2:l2 + 1, n * 256:(n + 1) * 256], mrow[s_][0:1, :], R_mrow[s_], [R_mrow[s_]], [R_mod[l2][n // 2]])
        for n in range(24):
            s_ = n % 3
            if len(pend) == 2:
                finish(*pend.pop(0))
            sc.dma(stg[s_], wv[:, :, n * 256:(n + 1) * 256], R_stg[s_], [R_none], [R_stg[s_]])
            sc.dma(brow[s_][0:1, :], b_ada[l2:l2 + 1, n * 256:(n + 1) * 256], R_brow[s_], [R_none], [R_brow[s_]])
            pend.append((n, s_))
            yield
        for p in pend:
            finish(*p)
            yield
        mod_done.add(l2)

    def gen_bg(l):
        yield from gen_merge_weights(l)
        if l + 1 < depth:
            yield from gen_mod(l + 1)

    def start_bg(l):
        bg[0] = gen_bg(l)

    def phase_attn_softmax(b, scale):
        A.release(cbf_mark)
        m = A.mark()
        K = KQ[b]
        KP = 96 if b == 0 else K
        qsb = [A.alloc([128, S], BF16) for _ in range(2)]
        ksb = [A.alloc([128, S], BF16) for _ in range(2)]
        vsb = [A.alloc([128, NT, 128], BF16) for _ in range(2)]
        R_q = [Res(f"aq{i}") for i in range(2)]
        R_k = [Res(f"ak{i}") for i in range(2)]
        R_v = [Res(f"av{i}") for i in range(2)]
        R_kf = [Res(f"akf{i}") for i in range(2)]
        for i in range(2):
            sc.op("pool", [], [R_v[i]], lambda e, i=i: e.memset(vsb[i][:, :, 64:128], 1.0))
            if b == 0:
                sc.op("pool", [], [R_q[i]], lambda e, i=i: e.memset(qsb[i][64:96, :], 0.0))
                sc.op("pool", [], [R_k[i]], lambda e, i=i: e.memset(ksb[i][64:96, :], 0.0))
                sc.op("pool", [R_q[i]], [R_q[i]], lambda e, i=i: e.memset(qsb[i][64:70, :], 1.0))
                sc.op("pool", [R_k[i]], [R_k[i], R_kf[i]], lambda e, i=i: e.memset(ksb[i][64:70, :], 1.0))
        NPT = 4
        pt = [A.alloc([128, 512], BF16) for _ in range(NPT)]
        R_pt = [Res(f"pt{i}") for i in range(NPT)]
        rec = [A.alloc([64, 512], F32) for _ in range(2)]
        R_rec = [Res(f"rec{i}") for i in range(2)]
        yo = [A.alloc([64, 512], BF16) for _ in range(2)]
        R_yo = [Res(f"yo{i}") for i in range(2)]
        SBK = [0, 1, 2, 3]
        OB = [4, 5]
        mask = masks[b]

        def load_head(h):
            s_ = h % 2
            q_, k_, v_ = qsb[s_], ksb[s_], vsb[s_]
            if b == 0:
                sc.dma(q_[0:67, :], qs[b][h, 0:67, :], R_q[s_], [R_qs[b][h], R_qF], [R_q[s_]])
                sc.dma(k_[0:64, :], ks[b][h, 0:64, :], R_k[s_], [R_ks[b][h]], [R_k[s_]])
                sc.dma(k_[67:70, :], ks[b][h, 67:70, :], R_kf[s_], [R_kF], [R_kf[s_]])
            else:
                sc.dma(q_[0:K, :], qs[b][h, :, :], R_q[s_], [R_qs[b][h]], [R_q[s_]])
                sc.dma(k_[0:K, :], ks[b][h, :, :], R_k[s_], [R_ks[b][h]], [R_k[s_]])
            sc.dma(v_[:, :, 0:64], vs[b][h].rearrange("p (t d) -> p t d", d=64), R_v[s_], [R_vs[b][h]], [R_v[s_]])

        tiles = []
        chain = 0
        for h in range(8):
            for qc in range(NQ):
                nkb = 4 * qc + 4
                for kb in range(nkb):
                    j = kb - 4 * qc
                    tiles.append(dict(h=h, s=h % 2, qc=qc, kb=kb, j=j, c0=(128 * j if j > 0 else 0), first=(kb == 0),
                                      last=(kb == nkb - 1), ob=OB[chain % 2], os=chain % 2,
                                      lasthead=(kb == nkb - 1 and qc == NQ - 1)))
                chain += 1
        T = len(tiles)

        def emit_qk(t):
            d = tiles[t]
            sbk = SBK[t % 4]
            q_, k_ = qsb[d["s"]], ksb[d["s"]]
            c0, kb, qc, j = d["c0"], d["kb"], d["qc"], d["j"]

            def f(e):
                ins = e.matmul(psf(sbk)[:, c0:512], k_[0:KP, kb * 128:(kb + 1) * 128],
                               q_[0:KP, qc * 512 + c0:(qc + 1) * 512], start=True, stop=(j < 0))
                if j >= 0:
                    ins = e.matmul(psf(sbk)[:, c0:c0 + 128], ident, mask, start=False, stop=True)
                return ins
            sc.op("pe", [R_q[d["s"]], R_k[d["s"]], R_kf[d["s"]], R_cbf], [PB[sbk]], f)

        def emit_exp(t):
            d = tiles[t]
            sbk = SBK[t % 4]
            pi = t % NPT
            c0 = d["c0"]
            sc.op("act", [PB[sbk]], [R_pt[pi]], lambda e: e.activation(
                out=pt[pi][:, c0:512], in_=psf(sbk)[:, c0:512], func=AF.Exp, scale=scale))

        def emit_pv(t):
            d = tiles[t]
            pi = t % NPT
            c0, kb, ob, os_ = d["c0"], d["kb"], d["ob"], d["os"]
            v_ = vsb[d["s"]]
            sc.op("pe", [R_v[d["s"]], R_pt[pi]], [PB[ob]], lambda e: e.matmul(
                psf(ob)[:, c0:512], v_[:, kb, :], pt[pi][:, c0:512], start=d["first"], stop=d["last"]))
            if d["last"]:
                h, qc = d["h"], d["qc"]
                sc.op("dve", [PB[ob]], [R_rec[os_]], lambda e: e.reciprocal(out=rec[os_], in_=psf(ob)[64:128, :]))
                sc.op("dve", [PB[ob], R_rec[os_]], [R_yo[os_]], lambda e: e.tensor_tensor(
                    out=yo[os_], in0=psf(ob)[0:64, :], in1=rec[os_], op=ALU.mult))
                sc.dma(ybr[b, h * 64:(h + 1) * 64, qc * 512:(qc + 1) * 512], yo[os_], R_yo[os_], [R_yo[os_]],
                       [R_ybr[b][h][qc]])
            if d["lasthead"] and d["h"] + 2 < 8:
                load_head(d["h"] + 2)

        load_head(0)
        load_head(1)
        for t in range(-2, T):
            if t + 2 < T:
                emit_qk(t + 2)
            if 0 <= t + 1 < T:
                emit_exp(t + 1)
            if t >= 0:
                emit_pv(t)
            if t % 16 == 0:
                bg_step()
        A.release(persist_mark)
        sc.barrier()

    def phase_attn_sb():
        b = 2
        A.release(cbf_mark)
        m = A.mark()
        qsb = [A.alloc([64, S], BF16) for _ in range(2)]
        ksb = [A.alloc([64, S], BF16) for _ in range(2)]
        vsb = [A.alloc([128, NT, 64], BF16) for _ in range(2)]
        R_q = [Res(f"sq{i}") for i in range(2)]
        R_k = [Res(f"sk{i}") for i in range(2)]
        R_v = [Res(f"sv{i}") for i in range(2)]
        NB = 4
        eb = [A.alloc([128, 512], F32) for _ in range(NB)]
        spb = [A.alloc([128, 512], BF16) for _ in range(NB)]
        ecb = [A.alloc([128, 512], F32) for _ in range(NB)]
        ab = [A.alloc([128, 512], BF16) for _ in range(NB)]
        R_e = [Res(f"e{i}") for i in range(NB)]
        R_sp = [Res(f"sp{i}") for i in range(NB)]
        R_ec = [Res(f"ec{i}") for i in range(NB)]
        R_a = [Res(f"a{i}") for i in range(NB)]
        yo = [A.alloc([64, 512], BF16) for _ in range(2)]
        R_yo = [Res(f"syo{i}") for i in range(2)]
        ZB = [0, 1, 2]
        ACC = [3, 4]
        OB = [5, 6]
        mask = masks[2]

        def load_head(h):
            s_ = h % 2
            sc.dma(qsb[s_], qs[b][h, :, :], R_q[s_], [R_qs[b][h]], [R_q[s_]])
            sc.dma(ksb[s_], ks[b][h, :, :], R_k[s_], [R_ks[b][h]], [R_k[s_]])
            sc.dma(vsb[s_], vs[b][h].rearrange("p (t d) -> p t d", d=64), R_v[s_], [R_vs[b][h]], [R_v[s_]])

        tiles = []
        chain = 0
        for h in range(8):
            for qc in range(NQ):
                nkb = 4 * qc + 4
                for idx, kb in enumerate(range(nkb - 1, -1, -1)):
                    j = kb - 4 * qc
                    tiles.append(dict(h=h, s=h % 2, qc=qc, kb=kb, j=j, c0=(128 * j if j > 0 else 0), first=(idx == 0),
                                      last=(idx == nkb - 1), acc=ACC[chain % 2], ob=OB[chain % 2], os=chain % 2,
                                      lasthead=(idx == nkb - 1 and qc == NQ - 1)))
                chain += 1
        T = len(tiles)

        def st_qk(t):
            d = tiles[t]
            zb = ZB[t % 3]
            q_, k_ = qsb[d["s"]], ksb[d["s"]]
            c0, kb, qc, j = d["c0"], d["kb"], d["qc"], d["j"]

            def f(e):
                ins = e.matmul(psf(zb)[:, c0:512], k_[:, kb * 128:(kb + 1) * 128],
                               q_[:, qc * 512 + c0:(qc + 1) * 512], start=True, stop=(j < 0))
                if j >= 0:
                    ins = e.matmul(psf(zb)[:, c0:c0 + 128], ident, mask, start=False, stop=True)
                return ins
            sc.op("pe", [R_q[d["s"]], R_k[d["s"]], R_cbf], [PB[zb]], f)

        def st_act1(t):
            d = tiles[t]
            zb = ZB[t % 3]
            bi = t % NB
            c0 = d["c0"]
            sc.op("act", [PB[zb]], [R_e[bi]], lambda e: e.activation(
                out=eb[bi][:, c0:512], in_=psf(zb)[:, c0:512], func=AF.Exp, scale=0.125))
            sc.op("act", [R_e[bi]], [R_sp[bi]], lambda e: e.activation(
                out=spb[bi][:, c0:512], in_=eb[bi][:, c0:512], func=AF.Ln, bias=1.0, scale=1.0))

        def st_tri(t):
            d = tiles[t]
            bi = t % NB
            c0, acc = d["c0"], d["acc"]
            sc.op("pe", [R_sp[bi], R_cbf], [PB[acc]], lambda e: e.matmul(
                psf(acc)[:, c0:512], ntri, spb[bi][:, c0:512], start=d["first"], stop=True, skip_group_check=True))

        def st_expc(t):
            d = tiles[t]
            bi = t % NB
            c0, acc = d["c0"], d["acc"]
            sc.op("act", [PB[acc]], [R_ec[bi]], lambda e: e.activation(
                out=ecb[bi][:, c0:512], in_=psf(acc)[:, c0:512], func=AF.Exp))
            sc.op("pool", [R_e[bi], R_ec[bi]], [R_a[bi]], lambda e: e.tensor_tensor(
                out=ab[bi][:, c0:512], in0=eb[bi][:, c0:512], in1=ecb[bi][:, c0:512], op=ALU.mult))

        def st_u(t):
            d = tiles[t]
            if d["last"]:
                return
            bi = t % NB
            c0, acc = d["c0"], d["acc"]
            sc.op("pe", [R_sp[bi], R_cbf], [PB[acc]], lambda e: e.matmul(
                psf(acc)[:, c0:512], nu, spb[bi][:, c0:512], start=False, stop=True, skip_group_check=True))

        def st_pv(t):
            d = tiles[t]
            bi = t % NB
            c0, kb, ob, os_ = d["c0"], d["kb"], d["ob"], d["os"]
            v_ = vsb[d["s"]]
            sc.op("pe", [R_v[d["s"]], R_a[bi]], [PB[ob]], lambda e: e.matmul(
                psf(ob)[0:64, c0:512], v_[:, kb, :], ab[bi][:, c0:512], start=d["first"], stop=d["last"],
                skip_group_check=True))
            if d["last"]:
                h, qc = d["h"], d["qc"]
                sc.op("dve", [PB[ob]], [R_yo[os_]], lambda e: e.tensor_copy(out=yo[os_], in_=psf(ob)[0:64, :]))
                sc.dma(ybr[b, h * 64:(h + 1) * 64, qc * 512:(qc + 1) * 512], yo[os_], R_yo[os_], [R_yo[os_]],
                       [R_ybr[b][h][qc]])
            if d["lasthead"] and d["h"] + 2 < 8:
                load_head(d["h"] + 2)

        load_head(0)
        load_head(1)
        for t in range(-2, T + 2):
            if 0 <= t - 1 < T:
                st_u(t - 1)
            if 0 <= t < T:
                st_tri(t)
            if 0 <= t + 2 < T:
                st_qk(t + 2)
            if 0 <= t + 1 < T:
                st_act1(t + 1)
            if 0 <= t < T:
                st_expc(t)
            if 0 <= t - 2 < T:
                st_pv(t - 2)
            if t % 16 == 0:
                bg_step()
        A.release(persist_mark)
        sc.barrier()

    def phase_merge(l):
        A.release(cbf_mark)
        m = A.mark()
        if not MW:
            bg[0] = gen_bg(l)
        bg_drain()
        wo, wg, wout = MW["wo"], MW["wg"], MW["wout"]
        R_wo, R_wg, R_wout = MW["R_wo"], MW["R_wg"], MW["R_wout"]
        GT = A.alloc([128, D], F32)
        R_GT = Res("GT")
        bcast_load(GT, modrow[l:l + 1, 2 * D:3 * D], R_GT, [R_mod[l][4], R_mod[l][5]])
        yb = [A.alloc([128, 4, 512], BF16) for _ in range(3)]
        R_yb = [Res(f"yb{b}") for b in range(3)]
        uc = [A.alloc([128, KC, 512], BF16) for _ in range(2)]
        R_uc = [Res(f"uc{i}") for i in range(2)]
        mT = A.alloc([128, KC, 512], BF16)
        R_mT = Res("mT")
        sig = [A.alloc([128, 512], F32) for _ in range(2)]
        R_sig = [Res(f"sig{i}") for i in range(2)]
        accm = [A.alloc([128, 512], F32) for _ in range(2)]
        R_accm = [Res(f"accm{i}") for i in range(2)]
        tmpm = [A.alloc([128, 512], F32) for _ in range(2)]
        R_tmpm = [Res(f"tmpm{i}") for i in range(2)]
        xt = [A.alloc([128, D], F32) for _ in range(2)]
        R_xt = [Res(f"mxt{i}") for i in range(2)]
        xo = [A.alloc([128, D], F32) for _ in range(2)]
        R_xo = [Res(f"mxo{i}") for i in range(2)]
        cnt = 0
        xcnt = 0
        for qc in range(NQ):
            cols = slice(qc * 512, (qc + 1) * 512)
            us = qc % 2
            sc.dma(uc[us], uts[:, :, cols], R_uc[us], [R_uts[qc]], [R_uc[us]])
            for b in branches:
                src = ybr[b].rearrange("(f p) s -> p f s", p=128)[:, :, cols]
                sc.dma(yb[b], src, R_yb[b], [R_ybr[b][h][qc] for h in range(8)], [R_yb[b]])
            for nci in range(KC):
                a_ = nci % 2
                for b in branches:
                    p1 = 0 + (cnt % 2)
                    p2 = 2 + (cnt % 2)
                    g_ = cnt % 2
                    cnt += 1

                    def mm1(e, b=b, p1=p1, nci=nci):
                        for f in range(4):
                            ins = e.matmul(psf(p1), wo[b][:, f, nci * 128:(nci + 1) * 128], yb[b][:, f, :],
                                           start=(f == 0), stop=(f == 3))
                        return ins
                    sc.op("pe", [R_wo[b], R_yb[b]], [PB[p1]], mm1)

                    def mm2(e, b=b, p2=p2, nci=nci):
                        for kc in range(KC):
                            ins = e.matmul(psf(p2), wg[:, kc, b * D + nci * 128:b * D + (nci + 1) * 128], uc[us][:, kc, :],
                                           start=(kc == 0), stop=(kc == KC - 1))
                        return ins
                    sc.op("pe", [R_wg, R_uc[us]], [PB[p2]], mm2)
                    sc.op("act", [PB[p2]], [R_sig[g_]], lambda e, g_=g_, p2=p2: e.activation(
                        out=sig[g_], in_=psf(p2), func=AF.Sigmoid))
                    if len(branches) == 1:
                        sc.op("dve", [PB[p1], R_sig[g_]], [R_mT], lambda e, g_=g_, p1=p1, nci=nci: e.tensor_tensor(
                            out=mT[:, nci, :], in0=psf(p1), in1=sig[g_], op=ALU.mult))
                    elif b == branches[0]:
                        sc.op("dve", [PB[p1], R_sig[g_]], [R_accm[a_]], lambda e, g_=g_, p1=p1, a_=a_: e.tensor_tensor(
                            out=accm[a_], in0=psf(p1), in1=sig[g_], op=ALU.mult))
                    else:
                        sc.op("dve", [PB[p1], R_sig[g_]], [R_tmpm[g_]], lambda e, g_=g_, p1=p1: e.tensor_tensor(
                            out=tmpm[g_], in0=psf(p1), in1=sig[g_], op=ALU.mult))
                        if b != branches[-1]:
                            sc.op("pool", [R_tmpm[g_], R_accm[a_]], [R_accm[a_]], lambda e, g_=g_, a_=a_: e.tensor_tensor(
                                out=accm[a_], in0=accm[a_], in1=tmpm[g_], op=ALU.add))
                        else:
                            sc.op("pool", [R_tmpm[g_], R_accm[a_]], [R_mT], lambda e, g_=g_, a_=a_, nci=nci: e.tensor_tensor(
                                out=mT[:, nci, :], in0=accm[a_], in1=tmpm[g_], op=ALU.add))
            for j in range(4):
                t = qc * 4 + j
                xs = xcnt % 2
                xcnt += 1
                srcx = x_in if l == 0 else y
                sc.dma(xt[xs], srcx[t * 128:(t + 1) * 128, :], R_xt[xs], [R_none if l == 0 else R_y[t]], [R_xt[xs]])
                for n in range(2):
                    po = 4 + n

                    def mmo(e, j=j, n=n, po=po):
                        for kc in range(KC):
                            ins = e.matmul(psf(po), mT[:, kc, j * 128:(j + 1) * 128], wout[:, kc, n * 512:(n + 1) * 512],
                                           start=(kc == 0), stop=(kc == KC - 1))
                        return ins
                    sc.op("pe", [R_mT, R_wout], [PB[po]], mmo)
                    sc.op("dve", [PB[po], R_GT], [R_xo[xs]], lambda e, n=n, po=po, xs=xs: e.tensor_tensor(
                        out=xo[xs][:, n * 512:(n + 1) * 512], in0=psf(po), in1=GT[:, n * 512:(n + 1) * 512], op=ALU.mult))
                sc.op("dve", [R_xo[xs], R_xt[xs]], [R_xo[xs]], lambda e, xs=xs: e.tensor_tensor(
                    out=xo[xs], in0=xo[xs], in1=xt[xs], op=ALU.add))
                sc.dma(y[t * 128:(t + 1) * 128, :], xo[xs], R_xo[xs], [R_xo[xs]], [R_y[t]])
        A.release(persist_mark)
        A.htop = A.n
        MW.clear()
        sc.barrier()

    def phase_ffn(l, final):
        import os as _os
        FT = int(_os.environ.get("FFNFT", "256"))
        A.release(cbf_mark)
        wg = A.alloc([128, KC, DFF], BF16)
        wu = A.alloc([128, KC, DFF], BF16)
        wd = A.alloc([128, FC, D], BF16)
        R_wg, R_wu, R_wd = Res("fwg"), Res("fwu"), Res("fwd")
        m = A.mark()
        stg = [A.alloc([128, KC, 256], F32) for _ in range(2)]
        R_stg = [Res(f"fstg{i}") for i in range(2)]
        ld = 0
        for (wsrc, dst, Rd) in ((w_gate, wg, R_wg), (w_up, wu, R_wu)):
            wvv = wsrc[l].rearrange("(kc p) n -> p kc n", p=128)
            for c in range(0, DFF, 256):
                s_ = ld % 2
                ld += 1
                sc.dma(stg[s_], wvv[:, :, c:c + 256], R_stg[s_], [R_none], [R_stg[s_]])
                cast(dst[:, :, c:c + 256], stg[s_], [R_stg[s_]], [Rd])
        wdv = w_down[l].rearrange("(fc p) n -> p fc n", p=128)
        for f0 in range(0, FC, 2):
            s_ = ld % 2
            ld += 1
            stv = stg[s_].rearrange("p k c -> p (k c)").rearrange("p (k c) -> p k c", c=1024)
            sc.dma(stv, wdv[:, f0:f0 + 2, :], R_stg[s_], [R_none], [R_stg[s_]])
            cast(wd[:, f0:f0 + 2, :], stv, [R_stg[s_]], [R_wd])
        sc.barrier()
        A.release(m)
        import os as _os
        kffn = int(_os.environ.get("KFFN", "9"))
        if kffn <= 1:
            A.release(persist_mark)
            return
        G, SH, R_G, R_SH = load_mod_tiles(l, 3, 4, g_ffn[l:l + 1, :])
        sc.barrier()
        A.release(A.mark() - D * 4)
        GT = A.alloc([128, D], F32)
        R_GT = Res("fGT")
        bcast_load(GT, modrow[l:l + 1, 5 * D:6 * D], R_GT, [R_mod[l][10], R_mod[l][11]])
        if final:
            GF = A.alloc([128, D], F32)
            R_GF = Res("GF")
            bcast_load(GF, g_final[0:1, :], R_GF, [R_none])
        nb = NormBufs(2, 1)
        xr = [A.alloc([128, D], F32) for _ in range(1)]
        R_xr = [Res(f"xr{i}") for i in range(1)]
        ufT2 = [A.alloc([128, KC, FT], BF16) for _ in range(2)]
        R_ufT2 = [Res(f"ufT{i}") for i in range(2)]
        hT = A.alloc([128, FC, FT], BF16)
        R_hT = Res("hT")
        sg = [A.alloc([128, FT], F32) for _ in range(2)]
        R_sg = [Res(f"fsg{i}") for i in range(2)]
        xo = [A.alloc([128, D], F32) for _ in range(2)]
        R_xo = [Res(f"fxo{i}") for i in range(2)]
        fss = [A.alloc([128, 2], F32) for _ in range(2)]
        R_fss = [Res(f"fss{i}") for i in range(2)]
        cnt = 0
        xcnt = 0
        NJ = FT // 128
        if kffn <= 2:
            sc.barrier()
            A.release(persist_mark)
            return
        ysrc = y
        NCH = S // FT

        def pro_load(ch):
            for j in range(NJ):
                t = ch * NJ + j
                sc.dma(nb.xt[j], ysrc[t * 128:(t + 1) * 128, :], nb.R_xt[j], [R_y[t]], [nb.R_xt[j]])

        def pro_norm(ch):
            for j in range(NJ):
                norm_tile(nb, j, G, SH, R_G, R_SH)

        def pro_T(ch):
            for j in range(NJ):
                transpose_tile(nb.u[j], nb.R_u[j], j % 2, ufT2[ch % 2][:, :, j * 128:(j + 1) * 128], R_ufT2[ch % 2],
                               "act" if j % 2 == 0 else "dve")

        def gu(ch, f):
            nonlocal cnt
            ufT, R_ufT = ufT2[ch % 2], R_ufT2[ch % 2]
            pg = 2 + (cnt % 2)
            pu = 4 + (cnt % 2)
            g_ = cnt % 2
            cnt += 1

            def mmg(e):
                for kc in range(KC):
                    ins = e.matmul(psf(pg)[:, 0:FT], wg[:, kc, f * 128:(f + 1) * 128], ufT[:, kc, :],
                                   start=(kc == 0), stop=(kc == KC - 1))
                return ins
            sc.op("pe", [R_wg, R_ufT], [PB[pg]], mmg)

            def mmu(e):
                for kc in range(KC):
                    ins = e.matmul(psf(pu)[:, 0:FT], wu[:, kc, f * 128:(f + 1) * 128], ufT[:, kc, :],
                                   start=(kc == 0), stop=(kc == KC - 1))
                return ins
            sc.op("pe", [R_wu, R_ufT], [PB[pu]], mmu)
            sc.op("act", [PB[pg]], [R_sg[g_]], lambda e: e.activation(out=sg[g_], in_=psf(pg)[:, 0:FT], func=AF.Silu))
            sc.op("dve", [PB[pu], R_sg[g_]], [R_hT], lambda e: e.tensor_tensor(
                out=hT[:, f, :], in0=psf(pu)[:, 0:FT], in1=sg[g_], op=ALU.mult))

        pro_load(0)
        pro_norm(0)
        pro_T(0)
        for ch in range(NCH):
            if ch + 1 < NCH:
                pro_load(ch + 1)
            for f in range(FC // 2):
                gu(ch, f)
            if ch + 1 < NCH:
                pro_norm(ch + 1)
                pro_T(ch + 1)
            for f in range(FC // 2, FC):
                gu(ch, f)
            for j in range(NJ):
                t = ch * NJ + j
                xs = xcnt % 2
                xcnt += 1
                sc.dma(xr[0], ysrc[t * 128:(t + 1) * 128, :], R_xr[0], [R_y[t]], [R_xr[0]])
                for n in range(2):
                    po = 6 + n

                    def mmd(e, j=j, n=n, po=po):
                        for f in range(FC):
                            ins = e.matmul(psf(po), hT[:, f, j * 128:(j + 1) * 128], wd[:, f, n * 512:(n + 1) * 512],
                                           start=(f == 0), stop=(f == FC - 1))
                        return ins
                    sc.op("pe", [R_hT, R_wd], [PB[po]], mmd)
                    sc.op("dve", [PB[po], R_GT], [R_xo[xs]], lambda e, n=n, po=po, xs=xs: e.tensor_tensor(
                        out=xo[xs][:, n * 512:(n + 1) * 512], in0=psf(po), in1=GT[:, n * 512:(n + 1) * 512], op=ALU.mult))
                sc.op("dve", [R_xo[xs], R_xr[0]], [R_xo[xs]], lambda e, xs=xs: e.tensor_tensor(
                    out=xo[xs], in0=xo[xs], in1=xr[0], op=ALU.add))
                if final:
                    ss = fss[xs]
                    Rss = R_fss[xs]
                    sc.op("act", [R_xo[xs]], [nb.R_junk, Rss], lambda e, xs=xs, ss=ss: e.activation(
                        out=nb.junk, in_=xo[xs], func=AF.Square, accum_out=ss[:, 0:1]))
                    sc.op("dve", [Rss], [Rss], lambda e, ss=ss: e.tensor_scalar(
                        out=ss[:, 0:1], in0=ss[:, 0:1], scalar1=1.0 / D, scalar2=EPS, op0=ALU.mult, op1=ALU.add))
                    sc.op("act", [Rss], [Rss], lambda e, ss=ss: e.activation(out=ss[:, 0:1], in_=ss[:, 0:1], func=AF.Ln))
                    sc.op("act", [Rss], [Rss], lambda e, ss=ss: e.activation(
                        out=ss[:, 0:1], in_=ss[:, 0:1], func=AF.Exp, scale=-0.5))
                    sc.op("dve", [R_xo[xs], Rss, R_GF], [R_xo[xs]], lambda e, xs=xs, ss=ss: e.scalar_tensor_tensor(
                        out=xo[xs], in0=xo[xs], scalar=ss[:, 0:1], in1=GF, op0=ALU.mult, op1=ALU.mult))
                sc.dma(y[t * 128:(t + 1) * 128, :], xo[xs], R_xo[xs], [R_xo[xs]], [R_y[t]])
        A.release(persist_mark)
        sc.barrier()

    cbf_mark = A.mark() - KC * S * 2
    assert cbf_mark >= 0

    import os as _os
    stop = int(_os.environ.get("KSTOP", "999"))
    plist = []
    for l in range(depth):
        plist.append(lambda l=l: phase_mod(l))
        plist.append(lambda l=l: phase_norm_attn(l))
        if 0 in branches:
            plist.append(lambda l=l: phase_proj_qkv(l, 0, OFF_FOXQ))
            plist.append(lambda l=l: phase_fox_F(l))
        if 1 in branches:
            plist.append(lambda l=l: phase_proj_mla(l))
        if 2 in branches:
            plist.append(lambda l=l: phase_proj_qkv(l, 2, OFF_SB))
        if 0 in branches:
            plist.append(lambda l=l: phase_attn_softmax(0, 0.125))
        if 1 in branches:
            plist.append(lambda l=l: phase_attn_softmax(1, float(96 ** -0.5)))
        if 2 in branches:
            plist.append(lambda l=l: start_bg(l))
            plist.append(lambda l=l: phase_attn_sb())
        plist.append(lambda l=l: phase_merge(l))
        plist.append(lambda l=l: phase_ffn(l, final=(l == depth - 1)))
    if _os.environ.get("ONLYFFN"):
        plist = [lambda: phase_mod(0), lambda: phase_ffn(0, final=bool(int(_os.environ.get("FFNFINAL", "1"))))]
    for i, p in enumerate(plist):
        if i >= stop:
            break
        p()
    sc.barrier(engines=("sp",))
    build.info = dict(peak=A.peak, nwait=sc.nwait, cnt=dict(sc.cnt), nsem=sc.nsem)
    return nc


_CACHE = {}


def _prep_core(inputs, bidx, S):
    c = np.asarray(inputs["c"], np.float32)[bidx]
    m = {
        "x": np.ascontiguousarray(np.asarray(inputs["x"], np.float32)[bidx, :S]),
        "c_t": np.ascontiguousarray(c.reshape(8, 128).T),
        "pos": np.ascontiguousarray(np.asarray(inputs["positions"], np.int32)[bidx, :S].reshape(1, S)),
        "g_final": np.ascontiguousarray(np.asarray(inputs["g_final"], np.float32).reshape(1, D)),
        "consts": make_consts(),
    }
    for k in ("g_mix", "w_ada", "b_ada", "w_in", "b_fox_f", "g_mla_q", "w_mla_uq", "g_mla_kv", "w_mla_ukv",
              "w_o_fox", "w_o_mla", "w_o_sb", "w_out", "g_ffn", "w_ffn_gate", "w_ffn_up", "w_ffn_down"):
        m[k] = np.ascontiguousarray(np.asarray(inputs[k], np.float32))
    return m


def kernel(**inputs):
    x = np.asarray(inputs["x"])
    B, S, _ = x.shape
    key = (S,)
    if key not in _CACHE:
        _CACHE[key] = build(S)
    nc = _CACHE[key]
    in_maps = [_prep_core(inputs, b, S) for b in range(B)]
    res = run_bass_kernel_spmd(nc, in_maps, core_ids=list(range(B)))
    out = np.stack([np.asarray(r["y"], np.float32) for r in res.results], axis=0)
    return out
```

```python
import numpy as np
import ml_dtypes
import concourse.bass as bass
import concourse.mybir as mybir
from concourse.bass_utils import run_bass_kernel_spmd

F32, BF16, I32, U8 = mybir.dt.float32, mybir.dt.bfloat16, mybir.dt.int32, mybir.dt.uint8
AF = mybir.ActivationFunctionType
ALU = mybir.AluOpType

D = 1024
KC = 8
DFF = 2816
FC = 22
DEPTH = 2
EPS = 1e-6
OFF_FOXQ, OFF_FOXF, OFF_QL, OFF_KVL, OFF_KR, OFF_SB, OFF_GATE, INW = 0, 1536, 1544, 1928, 2184, 2216, 3752, 6824
NEG = -30000.0
TWO_PI = float(2.0 * np.pi)
PI = float(np.pi)
CW1 = 6.28125
CW2 = float(2.0 * np.pi - 6.28125)

C_IDENT, C_NTRI, C_NU, C_NTRI2, C_NONES, C_MFOX, C_MMLA, C_MSB, C_INV, C_SGN, C_END = (
    0, 128, 256, 384, 512, 640, 768, 896, 1024, 1025, 1026)


def make_consts():
    c = np.zeros((128, C_END), np.float32)
    i = np.arange(128)
    c[:, C_IDENT:C_IDENT + 128] = np.eye(128)
    j, s = i[:, None], i[None, :]
    c[:, C_NTRI:C_NTRI + 128] = -1.0 * (j >= s)
    c[:, C_NU:C_NU + 128] = -1.0 * (j < s)
    c[:, C_NTRI2:C_NTRI2 + 128] = -1.0 * (j <= s)
    c[:, C_NONES:C_NONES + 128] = -1.0
    k, q = i[:, None], i[None, :]
    c[:, C_MFOX:C_MFOX + 128] = NEG * (k > q)
    c[:, C_MMLA:C_MMLA + 128] = NEG * ((k >= 64) & (q < 64))
    c[:, C_MSB:C_MSB + 128] = NEG * (k >= q)
    inv = (np.float32(10000.0) ** (-(np.arange(16, dtype=np.float32) / np.float32(16)))).astype(np.float32)
    c[64:80, C_INV] = inv
    c[80:96, C_INV] = inv
    c[64:80, C_SGN] = -1.0
    c[80:96, C_SGN] = 1.0
    return c


class Res:
    __slots__ = ("name", "w", "r", "sem", "semv", "excl")

    def __init__(self, name, excl=False):
        self.name = name
        self.excl = excl
        self.w = None
        self.r = {}
        self.sem = None
        self.semv = 0


class Sched:
    def __init__(self, nc):
        self.nc = nc
        self.eng = {"pe": nc.tensor, "act": nc.scalar, "dve": nc.vector, "pool": nc.gpsimd, "sp": nc.sync}
        self.sem = {k: nc.alloc_semaphore("s_" + k) for k in ("pe", "act", "dve", "pool")}
        self.cnt = {k: 0 for k in self.sem}
        self.seen = {k: {} for k in self.eng}
        self.dma_res = []
        self.nwait = 0
        self.sem_pool = []
        self.nsem = 0

    def _wait(self, e, tok):
        if tok is None:
            return
        sem, val, src = tok
        if src == "pe" and e == "pe":
            return
        sid = id(sem)
        if self.seen[e].get(sid, 0) >= val:
            return
        self.eng[e].wait_ge(sem, val)
        self.nwait += 1
        self.seen[e][sid] = val

    def _deps(self, e, reads, writes):
        for r in reads:
            self._wait(e, r.w)
            if r.excl:
                for tok in r.r.values():
                    if tok[2] != e:
                        self._wait(e, tok)
        for w in writes:
            self._wait(e, w.w)
            for tok in w.r.values():
                if tok[2] == e and e != "sp":
                    continue
                self._wait(e, tok)

    def _post(self, tok, reads, writes):
        sid = id(tok[0])
        for r in reads:
            r.r[sid] = tok
        for w in writes:
            w.w = tok
            w.r = {}

    def op(self, e, reads, writes, fn):
        self._deps(e, reads, writes)
        ins = fn(self.eng[e])
        self.cnt[e] += 1
        ins.then_inc(self.sem[e], 1)
        tok = (self.sem[e], self.cnt[e], e)
        self._post(tok, reads, writes)

    def dma(self, out_ap, in_ap, sb, reads, writes, q="sp", ser=True):
        wr = list(writes)
        if sb not in wr and ser:
            wr_dep = wr + [sb]
        else:
            wr_dep = wr
        self._deps(q, reads, wr_dep)
        if sb.sem is None:
            if self.sem_pool:
                sb.sem, sb.semv = self.sem_pool.pop()
            else:
                sb.sem = self.nc.alloc_semaphore(f"d{self.nsem}")
                sb.semv = 0
                self.nsem += 1
            self.dma_res.append(sb)
        ins = self.eng[q].dma_start(out=out_ap, in_=in_ap)
        sb.semv += 16
        ins.then_inc(sb.sem, 16)
        tok = (sb.sem, sb.semv, "dma")
        self._post(tok, reads, wr)

    def barrier(self, engines=("pe", "act", "dve", "pool", "sp")):
        toks = [(self.sem[k], self.cnt[k], k) for k in self.sem if self.cnt[k] > 0]
        toks += [(r.sem, r.semv, "dma") for r in self.dma_res if r.semv > 0]
        for e in engines:
            for t in toks:
                self._wait(e, t)
        if len(engines) == 5:
            for r in self.dma_res:
                self.sem_pool.append((r.sem, r.semv))
                r.sem = None
            self.dma_res = []


class Arena:
    def __init__(self, nc, nbytes):
        self.t = nc.alloc_sbuf_tensor("arena", [128, nbytes], U8)
        self.n = nbytes
        self.top = 0
        self.peak = 0
        self.htop = nbytes

    def alloc(self, shape, dt, parts=128):
        esz = {F32: 4, BF16: 2, I32: 4}[dt]
        free = int(np.prod(shape[1:]))
        nb = (free * esz + 31) // 32 * 32
        off = self.top
        assert off + nb <= self.htop, f"arena overflow {off}+{nb}>{self.htop}"
        self.top += nb
        self.peak = max(self.peak, self.top)
        v = self.t[:, off:off + free * esz].bitcast(dt)
        if len(shape) == 3:
            v = v.rearrange("p (a b) -> p a b", b=shape[2])
        elif len(shape) == 4:
            v = v.rearrange("p (a b c) -> p a b c", b=shape[2], c=shape[3])
        if shape[0] < 128:
            v = v[0:shape[0]]
        return v

    def alloc_top(self, shape, dt):
        esz = {F32: 4, BF16: 2, I32: 4}[dt]
        free = int(np.prod(shape[1:]))
        nb = (free * esz + 31) // 32 * 32
        self.htop -= nb
        off = self.htop
        assert off >= self.top, "arena top/bottom collision"
        v = self.t[:, off:off + free * esz].bitcast(dt)
        if len(shape) == 3:
            v = v.rearrange("p (a b) -> p a b", b=shape[2])
        return v

    def mark(self):
        return self.top

    def release(self, m):
        self.top = m


def build(S, depth=DEPTH, branches=(0, 1, 2)):
    NT = S // 128
    NQ = S // 512
    nc = bass.Bass("TRN2", target_bir_lowering=False)

    def din(name, shape, dt=F32):
        return nc.dram_tensor(name, list(shape), dt, kind="ExternalInput").ap()

    x_in = din("x", [S, D])
    c_t = din("c_t", [128, 8])
    pos = din("pos", [1, S], I32)
    g_mix = din("g_mix", [DEPTH, D])
    w_ada = din("w_ada", [DEPTH, D, 6 * D])
    b_ada = din("b_ada", [DEPTH, 6 * D])
    w_in = din("w_in", [DEPTH, D, INW])
    b_fox_f = din("b_fox_f", [DEPTH, 8])
    g_mla_q = din("g_mla_q", [DEPTH, 384])
    w_mla_uq = din("w_mla_uq", [DEPTH, 384, 768])
    g_mla_kv = din("g_mla_kv", [DEPTH, 256])
    w_mla_ukv = din("w_mla_ukv", [DEPTH, 256, 1024])
    w_o = [din("w_o_fox", [DEPTH, 512, D]), din("w_o_mla", [DEPTH, 512, D]), din("w_o_sb", [DEPTH, 512, D])]
    w_out = din("w_out", [DEPTH, D, D])
    g_ffn = din("g_ffn", [DEPTH, D])
    w_gate = din("w_ffn_gate", [DEPTH, D, DFF])
    w_up = din("w_ffn_up", [DEPTH, D, DFF])
    w_down = din("w_ffn_down", [DEPTH, DFF, D])
    g_final = din("g_final", [1, D])
    consts = din("consts", [128, C_END])
    y = nc.dram_tensor("y", [S, D], F32, kind="ExternalOutput").ap()

    KQ = [70, 96, 64]
    modrow = nc.dram_tensor("modrow", [DEPTH, 6 * D], F32).ap()
    qs = [nc.dram_tensor(f"qs{b}", [8, KQ[b], S], BF16).ap() for b in range(3)]
    ks = [nc.dram_tensor(f"ks{b}", [8, KQ[b], S], BF16).ap() for b in range(3)]
    VW = [128, 128, 64]
    vs = [nc.dram_tensor(f"vs{b}", [8, 128, NT * VW[b]], BF16).ap() for b in range(3)]
    ybr = nc.dram_tensor("ybr", [3, 512, S], BF16).ap()
    uts = nc.dram_tensor("uts", [128, KC, S], BF16).ap()

    sc = Sched(nc)
    A = Arena(nc, 207 * 1024)
    ps = [nc.alloc_psum_tensor(f"ps{i}", [128, 512], F32) for i in range(8)]
    PB = [Res(f"pb{i}", excl=True) for i in range(8)]

    def psf(i):
        return ps[i][:]

    def psb(i):
        return ps[i][:].bitcast(BF16)

    R_mod = [[Res(f"mod{l}_{n}") for n in range(12)] for l in range(DEPTH)]
    R_y = [Res(f"y{t}") for t in range(NT)]
    R_qs = [[Res(f"qs{b}_{h}") for h in range(8)] for b in range(3)]
    R_ks = [[Res(f"ks{b}_{h}") for h in range(8)] for b in range(3)]
    R_vs = [[Res(f"vs{b}_{h}") for h in range(8)] for b in range(3)]
    R_qF = Res("qF")
    R_kF = Res("kF")
    R_ybr = [[[Res(f"ybr{b}_{h}_{q}") for q in range(NQ)] for h in range(8)] for b in range(3)]
    R_none = Res("ext")
    R_uts = [Res(f"uts{q}") for q in range(NQ)]

    cst = A.alloc([128, C_END], F32)
    R_cst = Res("cst")
    cbf = A.alloc([128, 1024], BF16)
    R_cbf = Res("cbf")
    sc.dma(cst, consts, R_cst, [R_none], [R_cst])
    sc.op("dve", [R_cst], [R_cbf], lambda e: e.tensor_copy(out=cbf, in_=cst[:, 0:1024]))
    ident = cbf[:, C_IDENT:C_IDENT + 128]
    ntri = cbf[:, C_NTRI:C_NTRI + 128]
    nu = cbf[:, C_NU:C_NU + 128]
    masks = [cbf[:, C_MFOX:C_MFOX + 128], cbf[:, C_MMLA:C_MMLA + 128], cbf[:, C_MSB:C_MSB + 128]]
    ntri2_f = cst[:, C_NTRI2:C_NTRI2 + 128]
    nones_f = cst[:, C_NONES:C_NONES + 128]

    uT = A.alloc([128, KC, S], BF16)
    R_uT = [Res(f"uT{q}") for q in range(NQ)]
    persist_mark = A.mark()

    cast_ctr = [0]

    def cast(dst, src, reads, writes):
        cast_ctr[0] += 1
        if cast_ctr[0] % 2 == 0:
            sc.op("dve", reads, writes, lambda e: e.tensor_copy(out=dst, in_=src))
        else:
            sc.op("act", reads, writes, lambda e: e.copy(out=dst, in_=src))

    def bcast_load(dst, row_ap, Rdst, reads):
        sc.dma(dst, row_ap.partition_broadcast(128), Rdst, reads, [Rdst])

    def phase_mod(l):
        if l in mod_done:
            return
        m = A.mark()
        ct = A.alloc([128, 8], F32)
        cond = A.alloc([128, 8], F32)
        R_ct, R_cond = Res("ct"), Res("cond")
        wst = [A.alloc([128, KC, 512], F32) for _ in range(2)]
        R_wst = [Res(f"wst{i}") for i in range(2)]
        brow = [A.alloc([1, 512], F32) for _ in range(2)]
        R_brow = [Res(f"brow{i}") for i in range(2)]
        mrow = [A.alloc([1, 512], F32) for _ in range(2)]
        R_mrow = [Res(f"mrow{i}") for i in range(2)]
        sc.dma(ct, c_t, R_ct, [R_none], [R_ct])
        sc.op("act", [R_ct], [R_cond], lambda e: e.activation(out=cond, in_=ct, func=AF.Silu))
        wv = w_ada[l].rearrange("(kc p) n -> p kc n", p=128)
        for n in range(12):
            s_ = n % 2
            sc.dma(wst[s_], wv[:, :, n * 512:(n + 1) * 512], R_wst[s_], [R_none], [R_wst[s_]])
            sc.dma(brow[s_][0:1, :], b_ada[l:l + 1, n * 512:(n + 1) * 512], R_brow[s_], [R_none], [R_brow[s_]])
            pb = n % 2

            def mm(e, s_=s_, pb=pb):
                for kc in range(KC):
                    ins = e.matmul(psf(pb)[0:1, :], cond[:, kc:kc + 1], wst[s_][:, kc, :],
                                   start=(kc == 0), stop=(kc == KC - 1))
                return ins
            sc.op("pe", [R_cond, R_wst[s_]], [PB[pb]], mm)
            sc.op("dve", [PB[pb], R_brow[s_]], [R_mrow[s_]],
                  lambda e, s_=s_, pb=pb: e.tensor_tensor(out=mrow[s_][0:1, :], in0=psf(pb)[0:1, :],
                                                          in1=brow[s_][0:1, :], op=ALU.add))
            sc.dma(modrow[l:l + 1, n * 512:(n + 1) * 512], mrow[s_][0:1, :], R_mrow[s_], [R_mrow[s_]], [R_mod[l][n]])
        A.release(m)
        sc.barrier()

    def load_mod_tiles(l, i_sh, i_sc, grow):
        G = A.alloc([128, D], F32)
        SH = A.alloc([128, D], F32)
        GM = A.alloc([128, D], F32)
        R_G, R_SH, R_GM = Res("G"), Res("SH"), Res("GM")
        bcast_load(G, modrow[l:l + 1, i_sc * D:(i_sc + 1) * D], R_G, [R_mod[l][2 * i_sc], R_mod[l][2 * i_sc + 1]])
        bcast_load(SH, modrow[l:l + 1, i_sh * D:(i_sh + 1) * D], R_SH, [R_mod[l][2 * i_sh], R_mod[l][2 * i_sh + 1]])
        bcast_load(GM, grow, R_GM, [R_none])
        sc.op("dve", [R_G, R_GM], [R_G],
              lambda e: e.scalar_tensor_tensor(out=G, in0=G, scalar=1.0, in1=GM, op0=ALU.add, op1=ALU.mult))
        return G, SH, R_G, R_SH

    class NormBufs:
        def __init__(self, n=2, ntmp=2):
            self.n = n
            self.xt = [A.alloc([128, D], F32) for _ in range(n)]
            self.R_xt = [Res(f"xt{i}") for i in range(n)]
            self.tmp = [A.alloc([128, D], F32) for _ in range(ntmp)]
            self.R_tmp = [Res(f"ntmp{i}") for i in range(ntmp)]
            self.u = [A.alloc([128, D], BF16) for _ in range(n)]
            self.R_u = [Res(f"u{i}") for i in range(n)]
            self.junk = A.alloc([128, D], BF16)
            self.R_junk = Res("junk")
            self.ss = [A.alloc([128, 2], F32) for _ in range(n)]
            self.R_ss = [Res(f"ss{i}") for i in range(n)]

    def norm_tile(nb, s_, G, SH, R_G, R_SH):
        ti = s_ % len(nb.tmp)
        xt, ss, tmp, u = nb.xt[s_], nb.ss[s_], nb.tmp[ti], nb.u[s_]
        Rx, Rss, Rt, Ru = nb.R_xt[s_], nb.R_ss[s_], nb.R_tmp[ti], nb.R_u[s_]
        sc.op("act", [Rx], [nb.R_junk, Rss],
              lambda e: e.activation(out=nb.junk, in_=xt, func=AF.Square, accum_out=ss[:, 0:1]))
        sc.op("dve", [Rss], [Rss], lambda e: e.tensor_scalar(out=ss[:, 0:1], in0=ss[:, 0:1], scalar1=1.0 / D,
                                                             scalar2=EPS, op0=ALU.mult, op1=ALU.add))
        sc.op("act", [Rss], [Rss], lambda e: e.activation(out=ss[:, 0:1], in_=ss[:, 0:1], func=AF.Ln))
        sc.op("act", [Rss], [Rss], lambda e: e.activation(out=ss[:, 0:1], in_=ss[:, 0:1], func=AF.Exp, scale=-0.5))
        sc.op("dve", [Rx, Rss, R_G], [Rt],
              lambda e: e.scalar_tensor_tensor(out=tmp, in0=xt, scalar=ss[:, 0:1], in1=G, op0=ALU.mult, op1=ALU.mult))
        sc.op("dve", [Rt, R_SH], [Ru], lambda e: e.tensor_tensor(out=u, in0=tmp, in1=SH, op=ALU.add))

    def transpose_tile(u, Ru, pb, dst3, Rdst, eng):
        def tr(e):
            for kc in range(KC):
                ins = e.transpose(psb(pb)[:, kc * 128:(kc + 1) * 128], u[:, kc * 128:(kc + 1) * 128], ident)
            return ins
        sc.op("pe", [Ru, R_cbf], [PB[pb]], tr)
        src = psb(pb).rearrange("p (a b) -> p a b", b=128)
        if eng == "act":
            sc.op("act", [PB[pb]], [Rdst], lambda e: e.copy(out=dst3, in_=src))
        else:
            sc.op("dve", [PB[pb]], [Rdst], lambda e: e.tensor_copy(out=dst3, in_=src))

    def phase_norm_attn(l):
        m = A.mark()
        G, SH, R_G, R_SH = load_mod_tiles(l, 0, 1, g_mix[l:l + 1, :])
        nb = NormBufs(2)
        for t in range(NT):
            s_ = t % 2
            src = x_in if l == 0 else y
            sc.dma(nb.xt[s_], src[t * 128:(t + 1) * 128, :], nb.R_xt[s_], [R_none if l == 0 else R_y[t]], [nb.R_xt[s_]])
            norm_tile(nb, s_, G, SH, R_G, R_SH)
            transpose_tile(nb.u[s_], nb.R_u[s_], t % 2, uT[:, :, t * 128:(t + 1) * 128], R_uT[t // 4],
                           "act" if t % 2 == 0 else "dve")
        for q in range(NQ):
            sc.dma(uts[:, :, q * 512:(q + 1) * 512], uT[:, :, q * 512:(q + 1) * 512], R_uT[q], [R_uT[q]], [R_uts[q]])
        A.release(m)
        sc.barrier()

    def load_w_cols(l, wsrc_view, c0, ncols, stg, R_stg, dst, R_dst, kcs=KC):
        sc.dma(stg[:, 0:kcs, 0:ncols], wsrc_view[:, :, c0:c0 + ncols], R_stg, [R_none], [R_stg])
        sc.op("pool", [R_stg], [R_dst], lambda e: e.tensor_copy(out=dst[:, 0:kcs, 0:ncols], in_=stg[:, 0:kcs, 0:ncols]))

    def proj_featmajor(w3, Rw, c0, M, kcs, rhs_fn, R_rhs, pb, nkc_rows=None):
        def mm(e):
            for kc in range(kcs):
                ins = e.matmul(psf(pb)[0:M, :], w3[:, kc, c0:c0 + M], rhs_fn(kc), start=(kc == 0), stop=(kc == kcs - 1))
            return ins
        sc.op("pe", [Rw] + R_rhs, [PB[pb]], mm)

    def phase_proj_qkv(l, b, colbase):
        m = A.mark()
        wv = w_in[l].rearrange("(kc p) n -> p kc n", p=128)
        stg = [A.alloc([128, KC, 384], F32) for _ in range(2)]
        R_stg = [Res(f"stg{i}") for i in range(2)]
        wb = [A.alloc([128, KC, 384], BF16) for _ in range(2)]
        R_wb = [Res(f"wb{i}") for i in range(2)]
        R_stg3 = [[Res(f"stg{i}_{j}") for j in range(3)] for i in range(2)]
        qT2 = [A.alloc([128, S], BF16) for _ in range(2)]
        kT2 = [A.alloc([128, S], BF16) for _ in range(2)]
        W = VW[b]
        vh2 = [[A.alloc([128, NT, W], BF16) for _ in range(2)] for _ in range(2)]
        R_qT2 = [Res(f"qT{i}") for i in range(2)]
        R_kT2 = [Res(f"kT{i}") for i in range(2)]
        R_vh2 = [[Res(f"vh{i}_{j}") for j in range(2)] for i in range(2)]
        if W == 128:
            for i in range(2):
                for j in range(2):
                    sc.op("pool", [], [R_vh2[i][j]], lambda e, i=i, j=j: e.memset(vh2[i][j][:, :, 64:128], 1.0))
        for hp in range(4):
            s_ = hp % 2
            qT, kT = qT2[s_], kT2[s_]
            R_qT, R_kT = R_qT2[s_], R_kT2[s_]
            for i in range(3):
                c0 = colbase + i * 512 + hp * 128
                sc.dma(stg[s_][:, :, i * 128:(i + 1) * 128], wv[:, :, c0:c0 + 128], R_stg3[s_][i], [R_none], [R_stg3[s_][i]])
            cast(wb[s_], stg[s_], R_stg3[s_], [R_wb[s_]] + R_stg3[s_])
            cnt = 0
            for i, (dstT, R_d) in enumerate(((qT, R_qT), (kT, R_kT))):
                for tc in range(NQ):
                    pb = cnt % 2
                    cnt += 1
                    proj_featmajor(wb[s_], R_wb[s_], i * 128, 128, KC,
                                   lambda kc, tc=tc: uT[:, kc, tc * 512:(tc + 1) * 512], [R_uT[tc]], pb)
                    if cnt % 2 == 0:
                        sc.op("act", [PB[pb]], [R_d], lambda e, pb=pb, dstT=dstT, tc=tc: e.copy(
                            out=dstT[:, tc * 512:(tc + 1) * 512], in_=psf(pb)))
                    else:
                        sc.op("dve", [PB[pb]], [R_d], lambda e, pb=pb, dstT=dstT, tc=tc: e.tensor_copy(
                            out=dstT[:, tc * 512:(tc + 1) * 512], in_=psf(pb)))
            for tg in range(NT // 4):
                pb = 2 + tg % 2

                def mmv(e, tg=tg, pb=pb, s_=s_):
                    for j in range(4):
                        t = tg * 4 + j
                        for kc in range(KC):
                            ins = e.matmul(psf(pb)[:, j * 128:(j + 1) * 128], uT[:, kc, t * 128:(t + 1) * 128],
                                           wb[s_][:, kc, 256:384], start=(kc == 0), stop=(kc == KC - 1))
                    return ins
                sc.op("pe", [R_wb[s_], R_uT[tg]], [PB[pb]], mmv)
                src3 = psf(pb).rearrange("p (a b) -> p a b", b=128)
                sc.op("dve", [PB[pb]], [R_vh2[s_][0]], lambda e, tg=tg, src3=src3: e.tensor_copy(
                    out=vh2[s_][0][:, tg * 4:(tg + 1) * 4, 0:64], in_=src3[:, :, 0:64]))
                sc.op("act", [PB[pb]], [R_vh2[s_][1]], lambda e, tg=tg, src3=src3: e.copy(
                    out=vh2[s_][1][:, tg * 4:(tg + 1) * 4, 0:64], in_=src3[:, :, 64:128]))
            for hh in range(2):
                h = 2 * hp + hh
                sc.dma(qs[b][h, 0:64, :], qT[hh * 64:(hh + 1) * 64, :], R_qT, [R_qT], [R_qs[b][h]], ser=False)
                sc.dma(ks[b][h, 0:64, :], kT[hh * 64:(hh + 1) * 64, :], R_kT, [R_kT], [R_ks[b][h]], ser=False)
                sc.dma(vs[b][h].rearrange("p (t d) -> p t d", d=W), vh2[s_][hh], R_vh2[s_][hh],
                       [R_vh2[s_][hh]], [R_vs[b][h]])
        A.release(m)
        sc.barrier()

    def phase_fox_F(l):
        m = A.mark()
        wv = w_in[l].rearrange("(kc p) n -> p kc n", p=128)
        stg = A.alloc([128, KC, 8], F32)
        wF = A.alloc([128, KC, 8], BF16)
        R_stg, R_wF = Res("stgF"), Res("wF")
        bF = A.alloc([128, 8], F32)
        R_bF = Res("bF")
        lf = A.alloc([128, NT, 8], F32)
        R_lf = Res("lf")
        sc.dma(stg, wv[:, :, OFF_FOXF:OFF_FOXF + 8], R_stg, [R_none], [R_stg])
        cast(wF, stg, [R_stg], [R_wF])
        bcast_load(bF, b_fox_f[l:l + 1, :], R_bF, [R_none])

        def mmf(e):
            for t in range(NT):
                for kc in range(KC):
                    ins = e.matmul(psf(0)[:, t * 8:(t + 1) * 8], uT[:, kc, t * 128:(t + 1) * 128], wF[:, kc, :],
                                   start=(kc == 0), stop=(kc == KC - 1))
            return ins
        sc.op("pe", [R_wF] + R_uT, [PB[0]], mmf)
        for t in range(NT):
            sc.op("dve", [PB[0], R_bF], [R_lf], lambda e, t=t: e.tensor_tensor(
                out=lf[:, t, :], in0=psf(0)[:, t * 8:(t + 1) * 8], in1=bF, op=ALU.add))
        lf2 = lf.rearrange("p t h -> p (t h)")
        sc.op("act", [R_lf], [R_lf], lambda e: e.activation(out=lf2, in_=lf2, func=AF.Exp, scale=-1.0))
        sc.op("act", [R_lf], [R_lf], lambda e: e.activation(out=lf2, in_=lf2, func=AF.Ln, bias=1.0, scale=1.0))
        f8 = [A.alloc([8, 512], F32) for _ in range(2)]
        r1 = [A.alloc([8, 512], F32) for _ in range(2)]
        fall = [A.alloc([8, 3, 512], BF16) for _ in range(2)]
        nfall = [A.alloc([8, 3, 512], BF16) for _ in range(2)]
        R_f8 = [Res(f"f8{i}") for i in range(2)]
        R_r1 = [Res(f"r1{i}") for i in range(2)]
        R_fall = [Res(f"fall{i}") for i in range(2)]
        R_nfall = [Res(f"nfall{i}") for i in range(2)]
        for qc in range(NQ):
            s_ = qc % 2
            pb = 1 + qc % 2

            def mmc(e, qc=qc, pb=pb):
                for j in range(4):
                    ti = qc * 4 + j
                    for tj in range(ti + 1):
                        ins = e.matmul(psf(pb)[0:8, j * 128:(j + 1) * 128], lf[:, tj, :],
                                       ntri2_f if tj == ti else nones_f, start=(tj == 0), stop=(tj == ti))
                return ins
            sc.op("pe", [R_lf, R_cst], [PB[pb]], mmc)
            f, r, fa, nfa = f8[s_], r1[s_], fall[s_], nfall[s_]
            Rf, Rr, Rfa, Rnfa = R_f8[s_], R_r1[s_], R_fall[s_], R_nfall[s_]
            sc.op("dve", [PB[pb]], [Rf], lambda e, f=f, pb=pb: e.tensor_scalar(
                out=f, in0=psf(pb)[0:8, :], scalar1=8.0, scalar2=None, op0=ALU.mult))
            sc.op("dve", [Rf], [Rfa], lambda e, f=f, fa=fa: e.tensor_copy(out=fa[:, 0, :], in_=f))
            sc.op("dve", [Rf, Rfa], [Rr], lambda e, f=f, fa=fa, r=r: e.tensor_tensor(
                out=r, in0=f, in1=fa[:, 0, :], op=ALU.subtract))
            sc.op("dve", [Rr], [Rfa], lambda e, fa=fa, r=r: e.tensor_copy(out=fa[:, 1, :], in_=r))
            sc.op("dve", [Rr, Rfa], [Rr], lambda e, fa=fa, r=r: e.tensor_tensor(
                out=r, in0=r, in1=fa[:, 1, :], op=ALU.subtract))
            sc.op("dve", [Rr], [Rfa], lambda e, fa=fa, r=r: e.tensor_copy(out=fa[:, 2, :], in_=r))
            sc.op("dve", [Rfa], [Rnfa], lambda e, fa=fa, nfa=nfa: e.tensor_scalar(
                out=nfa, in0=fa, scalar1=-1.0, scalar2=None, op0=ALU.mult))
            sc.dma(qs[0][:, 64:67, qc * 512:(qc + 1) * 512], fa, Rfa, [Rfa], [R_qF])
            sc.dma(ks[0][:, 67:70, qc * 512:(qc + 1) * 512], nfa, Rnfa, [Rnfa], [R_kF])
        A.release(m)
        sc.barrier()

    def phase_proj_mla(l):
        m = A.mark()
        b = 1
        wv = w_in[l].rearrange("(kc p) n -> p kc n", p=128)
        stg = A.alloc([128, KC, 672], F32)
        R_stg = Res("stgm")
        wl = A.alloc([128, KC, 640], BF16)
        wkr = A.alloc([128, KC, 96], BF16)
        wkrs = A.alloc([128, KC, 96], BF16)
        R_wl, R_wkr = Res("wl"), Res("wkr")
        sc.dma(stg, wv[:, :, OFF_QL:OFF_QL + 672], R_stg, [R_none], [R_stg])
        cast(wl, stg[:, :, 0:640], [R_stg], [R_wl])
        sc.op("pool", [R_stg], [R_wkr], lambda e: e.tensor_copy(out=wkr, in_=stg[:, :, 576:672]))
        sc.op("pool", [R_stg], [R_wkr], lambda e: e.tensor_copy(out=wkrs[:, :, 0:64], in_=stg[:, :, 576:640]))
        sc.op("pool", [R_stg], [R_wkr], lambda e: e.tensor_copy(out=wkrs[:, :, 64:80], in_=stg[:, :, 656:672]))
        sc.op("pool", [R_stg], [R_wkr], lambda e: e.tensor_copy(out=wkrs[:, :, 80:96], in_=stg[:, :, 640:656]))
        stq = A.alloc([128, 3, 768], F32)
        R_stq = Res("stq")
        wq = A.alloc([128, 3, 768], BF16)
        wqs = A.alloc([128, 3, 768], BF16)
        R_wq = Res("wq")
        sc.dma(stq, w_mla_uq[l].rearrange("(kc p) n -> p kc n", p=128), R_stq, [R_none], [R_stq])
        cast(wq, stq, [R_stq], [R_wq])
        stq4 = stq.rearrange("p k (h c) -> p k h c", c=96)
        wqs4 = wqs.rearrange("p k (h c) -> p k h c", c=96)
        for kc in range(3):
            sc.op("pool", [R_stq], [R_wq], lambda e, kc=kc: e.tensor_copy(out=wqs4[:, kc, :, 0:64], in_=stq4[:, kc, :, 0:64]))
            sc.op("pool", [R_stq], [R_wq], lambda e, kc=kc: e.tensor_copy(out=wqs4[:, kc, :, 64:80], in_=stq4[:, kc, :, 80:96]))
            sc.op("pool", [R_stq], [R_wq], lambda e, kc=kc: e.tensor_copy(out=wqs4[:, kc, :, 80:96], in_=stq4[:, kc, :, 64:80]))
        stkv = A.alloc([128, 2, 1024], F32)
        R_stkv = Res("stkv")
        wkv = A.alloc([128, 2, 1024], BF16)
        R_wkv = Res("wkv")
        sc.dma(stkv, w_mla_ukv[l].rearrange("(kc p) n -> p kc n", p=128), R_stkv, [R_none], [R_stkv])
        cast(wkv, stkv, [R_stkv], [R_wkv])
        wkk = A.alloc([128, 2, 512], BF16)
        wvv = A.alloc([128, 2, 512], BF16)
        stkv4 = stkv.rearrange("p k (h c) -> p k h c", c=128)
        for kc in range(2):
            sc.op("pool", [R_stkv], [R_wkv], lambda e, kc=kc: e.tensor_copy(
                out=wkk[:, kc, :].rearrange("p (h c) -> p h c", c=64), in_=stkv4[:, kc, :, 0:64]))
            sc.op("pool", [R_stkv], [R_wkv], lambda e, kc=kc: e.tensor_copy(
                out=wvv[:, kc, :].rearrange("p (h c) -> p h c", c=64), in_=stkv4[:, kc, :, 64:128]))
        gq = A.alloc([128, 384], F32)
        gkv = A.alloc([128, 256], F32)
        R_gq, R_gkv = Res("gq"), Res("gkv")
        bcast_load(gq, g_mla_q[l:l + 1, :], R_gq, [R_none])
        bcast_load(gkv, g_mla_kv[l:l + 1, :], R_gkv, [R_none])
        posi = A.alloc([96, 512], I32)
        ang = A.alloc([96, 512], F32)
        cs = A.alloc([96, 512], F32)
        ss_ = A.alloc([96, 512], F32)
        R_posi, R_ang, R_cs, R_ss = Res("posi"), Res("ang"), Res("cs"), Res("ssn")
        cqT = A.alloc([128, 3, 512], BF16)
        ckvT = A.alloc([128, 2, 512], BF16)
        R_cqT, R_ckvT = Res("cqT"), Res("ckvT")
        lat = [A.alloc([128, 640], F32) for _ in range(2)]
        R_lat = [Res(f"lat{i}") for i in range(2)]
        latb = [A.alloc([128, 640], BF16) for _ in range(2)]
        R_latb = [Res(f"latb{i}") for i in range(2)]
        st2 = [A.alloc([128, 2], F32) for _ in range(2)]
        R_st2 = [Res(f"st2{i}") for i in range(2)]
        junk = A.alloc([128, 384], BF16)
        R_junk = Res("junkm")
        krope = A.alloc([96, 512], BF16)
        R_krope = Res("krope")
        t1 = [A.alloc([96, 512], F32) for _ in range(2)]
        t2 = [A.alloc([96, 512], F32) for _ in range(2)]
        R_t1 = [Res(f"t1{i}") for i in range(2)]
        R_t2 = [Res(f"t2{i}") for i in range(2)]
        qh = [A.alloc([96, 512], BF16) for _ in range(2)]
        kh = [A.alloc([96, 512], BF16) for _ in range(2)]
        R_qh = [Res(f"qh{i}") for i in range(2)]
        R_kh = [Res(f"kh{i}") for i in range(2)]
        vm = A.alloc([128, 8, 4, 128], BF16)
        R_vsb = Res("vsbm")
        sc.op("pool", [], [R_vsb], lambda e: e.memset(vm[:, :, :, 64:128], 1.0))
        inv_c = cst[64:96, C_INV:C_INV + 1]
        sgn_c = cst[64:96, C_SGN:C_SGN + 1]
        P = slice(64, 96)
        import os as _os
        kmla = int(_os.environ.get("KMLA", "99"))

        def _bail():
            A.release(m)
            sc.barrier()
        if kmla <= 1:
            return _bail()
        for qc in range(NQ):
            cols = slice(qc * 512, (qc + 1) * 512)
            sc.dma(posi[P, :], pos[0:1, cols].partition_broadcast(32), R_posi, [R_none], [R_posi])
            sc.op("dve", [R_posi], [R_ang], lambda e: e.tensor_copy(out=ang[P, :], in_=posi[P, :]))
            sc.op("dve", [R_ang, R_cst], [R_ang], lambda e: e.tensor_scalar(
                out=ang[P, :], in0=ang[P, :], scalar1=inv_c, scalar2=None, op0=ALU.mult))
            sc.op("dve", [R_ang], [R_cs], lambda e: e.tensor_scalar(
                out=cs[P, :], in0=ang[P, :], scalar1=1.0 / TWO_PI, scalar2=None, op0=ALU.mult))
            sc.op("dve", [R_cs], [R_posi], lambda e: e.tensor_copy(out=posi[P, :], in_=cs[P, :]))
            sc.op("dve", [R_posi], [R_cs], lambda e: e.tensor_copy(out=cs[P, :], in_=posi[P, :]))
            sc.op("dve", [R_cs, R_ang], [R_ang], lambda e: e.scalar_tensor_tensor(
                out=ang[P, :], in0=cs[P, :], scalar=-CW1, in1=ang[P, :], op0=ALU.mult, op1=ALU.add))
            sc.op("dve", [R_cs, R_ang], [R_ang], lambda e: e.scalar_tensor_tensor(
                out=ang[P, :], in0=cs[P, :], scalar=-CW2, in1=ang[P, :], op0=ALU.mult, op1=ALU.add))
            sc.op("dve", [R_ang], [R_cs], lambda e: e.tensor_scalar(
                out=cs[P, :], in0=ang[P, :], scalar1=PI, scalar2=TWO_PI, op0=ALU.is_gt, op1=ALU.mult))
            sc.op("dve", [R_cs, R_ang], [R_ang], lambda e: e.tensor_tensor(
                out=ang[P, :], in0=ang[P, :], in1=cs[P, :], op=ALU.subtract))
            sc.op("dve", [R_ang], [R_cs], lambda e: e.tensor_scalar(
                out=cs[P, :], in0=ang[P, :], scalar1=-PI, scalar2=TWO_PI, op0=ALU.is_lt, op1=ALU.mult))
            sc.op("dve", [R_cs, R_ang], [R_ang], lambda e: e.tensor_tensor(
                out=ang[P, :], in0=ang[P, :], in1=cs[P, :], op=ALU.add))
            sc.op("dve", [R_ang], [R_cs], lambda e: e.tensor_scalar(
                out=cs[P, :], in0=ang[P, :], scalar1=PI / 2, scalar2=None, op0=ALU.add))
            sc.op("dve", [R_cs], [R_ss], lambda e: e.tensor_scalar(
                out=ss_[P, :], in0=cs[P, :], scalar1=PI, scalar2=TWO_PI, op0=ALU.is_gt, op1=ALU.mult))
            sc.op("dve", [R_cs, R_ss], [R_cs], lambda e: e.tensor_tensor(
                out=cs[P, :], in0=cs[P, :], in1=ss_[P, :], op=ALU.subtract))
            sc.op("act", [R_cs], [R_cs], lambda e: e.activation(out=cs[P, :], in_=cs[P, :], func=AF.Sin))
            sc.op("act", [R_ang], [R_ss], lambda e: e.activation(out=ss_[P, :], in_=ang[P, :], func=AF.Sin))
            sc.op("dve", [R_ss, R_cst], [R_ss], lambda e: e.tensor_scalar(
                out=ss_[P, :], in0=ss_[P, :], scalar1=sgn_c, scalar2=None, op0=ALU.mult))
            if kmla <= 2:
                return _bail()
            def lat_A(j):
                    t = qc * 4 + j
                    s_ = j % 2
                    tok = slice(t * 128, (t + 1) * 128)
                    pA, pB = (0, 1) if j % 2 == 0 else (6, 7)

                    def mml(e, tok=tok, pA=pA, pB=pB):
                        for kc in range(KC):
                            e.matmul(psf(pA)[:, 0:384], uT[:, kc, tok], wl[:, kc, 0:384], start=(kc == 0), stop=(kc == KC - 1))
                        for kc in range(KC):
                            ins = e.matmul(psf(pB)[:, 0:256], uT[:, kc, tok], wl[:, kc, 384:640], start=(kc == 0), stop=(kc == KC - 1))
                        return ins
                    sc.op("pe", [R_wl, R_uT[qc]], [PB[pA], PB[pB]], mml)

            def lat_B(j):
                    t = qc * 4 + j
                    s_ = j % 2
                    tok = slice(t * 128, (t + 1) * 128)
                    pA, pB = (0, 1) if j % 2 == 0 else (6, 7)
                    la, lb, st, Rla, Rlb, Rst = lat[s_], latb[s_], st2[s_], R_lat[s_], R_latb[s_], R_st2[s_]
                    sc.op("act", [PB[pA]], [R_junk, Rst], lambda e, st=st: e.activation(
                        out=junk[:, 0:384], in_=psf(pA)[:, 0:384], func=AF.Square, accum_out=st[:, 0:1]))
                    sc.op("act", [PB[pB]], [R_junk, Rst], lambda e, st=st: e.activation(
                        out=junk[:, 0:256], in_=psf(pB)[:, 0:256], func=AF.Square, accum_out=st[:, 1:2]))
                    sc.op("dve", [Rst], [Rst], lambda e, st=st: e.tensor_scalar(
                        out=st[:, 0:1], in0=st[:, 0:1], scalar1=1.0 / 384, scalar2=EPS, op0=ALU.mult, op1=ALU.add))
                    sc.op("dve", [Rst], [Rst], lambda e, st=st: e.tensor_scalar(
                        out=st[:, 1:2], in0=st[:, 1:2], scalar1=1.0 / 256, scalar2=EPS, op0=ALU.mult, op1=ALU.add))
                    sc.op("act", [Rst], [Rst], lambda e, st=st: e.activation(out=st, in_=st, func=AF.Ln))
                    sc.op("act", [Rst], [Rst], lambda e, st=st: e.activation(out=st, in_=st, func=AF.Exp, scale=-0.5))
                    sc.op("dve", [PB[pA], Rst, R_gq], [Rlb], lambda e, st=st, lb=lb: e.scalar_tensor_tensor(
                        out=lb[:, 0:384], in0=psf(pA)[:, 0:384], scalar=st[:, 0:1], in1=gq, op0=ALU.mult, op1=ALU.mult))
                    sc.op("dve", [PB[pB], Rst, R_gkv], [Rlb], lambda e, st=st, lb=lb: e.scalar_tensor_tensor(
                        out=lb[:, 384:640], in0=psf(pB)[:, 0:256], scalar=st[:, 1:2], in1=gkv, op0=ALU.mult, op1=ALU.mult))
                    pT = 2 if j % 2 == 0 else 5

                    def trl(e, lb=lb, pT=pT):
                        for kc in range(5):
                            ins = e.transpose(psb(pT)[:, kc * 128:(kc + 1) * 128], lb[:, kc * 128:(kc + 1) * 128], ident)
                        return ins
                    sc.op("pe", [Rlb, R_cbf], [PB[pT]], trl)
                    srcq = psb(pT)[:, 0:384].rearrange("p (a b) -> p a b", b=128)
                    srck = psb(pT)[:, 384:640].rearrange("p (a b) -> p a b", b=128)
                    sc.op("act", [PB[pT]], [R_cqT], lambda e, j=j, srcq=srcq: e.copy(out=cqT[:, :, j * 128:(j + 1) * 128], in_=srcq))
                    sc.op("dve", [PB[pT]], [R_ckvT], lambda e, j=j, srck=srck: e.tensor_copy(
                        out=ckvT[:, :, j * 128:(j + 1) * 128], in_=srck))

            lat_A(0)
            for j in range(4):
                if j + 1 < 4:
                    lat_A(j + 1)
                lat_B(j)
            if kmla <= 3:
                return _bail()
            pA, pB = 3, 4
            proj_featmajor(wkr, R_wkr, 0, 96, KC, lambda kc, cols=cols: uT[:, kc, cols], [R_uT[qc]], pA)
            proj_featmajor(wkrs, R_wkr, 0, 96, KC, lambda kc, cols=cols: uT[:, kc, cols], [R_uT[qc]], pB)
            sc.op("dve", [PB[pA], R_cs], [R_t1[0]], lambda e: e.tensor_tensor(
                out=t1[0][P, :], in0=psf(pA)[P, :], in1=cs[P, :], op=ALU.mult))
            sc.op("dve", [PB[pB], R_ss], [R_t2[0]], lambda e: e.tensor_tensor(
                out=t2[0][P, :], in0=psf(pB)[P, :], in1=ss_[P, :], op=ALU.mult))
            sc.op("pool", [R_t1[0], R_t2[0]], [R_krope], lambda e: e.tensor_tensor(
                out=krope[P, :], in0=t1[0][P, :], in1=t2[0][P, :], op=ALU.add))
            if kmla <= 4:
                return _bail()
            for h in range(8):
                s_ = h % 2
                pA, pB, pK = 3 + 3 * s_ - 3 * s_, 4, 5
                pA = 3 if s_ == 0 else 6
                pB = 4 if s_ == 0 else 7
                pK = 5 if s_ == 0 else 2
                proj_featmajor(wq, R_wq, h * 96, 96, 3, lambda kc: cqT[:, kc, :], [R_cqT], pA)
                proj_featmajor(wqs, R_wq, h * 96, 96, 3, lambda kc: cqT[:, kc, :], [R_cqT], pB)
                q_, k_, Rq, Rk = qh[s_], kh[s_], R_qh[s_], R_kh[s_]
                a1, a2, Ra1, Ra2 = t1[s_], t2[s_], R_t1[s_], R_t2[s_]
                sc.op("act", [PB[pA]], [Rq], lambda e, q_=q_, pA=pA: e.copy(out=q_[0:64, :], in_=psf(pA)[0:64, :]))
                sc.op("dve", [PB[pA], R_cs], [Ra1], lambda e, a1=a1, pA=pA: e.tensor_tensor(
                    out=a1[P, :], in0=psf(pA)[P, :], in1=cs[P, :], op=ALU.mult))
                sc.op("dve", [PB[pB], R_ss], [Ra2], lambda e, a2=a2, pB=pB: e.tensor_tensor(
                    out=a2[P, :], in0=psf(pB)[P, :], in1=ss_[P, :], op=ALU.mult))
                sc.op("pool", [Ra1, Ra2], [Rq], lambda e, a1=a1, a2=a2, q_=q_: e.tensor_tensor(
                    out=q_[P, :], in0=a1[P, :], in1=a2[P, :], op=ALU.add))
                sc.dma(qs[b][h, :, cols], q_[0:96, :], Rq, [Rq], [R_qs[b][h]])
                def mmk(e, h=h, pK=pK):
                    for kc in range(2):
                        ins = e.matmul(psf(pK)[0:64, :], wkk[:, kc, h * 64:(h + 1) * 64], ckvT[:, kc, :],
                                       start=(kc == 0), stop=(kc == 1))
                    return ins
                sc.op("pe", [R_wkv, R_ckvT], [PB[pK]], mmk)
                sc.op("act", [PB[pK]], [Rk], lambda e, k_=k_, pK=pK: e.copy(out=k_[0:64, :], in_=psf(pK)[0:64, :]))
                sc.dma(ks[b][h, 0:64, cols], k_[0:64, :], Rk, [Rk], [R_ks[b][h]])
                sc.dma(ks[b][h, 64:96, cols], krope[P, :], R_krope, [R_krope], [R_ks[b][h]], ser=False)
            if kmla <= 5:
                return _bail()
            for j in range(4):
                pV = j % 2

                def mmv(e, j=j, pV=pV):
                    for kc in range(2):
                        ins = e.matmul(psf(pV), ckvT[:, kc, j * 128:(j + 1) * 128],
                                       wvv[:, kc, :], start=(kc == 0), stop=(kc == 1))
                    return ins
                sc.op("pe", [R_wkv, R_ckvT], [PB[pV]], mmv)
                sc.op("dve", [PB[pV]], [R_vsb], lambda e, j=j, pV=pV: e.tensor_copy(
                    out=vm[:, :, j, 0:64], in_=psf(pV).rearrange("p (h c) -> p h c", c=64)))
            for h in range(8):
                dst = vs[b][h].rearrange("p (t d) -> p t d", d=128)[:, qc * 4:(qc + 1) * 4, :]
                sc.dma(dst, vm[:, h, :, :], R_vsb, [R_vsb], [R_vs[b][h]], ser=False)
        A.release(m)
        sc.barrier()

    bg = [None]
    MW = {}

    def bg_step():
        if bg[0] is not None:
            try:
                next(bg[0])
            except StopIteration:
                bg[0] = None

    def bg_drain():
        while bg[0] is not None:
            bg_step()

    def gen_merge_weights(l):
        wv = w_in[l].rearrange("(kc p) n -> p kc n", p=128)
        wo = [A.alloc_top([128, 4, D], BF16) for _ in range(3)]
        wg = A.alloc_top([128, KC, 3 * D], BF16)
        wout = A.alloc_top([128, KC, D], BF16)
        stg = [A.alloc_top([128, KC, 256], F32) for _ in range(3)]
        R_stg = [Res(f"bstg{i}") for i in range(3)]
        R_wo = [Res(f"bwo{i}") for i in range(3)]
        R_wg, R_wout = Res("bwg"), Res("bwout")
        MW.update(wo=wo, wg=wg, wout=wout, R_wo=R_wo, R_wg=R_wg, R_wout=R_wout, stg=stg, R_stg=R_stg)
        units = []
        for b in range(3):
            wob = w_o[b][l].rearrange("(kc p) n -> p kc n", p=128)
            for half in range(2):
                units.append((wob[:, :, half * 512:(half + 1) * 512], 4, 512, wo[b][:, :, half * 512:(half + 1) * 512], R_wo[b]))
        for n in range(12):
            units.append((wv[:, :, OFF_GATE + n * 256:OFF_GATE + (n + 1) * 256], KC, 256, wg[:, :, n * 256:(n + 1) * 256], R_wg))
        wov = w_out[l].rearrange("(kc p) n -> p kc n", p=128)
        for n in range(4):
            units.append((wov[:, :, n * 256:(n + 1) * 256], KC, 256, wout[:, :, n * 256:(n + 1) * 256], R_wout))
        pend = []
        for i, (src, a_, c_, dst, Rd) in enumerate(units):
            s_ = i % 3
            stv = stg[s_].rearrange("p k c -> p (k c)").rearrange("p (k c) -> p k c", c=c_)
            if len(pend) == 2:
                pstv, pdst, pRd, ps_ = pend.pop(0)
                sc.op("dve", [R_stg[ps_]], [pRd], lambda e, pstv=pstv, pdst=pdst: e.tensor_copy(out=pdst, in_=pstv))
            sc.dma(stv, src, R_stg[s_], [R_none], [R_stg[s_]])
            pend.append((stv, dst, Rd, s_))
            yield
        for (pstv, pdst, pRd, ps_) in pend:
            sc.op("dve", [R_stg[ps_]], [pRd], lambda e, pstv=pstv, pdst=pdst: e.tensor_copy(out=pdst, in_=pstv))
            yield

    mod_done = set()

    def gen_mod(l2):
        stg, R_stg = MW["stg"], MW["R_stg"]
        ct = A.alloc_top([128, 8], F32)
        cond = A.alloc_top([128, 8], F32)
        R_ct, R_cond = Res("bct"), Res("bcond")
        brow = [A.alloc_top([128, 256], F32) for _ in range(3)]
        mrow = [A.alloc_top([128, 256], F32) for _ in range(3)]
        R_brow = [Res(f"bbrow{i}") for i in range(3)]
        R_mrow = [Res(f"bmrow{i}") for i in range(3)]
        sc.dma(ct, c_t, R_ct, [R_none], [R_ct])
        sc.op("act", [R_ct], [R_cond], lambda e: e.activation(out=cond, in_=ct, func=AF.Silu))
        wv = w_ada[l2].rearrange("(kc p) n -> p kc n", p=128)
        pb = 7
        pend = []

        def finish(n, s_):
            def mm(e):
                for kc in range(KC):
                    ins = e.matmul(psf(pb)[0:1, 0:256], cond[:, kc:kc + 1], stg[s_][:, kc, :],
                                   start=(kc == 0), stop=(kc == KC - 1))
                return ins
            sc.op("pe", [R_cond, R_stg[s_]], [PB[pb]], mm)
            sc.op("dve", [PB[pb], R_brow[s_]], [R_mrow[s_]], lambda e: e.tensor_tensor(
                out=mrow[s_][0:1, :], in0=psf(pb)[0:1, 0:256], in1=brow[s_][0:1, :], op=ALU.add))
            sc.dma(modrow[l2:l2 + 1, n * 256:(n + 1) * 256], mrow[s_][0:1, :], R_mrow[s_], [R_mrow[s_]], [R_mod[l2][n // 2]])
        for n in range(24):
            s_ = n % 3
            if len(pend) == 2:
                finish(*pend.pop(0))
            sc.dma(stg[s_], wv[:, :, n * 256:(n + 1) * 256], R_stg[s_], [R_none], [R_stg[s_]])
            sc.dma(brow[s_][0:1, :], b_ada[l2:l2 + 1, n * 256:(n + 1) * 256], R_brow[s_], [R_none], [R_brow[s_]])
            pend.append((n, s_))
            yield
        for p in pend:
            finish(*p)
            yield
        mod_done.add(l2)

    def gen_bg(l):
        yield from gen_merge_weights(l)
        if l + 1 < depth:
            yield from gen_mod(l + 1)

    def start_bg(l):
        bg[0] = gen_bg(l)

    def phase_attn_softmax(b, scale):
        A.release(cbf_mark)
        m = A.mark()
        K = KQ[b]
        KP = 96 if b == 0 else K
        qsb = [A.alloc([128, S], BF16) for _ in range(2)]
        ksb = [A.alloc([128, S], BF16) for _ in range(2)]
        vsb = [A.alloc([128, NT, 128], BF16) for _ in range(2)]
        R_q = [Res(f"aq{i}") for i in range(2)]
        R_k = [Res(f"ak{i}") for i in range(2)]
        R_v = [Res(f"av{i}") for i in range(2)]
        R_kf = [Res(f"akf{i}") for i in range(2)]
        for i in range(2):
            if b == 0:
                sc.op("pool", [], [R_q[i]], lambda e, i=i: e.memset(qsb[i][64:96, :], 0.0))
                sc.op("pool", [], [R_k[i]], lambda e, i=i: e.memset(ksb[i][64:96, :], 0.0))
                sc.op("pool", [R_q[i]], [R_q[i]], lambda e, i=i: e.memset(qsb[i][64:70, :], 1.0))
                sc.op("pool", [R_k[i]], [R_k[i], R_kf[i]], lambda e, i=i: e.memset(ksb[i][64:70, :], 1.0))
        NPT = 4
        pt = [A.alloc([128, 512], BF16) for _ in range(NPT)]
        R_pt = [Res(f"pt{i}") for i in range(NPT)]
        rec = [A.alloc([64, 512], F32) for _ in range(2)]
        R_rec = [Res(f"rec{i}") for i in range(2)]
        yo = [A.alloc([64, 512], BF16) for _ in range(2)]
        R_yo = [Res(f"yo{i}") for i in range(2)]
        SBK = [0, 1, 2, 3]
        OB = [4, 5]
        mask = masks[b]

        def load_head(h):
            s_ = h % 2
            q_, k_, v_ = qsb[s_], ksb[s_], vsb[s_]
            if b == 0:
                sc.dma(q_[0:67, :], qs[b][h, 0:67, :], R_q[s_], [R_qs[b][h], R_qF], [R_q[s_]])
                sc.dma(k_[0:64, :], ks[b][h, 0:64, :], R_k[s_], [R_ks[b][h]], [R_k[s_]])
                sc.dma(k_[67:70, :], ks[b][h, 67:70, :], R_kf[s_], [R_kF], [R_kf[s_]])
            else:
                sc.dma(q_[0:K, :], qs[b][h, :, :], R_q[s_], [R_qs[b][h]], [R_q[s_]])
                sc.dma(k_[0:K, :], ks[b][h, :, :], R_k[s_], [R_ks[b][h]], [R_k[s_]])
            sc.dma(v_, vs[b][h].rearrange("p (t d) -> p t d", d=128), R_v[s_], [R_vs[b][h]], [R_v[s_]])

        tiles = []
        chain = 0
        for h in range(8):
            for qc in range(NQ):
                nkb = 4 * qc + 4
                for kb in range(nkb):
                    j = kb - 4 * qc
                    tiles.append(dict(h=h, s=h % 2, qc=qc, kb=kb, j=j, c0=(128 * j if j > 0 else 0), first=(kb == 0),
                                      last=(kb == nkb - 1), ob=OB[chain % 2], os=chain % 2,
                                      lasthead=(kb == nkb - 1 and qc == NQ - 1)))
                chain += 1
        T = len(tiles)

        def emit_qk(t):
            d = tiles[t]
            sbk = SBK[t % 4]
            q_, k_ = qsb[d["s"]], ksb[d["s"]]
            c0, kb, qc, j = d["c0"], d["kb"], d["qc"], d["j"]

            def f(e):
                ins = e.matmul(psf(sbk)[:, c0:512], k_[0:KP, kb * 128:(kb + 1) * 128],
                               q_[0:KP, qc * 512 + c0:(qc + 1) * 512], start=True, stop=(j < 0))
                if j >= 0:
                    ins = e.matmul(psf(sbk)[:, c0:c0 + 128], ident, mask, start=False, stop=True)
                return ins
            sc.op("pe", [R_q[d["s"]], R_k[d["s"]], R_kf[d["s"]], R_cbf], [PB[sbk]], f)

        def emit_exp(t):
            d = tiles[t]
            sbk = SBK[t % 4]
            pi = t % NPT
            c0 = d["c0"]
            sc.op("act", [PB[sbk]], [R_pt[pi]], lambda e: e.activation(
                out=pt[pi][:, c0:512], in_=psf(sbk)[:, c0:512], func=AF.Exp, scale=scale))

        def emit_pv(t):
            d = tiles[t]
            pi = t % NPT
            c0, kb, ob, os_ = d["c0"], d["kb"], d["ob"], d["os"]
            v_ = vsb[d["s"]]
            sc.op("pe", [R_v[d["s"]], R_pt[pi]], [PB[ob]], lambda e: e.matmul(
                psf(ob)[:, c0:512], v_[:, kb, :], pt[pi][:, c0:512], start=d["first"], stop=d["last"]))
            if d["last"]:
                h, qc = d["h"], d["qc"]
                sc.op("dve", [PB[ob]], [R_rec[os_]], lambda e: e.reciprocal(out=rec[os_], in_=psf(ob)[64:128, :]))
                sc.op("dve", [PB[ob], R_rec[os_]], [R_yo[os_]], lambda e: e.tensor_tensor(
                    out=yo[os_], in0=psf(ob)[0:64, :], in1=rec[os_], op=ALU.mult))
                sc.dma(ybr[b, h * 64:(h + 1) * 64, qc * 512:(qc + 1) * 512], yo[os_], R_yo[os_], [R_yo[os_]],
                       [R_ybr[b][h][qc]])
            if d["lasthead"] and d["h"] + 2 < 8:
                load_head(d["h"] + 2)

        load_head(0)
        load_head(1)
        for t in range(-2, T):
            if t + 2 < T:
                emit_qk(t + 2)
            if 0 <= t + 1 < T:
                emit_exp(t + 1)
            if t >= 0:
                emit_pv(t)
            if t % 16 == 0:
                bg_step()
        A.release(persist_mark)
        sc.barrier()

    def phase_attn_sb():
        b = 2
        A.release(cbf_mark)
        m = A.mark()
        qsb = [A.alloc([64, S], BF16) for _ in range(2)]
        ksb = [A.alloc([64, S], BF16) for _ in range(2)]
        vsb = [A.alloc([128, NT, 64], BF16) for _ in range(2)]
        R_q = [Res(f"sq{i}") for i in range(2)]
        R_k = [Res(f"sk{i}") for i in range(2)]
        R_v = [Res(f"sv{i}") for i in range(2)]
        NB = 4
        eb = [A.alloc([128, 512], F32) for _ in range(NB)]
        spb = [A.alloc([128, 512], BF16) for _ in range(NB)]
        ecb = [A.alloc([128, 512], F32) for _ in range(NB)]
        ab = [A.alloc([128, 512], BF16) for _ in range(NB)]
        R_e = [Res(f"e{i}") for i in range(NB)]
        R_sp = [Res(f"sp{i}") for i in range(NB)]
        R_ec = [Res(f"ec{i}") for i in range(NB)]
        R_a = [Res(f"a{i}") for i in range(NB)]
        yo = [A.alloc([64, 512], BF16) for _ in range(2)]
        R_yo = [Res(f"syo{i}") for i in range(2)]
        ZB = [0, 1, 2]
        ACC = [3, 4]
        OB = [5, 6]
        mask = masks[2]

        def load_head(h):
            s_ = h % 2
            sc.dma(qsb[s_], qs[b][h, :, :], R_q[s_], [R_qs[b][h]], [R_q[s_]])
            sc.dma(ksb[s_], ks[b][h, :, :], R_k[s_], [R_ks[b][h]], [R_k[s_]])
            sc.dma(vsb[s_], vs[b][h].rearrange("p (t d) -> p t d", d=64), R_v[s_], [R_vs[b][h]], [R_v[s_]])

        tiles = []
        chain = 0
        for h in range(8):
            for qc in range(NQ):
                nkb = 4 * qc + 4
                for idx, kb in enumerate(range(nkb - 1, -1, -1)):
                    j = kb - 4 * qc
                    tiles.append(dict(h=h, s=h % 2, qc=qc, kb=kb, j=j, c0=(128 * j if j > 0 else 0), first=(idx == 0),
                                      last=(idx == nkb - 1), acc=ACC[chain % 2], ob=OB[chain % 2], os=chain % 2,
                                      lasthead=(idx == nkb - 1 and qc == NQ - 1)))
                chain += 1
        T = len(tiles)

        def st_qk(t):
            d = tiles[t]
            zb = ZB[t % 3]
            q_, k_ = qsb[d["s"]], ksb[d["s"]]
            c0, kb, qc, j = d["c0"], d["kb"], d["qc"], d["j"]

            def f(e):
                ins = e.matmul(psf(zb)[:, c0:512], k_[:, kb * 128:(kb + 1) * 128],
                               q_[:, qc * 512 + c0:(qc + 1) * 512], start=True, stop=(j < 0))
                if j >= 0:
                    ins = e.matmul(psf(zb)[:, c0:c0 + 128], ident, mask, start=False, stop=True)
                return ins
            sc.op("pe", [R_q[d["s"]], R_k[d["s"]], R_cbf], [PB[zb]], f)

        def st_act1(t):
            d = tiles[t]
            zb = ZB[t % 3]
            bi = t % NB
            c0 = d["c0"]
            sc.op("act", [PB[zb]], [R_e[bi]], lambda e: e.activation(
                out=eb[bi][:, c0:512], in_=psf(zb)[:, c0:512], func=AF.Exp, scale=0.125))
            sc.op("act", [R_e[bi]], [R_sp[bi]], lambda e: e.activation(
                out=spb[bi][:, c0:512], in_=eb[bi][:, c0:512], func=AF.Ln, bias=1.0, scale=1.0))

        def st_tri(t):
            d = tiles[t]
            bi = t % NB
            c0, acc = d["c0"], d["acc"]
            sc.op("pe", [R_sp[bi], R_cbf], [PB[acc]], lambda e: e.matmul(
                psf(acc)[:, c0:512], ntri, spb[bi][:, c0:512], start=d["first"], stop=True, skip_group_check=True))

        def st_expc(t):
            d = tiles[t]
            bi = t % NB
            c0, acc = d["c0"], d["acc"]
            sc.op("act", [PB[acc]], [R_ec[bi]], lambda e: e.activation(
                out=ecb[bi][:, c0:512], in_=psf(acc)[:, c0:512], func=AF.Exp))
            sc.op("pool", [R_e[bi], R_ec[bi]], [R_a[bi]], lambda e: e.tensor_tensor(
                out=ab[bi][:, c0:512], in0=eb[bi][:, c0:512], in1=ecb[bi][:, c0:512], op=ALU.mult))

        def st_u(t):
            d = tiles[t]
            if d["last"]:
                return
            bi = t % NB
            c0, acc = d["c0"], d["acc"]
            sc.op("pe", [R_sp[bi], R_cbf], [PB[acc]], lambda e: e.matmul(
                psf(acc)[:, c0:512], nu, spb[bi][:, c0:512], start=False, stop=True, skip_group_check=True))

        def st_pv(t):
            d = tiles[t]
            bi = t % NB
            c0, kb, ob, os_ = d["c0"], d["kb"], d["ob"], d["os"]
            v_ = vsb[d["s"]]
            sc.op("pe", [R_v[d["s"]], R_a[bi]], [PB[ob]], lambda e: e.matmul(
                psf(ob)[0:64, c0:512], v_[:, kb, :], ab[bi][:, c0:512], start=d["first"], stop=d["last"],
                skip_group_check=True))
            if d["last"]:
                h, qc = d["h"], d["qc"]
                sc.op("dve", [PB[ob]], [R_yo[os_]], lambda e: e.tensor_copy(out=yo[os_], in_=psf(ob)[0:64, :]))
                sc.dma(ybr[b, h * 64:(h + 1) * 64, qc * 512:(qc + 1) * 512], yo[os_], R_yo[os_], [R_yo[os_]],
                       [R_ybr[b][h][qc]])
            if d["lasthead"] and d["h"] + 2 < 8:
                load_head(d["h"] + 2)

        load_head(0)
        load_head(1)
        for t in range(-2, T + 2):
            if 0 <= t - 1 < T:
                st_u(t - 1)
            if 0 <= t < T:
                st_tri(t)
            if 0 <= t + 2 < T:
                st_qk(t + 2)
            if 0 <= t + 1 < T:
                st_act1(t + 1)
            if 0 <= t < T:
                st_expc(t)
            if 0 <= t - 2 < T:
                st_pv(t - 2)
            if t % 16 == 0:
                bg_step()
        A.release(persist_mark)
        sc.barrier()

    def phase_merge(l):
        A.release(cbf_mark)
        m = A.mark()
        if not MW:
            bg[0] = gen_bg(l)
        bg_drain()
        wo, wg, wout = MW["wo"], MW["wg"], MW["wout"]
        R_wo, R_wg, R_wout = MW["R_wo"], MW["R_wg"], MW["R_wout"]
        GT = A.alloc([128, D], F32)
        R_GT = Res("GT")
        bcast_load(GT, modrow[l:l + 1, 2 * D:3 * D], R_GT, [R_mod[l][4], R_mod[l][5]])
        yb = [A.alloc([128, 4, 512], BF16) for _ in range(3)]
        R_yb = [Res(f"yb{b}") for b in range(3)]
        uc = [A.alloc([128, KC, 512], BF16) for _ in range(2)]
        R_uc = [Res(f"uc{i}") for i in range(2)]
        mT = A.alloc([128, KC, 512], BF16)
        R_mT = Res("mT")
        sig = [A.alloc([128, 512], F32) for _ in range(2)]
        R_sig = [Res(f"sig{i}") for i in range(2)]
        accm = [A.alloc([128, 512], F32) for _ in range(2)]
        R_accm = [Res(f"accm{i}") for i in range(2)]
        tmpm = [A.alloc([128, 512], F32) for _ in range(2)]
        R_tmpm = [Res(f"tmpm{i}") for i in range(2)]
        xt = [A.alloc([128, D], F32) for _ in range(2)]
        R_xt = [Res(f"mxt{i}") for i in range(2)]
        xo = [A.alloc([128, D], F32) for _ in range(2)]
        R_xo = [Res(f"mxo{i}") for i in range(2)]
        cnt = 0
        xcnt = 0
        for qc in range(NQ):
            cols = slice(qc * 512, (qc + 1) * 512)
            us = qc % 2
            sc.dma(uc[us], uts[:, :, cols], R_uc[us], [R_uts[qc]], [R_uc[us]])
            for b in branches:
                src = ybr[b].rearrange("(f p) s -> p f s", p=128)[:, :, cols]
                sc.dma(yb[b], src, R_yb[b], [R_ybr[b][h][qc] for h in range(8)], [R_yb[b]])
            for nci in range(KC):
                a_ = nci % 2
                for b in branches:
                    p1 = 0 + (cnt % 2)
                    p2 = 2 + (cnt % 2)
                    g_ = cnt % 2
                    cnt += 1

                    def mm1(e, b=b, p1=p1, nci=nci):
                        for f in range(4):
                            ins = e.matmul(psf(p1), wo[b][:, f, nci * 128:(nci + 1) * 128], yb[b][:, f, :],
                                           start=(f == 0), stop=(f == 3))
                        return ins
                    sc.op("pe", [R_wo[b], R_yb[b]], [PB[p1]], mm1)

                    def mm2(e, b=b, p2=p2, nci=nci):
                        for kc in range(KC):
                            ins = e.matmul(psf(p2), wg[:, kc, b * D + nci * 128:b * D + (nci + 1) * 128], uc[us][:, kc, :],
                                           start=(kc == 0), stop=(kc == KC - 1))
                        return ins
                    sc.op("pe", [R_wg, R_uc[us]], [PB[p2]], mm2)
                    sc.op("act", [PB[p2]], [R_sig[g_]], lambda e, g_=g_, p2=p2: e.activation(
                        out=sig[g_], in_=psf(p2), func=AF.Sigmoid))
                    if len(branches) == 1:
                        sc.op("dve", [PB[p1], R_sig[g_]], [R_mT], lambda e, g_=g_, p1=p1, nci=nci: e.tensor_tensor(
                            out=mT[:, nci, :], in0=psf(p1), in1=sig[g_], op=ALU.mult))
                    elif b == branches[0]:
                        sc.op("dve", [PB[p1], R_sig[g_]], [R_accm[a_]], lambda e, g_=g_, p1=p1, a_=a_: e.tensor_tensor(
                            out=accm[a_], in0=psf(p1), in1=sig[g_], op=ALU.mult))
                    else:
                        sc.op("dve", [PB[p1], R_sig[g_]], [R_tmpm[g_]], lambda e, g_=g_, p1=p1: e.tensor_tensor(
                            out=tmpm[g_], in0=psf(p1), in1=sig[g_], op=ALU.mult))
                        if b != branches[-1]:
                            sc.op("pool", [R_tmpm[g_], R_accm[a_]], [R_accm[a_]], lambda e, g_=g_, a_=a_: e.tensor_tensor(
                                out=accm[a_], in0=accm[a_], in1=tmpm[g_], op=ALU.add))
                        else:
                            sc.op("pool", [R_tmpm[g_], R_accm[a_]], [R_mT], lambda e, g_=g_, a_=a_, nci=nci: e.tensor_tensor(
                                out=mT[:, nci, :], in0=accm[a_], in1=tmpm[g_], op=ALU.add))
            for j in range(4):
                t = qc * 4 + j
                xs = xcnt % 2
                xcnt += 1
                srcx = x_in if l == 0 else y
                sc.dma(xt[xs], srcx[t * 128:(t + 1) * 128, :], R_xt[xs], [R_none if l == 0 else R_y[t]], [R_xt[xs]])
                for n in range(2):
                    po = 4 + n

                    def mmo(e, j=j, n=n, po=po):
                        for kc in range(KC):
                            ins = e.matmul(psf(po), mT[:, kc, j * 128:(j + 1) * 128], wout[:, kc, n * 512:(n + 1) * 512],
                                           start=(kc == 0), stop=(kc == KC - 1))
                        return ins
                    sc.op("pe", [R_mT, R_wout], [PB[po]], mmo)
                    sc.op("dve", [PB[po], R_GT], [R_xo[xs]], lambda e, n=n, po=po, xs=xs: e.tensor_tensor(
                        out=xo[xs][:, n * 512:(n + 1) * 512], in0=psf(po), in1=GT[:, n * 512:(n + 1) * 512], op=ALU.mult))
                sc.op("dve", [R_xo[xs], R_xt[xs]], [R_xo[xs]], lambda e, xs=xs: e.tensor_tensor(
                    out=xo[xs], in0=xo[xs], in1=xt[xs], op=ALU.add))
                sc.dma(y[t * 128:(t + 1) * 128, :], xo[xs], R_xo[xs], [R_xo[xs]], [R_y[t]])
        A.release(persist_mark)
        A.htop = A.n
        MW.clear()
        sc.barrier()

    def phase_ffn(l, final):
        import os as _os
        FT = int(_os.environ.get("FFNFT", "256"))
        A.release(cbf_mark)
        wg = A.alloc([128, KC, DFF], BF16)
        wu = A.alloc([128, KC, DFF], BF16)
        wd = A.alloc([128, FC, D], BF16)
        R_wg, R_wu, R_wd = Res("fwg"), Res("fwu"), Res("fwd")
        m = A.mark()
        stg = [A.alloc([128, KC, 256], F32) for _ in range(2)]
        R_stg = [Res(f"fstg{i}") for i in range(2)]
        ld = 0
        for (wsrc, dst, Rd) in ((w_gate, wg, R_wg), (w_up, wu, R_wu)):
            wvv = wsrc[l].rearrange("(kc p) n -> p kc n", p=128)
            for c in range(0, DFF, 256):
                s_ = ld % 2
                ld += 1
                sc.dma(stg[s_], wvv[:, :, c:c + 256], R_stg[s_], [R_none], [R_stg[s_]])
                cast(dst[:, :, c:c + 256], stg[s_], [R_stg[s_]], [Rd])
        wdv = w_down[l].rearrange("(fc p) n -> p fc n", p=128)
        for f0 in range(0, FC, 2):
            s_ = ld % 2
            ld += 1
            stv = stg[s_].rearrange("p k c -> p (k c)").rearrange("p (k c) -> p k c", c=1024)
            sc.dma(stv, wdv[:, f0:f0 + 2, :], R_stg[s_], [R_none], [R_stg[s_]])
            cast(wd[:, f0:f0 + 2, :], stv, [R_stg[s_]], [R_wd])
        sc.barrier()
        A.release(m)
        import os as _os
        kffn = int(_os.environ.get("KFFN", "9"))
        if kffn <= 1:
            A.release(persist_mark)
            return
        G, SH, R_G, R_SH = load_mod_tiles(l, 3, 4, g_ffn[l:l + 1, :])
        sc.barrier()
        A.release(A.mark() - D * 4)
        GT = A.alloc([128, D], F32)
        R_GT = Res("fGT")
        bcast_load(GT, modrow[l:l + 1, 5 * D:6 * D], R_GT, [R_mod[l][10], R_mod[l][11]])
        if final:
            GF = A.alloc([128, D], F32)
            R_GF = Res("GF")
            bcast_load(GF, g_final[0:1, :], R_GF, [R_none])
        nb = NormBufs(2, 1)
        xr = [A.alloc([128, D], F32) for _ in range(1)]
        R_xr = [Res(f"xr{i}") for i in range(1)]
        ufT2 = [A.alloc([128, KC, FT], BF16) for _ in range(2)]
        R_ufT2 = [Res(f"ufT{i}") for i in range(2)]
        hT = A.alloc([128, FC, FT], BF16)
        R_hT = Res("hT")
        sg = [A.alloc([128, FT], F32) for _ in range(2)]
        R_sg = [Res(f"fsg{i}") for i in range(2)]
        xo = [A.alloc([128, D], F32) for _ in range(2)]
        R_xo = [Res(f"fxo{i}") for i in range(2)]
        fss = [A.alloc([128, 2], F32) for _ in range(2)]
        R_fss = [Res(f"fss{i}") for i in range(2)]
        cnt = 0
        xcnt = 0
        NJ = FT // 128
        if kffn <= 2:
            sc.barrier()
            A.release(persist_mark)
            return
        ysrc = y
        NCH = S // FT

        def pro_load(ch):
            for j in range(NJ):
                t = ch * NJ + j
                sc.dma(nb.xt[j], ysrc[t * 128:(t + 1) * 128, :], nb.R_xt[j], [R_y[t]], [nb.R_xt[j]])

        def pro_norm(ch):
            for j in range(NJ):
                norm_tile(nb, j, G, SH, R_G, R_SH)

        def pro_T(ch):
            for j in range(NJ):
                transpose_tile(nb.u[j], nb.R_u[j], j % 2, ufT2[ch % 2][:, :, j * 128:(j + 1) * 128], R_ufT2[ch % 2],
                               "act" if j % 2 == 0 else "dve")

        def gu(ch, f):
            nonlocal cnt
            ufT, R_ufT = ufT2[ch % 2], R_ufT2[ch % 2]
            pg = 2 + (cnt % 2)
            pu = 4 + (cnt % 2)
            g_ = cnt % 2
            cnt += 1

            def mmg(e):
                for kc in range(KC):
                    ins = e.matmul(psf(pg)[:, 0:FT], wg[:, kc, f * 128:(f + 1) * 128], ufT[:, kc, :],
                                   start=(kc == 0), stop=(kc == KC - 1))
                return ins
            sc.op("pe", [R_wg, R_ufT], [PB[pg]], mmg)

            def mmu(e):
                for kc in range(KC):
                    ins = e.matmul(psf(pu)[:, 0:FT], wu[:, kc, f * 128:(f + 1) * 128], ufT[:, kc, :],
                                   start=(kc == 0), stop=(kc == KC - 1))
                return ins
            sc.op("pe", [R_wu, R_ufT], [PB[pu]], mmu)
            sc.op("act", [PB[pg]], [R_sg[g_]], lambda e: e.activation(out=sg[g_], in_=psf(pg)[:, 0:FT], func=AF.Silu))
            sc.op("dve", [PB[pu], R_sg[g_]], [R_hT], lambda e: e.tensor_tensor(
                out=hT[:, f, :], in0=psf(pu)[:, 0:FT], in1=sg[g_], op=ALU.mult))

        pro_load(0)
        pro_norm(0)
        pro_T(0)
        for ch in range(NCH):
            if ch + 1 < NCH:
                pro_load(ch + 1)
            for f in range(FC // 2):
                gu(ch, f)
            if ch + 1 < NCH:
                pro_norm(ch + 1)
                pro_T(ch + 1)
            for f in range(FC // 2, FC):
                gu(ch, f)
            for j in range(NJ):
                t = ch * NJ + j
                xs = xcnt % 2
                xcnt += 1
                sc.dma(xr[0], ysrc[t * 128:(t + 1) * 128, :], R_xr[0], [R_y[t]], [R_xr[0]])
                for n in range(2):
                    po = 6 + n

                    def mmd(e, j=j, n=n, po=po):
                        for f in range(FC):
                            ins = e.matmul(psf(po), hT[:, f, j * 128:(j + 1) * 128], wd[:, f, n * 512:(n + 1) * 512],
                                           start=(f == 0), stop=(f == FC - 1))
                        return ins
                    sc.op("pe", [R_hT, R_wd], [PB[po]], mmd)
                    sc.op("dve", [PB[po], R_GT], [R_xo[xs]], lambda e, n=n, po=po, xs=xs: e.tensor_tensor(
                        out=xo[xs][:, n * 512:(n + 1) * 512], in0=psf(po), in1=GT[:, n * 512:(n + 1) * 512], op=ALU.mult))
                sc.op("dve", [R_xo[xs], R_xr[0]], [R_xo[xs]], lambda e, xs=xs: e.tensor_tensor(
                    out=xo[xs], in0=xo[xs], in1=xr[0], op=ALU.add))
                if final:
                    ss = fss[xs]
                    Rss = R_fss[xs]
                    sc.op("act", [R_xo[xs]], [nb.R_junk, Rss], lambda e, xs=xs, ss=ss: e.activation(
                        out=nb.junk, in_=xo[xs], func=AF.Square, accum_out=ss[:, 0:1]))
                    sc.op("dve", [Rss], [Rss], lambda e, ss=ss: e.tensor_scalar(
                        out=ss[:, 0:1], in0=ss[:, 0:1], scalar1=1.0 / D, scalar2=EPS, op0=ALU.mult, op1=ALU.add))
                    sc.op("act", [Rss], [Rss], lambda e, ss=ss: e.activation(out=ss[:, 0:1], in_=ss[:, 0:1], func=AF.Ln))
                    sc.op("act", [Rss], [Rss], lambda e, ss=ss: e.activation(
                        out=ss[:, 0:1], in_=ss[:, 0:1], func=AF.Exp, scale=-0.5))
                    sc.op("dve", [R_xo[xs], Rss, R_GF], [R_xo[xs]], lambda e, xs=xs, ss=ss: e.scalar_tensor_tensor(
                        out=xo[xs], in0=xo[xs], scalar=ss[:, 0:1], in1=GF, op0=ALU.mult, op1=ALU.mult))
                sc.dma(y[t * 128:(t + 1) * 128, :], xo[xs], R_xo[xs], [R_xo[xs]], [R_y[t]])
        A.release(persist_mark)
        sc.barrier()

    cbf_mark = A.mark() - KC * S * 2
    assert cbf_mark >= 0

    import os as _os
    stop = int(_os.environ.get("KSTOP", "999"))
    plist = []
    for l in range(depth):
        plist.append(lambda l=l: phase_mod(l))
        plist.append(lambda l=l: phase_norm_attn(l))
        if 0 in branches:
            plist.append(lambda l=l: phase_proj_qkv(l, 0, OFF_FOXQ))
            plist.append(lambda l=l: phase_fox_F(l))
        if 1 in branches:
            plist.append(lambda l=l: phase_proj_mla(l))
        if 2 in branches:
            plist.append(lambda l=l: phase_proj_qkv(l, 2, OFF_SB))
        if 0 in branches:
            plist.append(lambda l=l: phase_attn_softmax(0, 0.125))
        if 1 in branches:
            plist.append(lambda l=l: phase_attn_softmax(1, float(96 ** -0.5)))
        if 2 in branches:
            plist.append(lambda l=l: start_bg(l))
            plist.append(lambda l=l: phase_attn_sb())
        plist.append(lambda l=l: phase_merge(l))
        plist.append(lambda l=l: phase_ffn(l, final=(l == depth - 1)))
    if _os.environ.get("ONLYFFN"):
        plist = [lambda: phase_mod(0), lambda: phase_ffn(0, final=bool(int(_os.environ.get("FFNFINAL", "1"))))]
    for i, p in enumerate(plist):
        if i >= stop:
            break
        p()
    sc.barrier(engines=("sp",))
    build.info = dict(peak=A.peak, nwait=sc.nwait, cnt=dict(sc.cnt), nsem=sc.nsem)
    return nc


_CACHE = {}


def _prep_core(inputs, bidx, S):
    c = np.asarray(inputs["c"], np.float32)[bidx]
    m = {
        "x": np.ascontiguousarray(np.asarray(inputs["x"], np.float32)[bidx, :S]),
        "c_t": np.ascontiguousarray(c.reshape(8, 128).T),
        "pos": np.ascontiguousarray(np.asarray(inputs["positions"], np.int32)[bidx, :S].reshape(1, S)),
        "g_final": np.ascontiguousarray(np.asarray(inputs["g_final"], np.float32).reshape(1, D)),
        "consts": make_consts(),
    }
    for k in ("g_mix", "w_ada", "b_ada", "w_in", "b_fox_f", "g_mla_q", "w_mla_uq", "g_mla_kv", "w_mla_ukv",
              "w_o_fox", "w_o_mla", "w_o_sb", "w_out", "g_ffn", "w_ffn_gate", "w_ffn_up", "w_ffn_down"):
        m[k] = np.ascontiguousarray(np.asarray(inputs[k], np.float32))
    return m


def kernel(**inputs):
    x = np.asarray(inputs["x"])
    B, S, _ = x.shape
    key = (S,)
    if key not in _CACHE:
        _CACHE[key] = build(S)
    nc = _CACHE[key]
    in_maps = [_prep_core(inputs, b, S) for b in range(B)]
    res = run_bass_kernel_spmd(nc, in_maps, core_ids=list(range(B)))
    out = np.stack([np.asarray(r["y"], np.float32) for r in res.results], axis=0)
    return out
```

```python
import numpy as np
import ml_dtypes
import concourse.bass as bass
import concourse.mybir as mybir
from concourse.bass_utils import run_bass_kernel_spmd

F32, BF16, I32, U8 = mybir.dt.float32, mybir.dt.bfloat16, mybir.dt.int32, mybir.dt.uint8
AF = mybir.ActivationFunctionType
ALU = mybir.AluOpType

D = 1024
KC = 8
DFF = 2816
FC = 22
DEPTH = 2
EPS = 1e-6
OFF_FOXQ, OFF_FOXF, OFF_QL, OFF_KVL, OFF_KR, OFF_SB, OFF_GATE, INW = 0, 1536, 1544, 1928, 2184, 2216, 3752, 6824
NEG = -30000.0
TWO_PI = float(2.0 * np.pi)
PI = float(np.pi)
CW1 = 6.28125
CW2 = float(2.0 * np.pi - 6.28125)

C_IDENT, C_NTRI, C_NU, C_NTRI2, C_NONES, C_MFOX, C_MMLA, C_MSB, C_INV, C_SGN, C_END = (
    0, 128, 256, 384, 512, 640, 768, 896, 1024, 1025, 1026)


def make_consts():
    c = np.zeros((128, C_END), np.float32)
    i = np.arange(128)
    c[:, C_IDENT:C_IDENT + 128] = np.eye(128)
    j, s = i[:, None], i[None, :]
    c[:, C_NTRI:C_NTRI + 128] = -1.0 * (j >= s)
    c[:, C_NU:C_NU + 128] = -1.0 * (j < s)
    c[:, C_NTRI2:C_NTRI2 + 128] = -1.0 * (j <= s)
    c[:, C_NONES:C_NONES + 128] = -1.0
    k, q = i[:, None], i[None, :]
    c[:, C_MFOX:C_MFOX + 128] = NEG * (k > q)
    c[:, C_MMLA:C_MMLA + 128] = NEG * ((k >= 64) & (q < 64))
    c[:, C_MSB:C_MSB + 128] = NEG * (k >= q)
    inv = (np.float32(10000.0) ** (-(np.arange(16, dtype=np.float32) / np.float32(16)))).astype(np.float32)
    c[64:80, C_INV] = inv
    c[80:96, C_INV] = inv
    c[64:80, C_SGN] = -1.0
    c[80:96, C_SGN] = 1.0
    return c


class Res:
    __slots__ = ("name", "w", "r", "sem", "semv", "excl")

    def __init__(self, name, excl=False):
        self.name = name
        self.excl = excl
        self.w = None
        self.r = {}
        self.sem = None
        self.semv = 0


class Sched:
    def __init__(self, nc):
        self.nc = nc
        self.eng = {"pe": nc.tensor, "act": nc.scalar, "dve": nc.vector, "pool": nc.gpsimd, "sp": nc.sync}
        self.sem = {k: nc.alloc_semaphore("s_" + k) for k in ("pe", "act", "dve", "pool")}
        self.cnt = {k: 0 for k in self.sem}
        self.seen = {k: {} for k in self.eng}
        self.dma_res = []
        self.nwait = 0
        self.sem_pool = []
        self.nsem = 0

    def _wait(self, e, tok):
        if tok is None:
            return
        sem, val, src = tok
        if src == "pe" and e == "pe":
            return
        sid = id(sem)
        if self.seen[e].get(sid, 0) >= val:
            return
        self.eng[e].wait_ge(sem, val)
        self.nwait += 1
        self.seen[e][sid] = val

    def _deps(self, e, reads, writes):
        for r in reads:
            self._wait(e, r.w)
            if r.excl:
                for tok in r.r.values():
                    if tok[2] != e:
                        self._wait(e, tok)
        for w in writes:
            self._wait(e, w.w)
            for tok in w.r.values():
                if tok[2] == e and e != "sp":
                    continue
                self._wait(e, tok)

    def _post(self, tok, reads, writes):
        sid = id(tok[0])
        for r in reads:
            r.r[sid] = tok
        for w in writes:
            w.w = tok
            w.r = {}

    def op(self, e, reads, writes, fn):
        self._deps(e, reads, writes)
        ins = fn(self.eng[e])
        self.cnt[e] += 1
        ins.then_inc(self.sem[e], 1)
        tok = (self.sem[e], self.cnt[e], e)
        self._post(tok, reads, writes)

    def dma(self, out_ap, in_ap, sb, reads, writes, q="sp", ser=True):
        wr = list(writes)
        if sb not in wr and ser:
            wr_dep = wr + [sb]
        else:
            wr_dep = wr
        self._deps(q, reads, wr_dep)
        if sb.sem is None:
            if self.sem_pool:
                sb.sem, sb.semv = self.sem_pool.pop()
            else:
                sb.sem = self.nc.alloc_semaphore(f"d{self.nsem}")
                sb.semv = 0
                self.nsem += 1
            self.dma_res.append(sb)
        ins = self.eng[q].dma_start(out=out_ap, in_=in_ap)
        sb.semv += 16
        ins.then_inc(sb.sem, 16)
        tok = (sb.sem, sb.semv, "dma")
        self._post(tok, reads, wr)

    def barrier(self, engines=("pe", "act", "dve", "pool", "sp")):
        toks = [(self.sem[k], self.cnt[k], k) for k in self.sem if self.cnt[k] > 0]
        toks += [(r.sem, r.semv, "dma") for r in self.dma_res if r.semv > 0]
        for e in engines:
            for t in toks:
                self._wait(e, t)
        if len(engines) == 5:
            for r in self.dma_res:
                self.sem_pool.append((r.sem, r.semv))
                r.sem = None
            self.dma_res = []


class Arena:
    def __init__(self, nc, nbytes):
        self.t = nc.alloc_sbuf_tensor("arena", [128, nbytes], U8)
        self.n = nbytes
        self.top = 0
        self.peak = 0
        self.htop = nbytes

    def alloc(self, shape, dt, parts=128):
        esz = {F32: 4, BF16: 2, I32: 4}[dt]
        free = int(np.prod(shape[1:]))
        nb = (free * esz + 31) // 32 * 32
        off = self.top
        assert off + nb <= self.htop, f"arena overflow {off}+{nb}>{self.htop}"
        self.top += nb
        self.peak = max(self.peak, self.top)
        v = self.t[:, off:off + free * esz].bitcast(dt)
        if len(shape) == 3:
            v = v.rearrange("p (a b) -> p a b", b=shape[2])
        elif len(shape) == 4:
            v = v.rearrange("p (a b c) -> p a b c", b=shape[2], c=shape[3])
        if shape[0] < 128:
            v = v[0:shape[0]]
        return v

    def alloc_top(self, shape, dt):
        esz = {F32: 4, BF16: 2, I32: 4}[dt]
        free = int(np.prod(shape[1:]))
        nb = (free * esz + 31) // 32 * 32
        self.htop -= nb
        off = self.htop
        assert off >= self.top, "arena top/bottom collision"
        v = self.t[:, off:off + free * esz].bitcast(dt)
        if len(shape) == 3:
            v = v.rearrange("p (a b) -> p a b", b=shape[2])
        return v

    def mark(self):
        return self.top

    def release(self, m):
        self.top = m


def build(S, depth=DEPTH, branches=(0, 1, 2)):
    NT = S // 128
    NQ = S // 512
    nc = bass.Bass("TRN2", target_bir_lowering=False)

    def din(name, shape, dt=F32):
        return nc.dram_tensor(name, list(shape), dt, kind="ExternalInput").ap()

    x_in = din("x", [S, D])
    c_t = din("c_t", [128, 8])
    pos = din("pos", [1, S], I32)
    g_mix = din("g_mix", [DEPTH, D])
    w_ada = din("w_ada", [DEPTH, D, 6 * D])
    b_ada = din("b_ada", [DEPTH, 6 * D])
    w_in = din("w_in", [DEPTH, D, INW])
    b_fox_f = din("b_fox_f", [DEPTH, 8])
    g_mla_q = din("g_mla_q", [DEPTH, 384])
    w_mla_uq = din("w_mla_uq", [DEPTH, 384, 768])
    g_mla_kv = din("g_mla_kv", [DEPTH, 256])
    w_mla_ukv = din("w_mla_ukv", [DEPTH, 256, 1024])
    w_o = [din("w_o_fox", [DEPTH, 512, D]), din("w_o_mla", [DEPTH, 512, D]), din("w_o_sb", [DEPTH, 512, D])]
    w_out = din("w_out", [DEPTH, D, D])
    g_ffn = din("g_ffn", [DEPTH, D])
    w_gate = din("w_ffn_gate", [DEPTH, D, DFF])
    w_up = din("w_ffn_up", [DEPTH, D, DFF])
    w_down = din("w_ffn_down", [DEPTH, DFF, D])
    g_final = din("g_final", [1, D])
    consts = din("consts", [128, C_END])
    y = nc.dram_tensor("y", [S, D], F32, kind="ExternalOutput").ap()

    KQ = [70, 96, 64]
    modrow = nc.dram_tensor("modrow", [DEPTH, 6 * D], F32).ap()
    qs = [nc.dram_tensor(f"qs{b}", [8, KQ[b], S], BF16).ap() for b in range(3)]
    ks = [nc.dram_tensor(f"ks{b}", [8, KQ[b], S], BF16).ap() for b in range(3)]
    VW = [128, 128, 64]
    vs = [nc.dram_tensor(f"vs{b}", [8, 128, NT * VW[b]], BF16).ap() for b in range(3)]
    ybr = nc.dram_tensor("ybr", [3, 512, S], BF16).ap()
    uts = nc.dram_tensor("uts", [128, KC, S], BF16).ap()

    sc = Sched(nc)
    A = Arena(nc, 207 * 1024)
    ps = [nc.alloc_psum_tensor(f"ps{i}", [128, 512], F32) for i in range(8)]
    PB = [Res(f"pb{i}", excl=True) for i in range(8)]

    def psf(i):
        return ps[i][:]

    def psb(i):
        return ps[i][:].bitcast(BF16)

    R_mod = [[Res(f"mod{l}_{n}") for n in range(12)] for l in range(DEPTH)]
    R_y = [Res(f"y{t}") for t in range(NT)]
    R_qs = [[Res(f"qs{b}_{h}") for h in range(8)] for b in range(3)]
    R_ks = [[Res(f"ks{b}_{h}") for h in range(8)] for b in range(3)]
    R_vs = [[Res(f"vs{b}_{h}") for h in range(8)] for b in range(3)]
    R_qF = Res("qF")
    R_kF = Res("kF")
    R_ybr = [[[Res(f"ybr{b}_{h}_{q}") for q in range(NQ)] for h in range(8)] for b in range(3)]
    R_none = Res("ext")
    R_uts = [Res(f"uts{q}") for q in range(NQ)]

    cst = A.alloc([128, C_END], F32)
    R_cst = Res("cst")
    cbf = A.alloc([128, 1024], BF16)
    R_cbf = Res("cbf")
    sc.dma(cst, consts, R_cst, [R_none], [R_cst])
    sc.op("dve", [R_cst], [R_cbf], lambda e: e.tensor_copy(out=cbf, in_=cst[:, 0:1024]))
    ident = cbf[:, C_IDENT:C_IDENT + 128]
    ntri = cbf[:, C_NTRI:C_NTRI + 128]
    nu = cbf[:, C_NU:C_NU + 128]
    masks = [cbf[:, C_MFOX:C_MFOX + 128], cbf[:, C_MMLA:C_MMLA + 128], cbf[:, C_MSB:C_MSB + 128]]
    ntri2_f = cst[:, C_NTRI2:C_NTRI2 + 128]
    nones_f = cst[:, C_NONES:C_NONES + 128]

    uT = A.alloc([128, KC, S], BF16)
    R_uT = [Res(f"uT{q}") for q in range(NQ)]
    persist_mark = A.mark()

    cast_ctr = [0]

    def cast(dst, src, reads, writes):
        cast_ctr[0] += 1
        if cast_ctr[0] % 2 == 0:
            sc.op("dve", reads, writes, lambda e: e.tensor_copy(out=dst, in_=src))
        else:
            sc.op("act", reads, writes, lambda e: e.copy(out=dst, in_=src))

    def bcast_load(dst, row_ap, Rdst, reads):
        sc.dma(dst, row_ap.partition_broadcast(128), Rdst, reads, [Rdst])

    def phase_mod(l):
        if l in mod_done:
            return
        m = A.mark()
        ct = A.alloc([128, 8], F32)
        cond = A.alloc([128, 8], F32)
        R_ct, R_cond = Res("ct"), Res("cond")
        wst = [A.alloc([128, KC, 512], F32) for _ in range(2)]
        R_wst = [Res(f"wst{i}") for i in range(2)]
        brow = [A.alloc([1, 512], F32) for _ in range(2)]
        R_brow = [Res(f"brow{i}") for i in range(2)]
        mrow = [A.alloc([1, 512], F32) for _ in range(2)]
        R_mrow = [Res(f"mrow{i}") for i in range(2)]
        sc.dma(ct, c_t, R_ct, [R_none], [R_ct])
        sc.op("act", [R_ct], [R_cond], lambda e: e.activation(out=cond, in_=ct, func=AF.Silu))
        wv = w_ada[l].rearrange("(kc p) n -> p kc n", p=128)
        for n in range(12):
            s_ = n % 2
            sc.dma(wst[s_], wv[:, :, n * 512:(n + 1) * 512], R_wst[s_], [R_none], [R_wst[s_]])
            sc.dma(brow[s_][0:1, :], b_ada[l:l + 1, n * 512:(n + 1) * 512], R_brow[s_], [R_none], [R_brow[s_]])
            pb = n % 2

            def mm(e, s_=s_, pb=pb):
                for kc in range(KC):
                    ins = e.matmul(psf(pb)[0:1, :], cond[:, kc:kc + 1], wst[s_][:, kc, :],
                                   start=(kc == 0), stop=(kc == KC - 1))
                return ins
            sc.op("pe", [R_cond, R_wst[s_]], [PB[pb]], mm)
            sc.op("dve", [PB[pb], R_brow[s_]], [R_mrow[s_]],
                  lambda e, s_=s_, pb=pb: e.tensor_tensor(out=mrow[s_][0:1, :], in0=psf(pb)[0:1, :],
                                                          in1=brow[s_][0:1, :], op=ALU.add))
            sc.dma(modrow[l:l + 1, n * 512:(n + 1) * 512], mrow[s_][0:1, :], R_mrow[s_], [R_mrow[s_]], [R_mod[l][n]])
        A.release(m)
        sc.barrier()

    def load_mod_tiles(l, i_sh, i_sc, grow, GM=None):
        G = A.alloc([128, D], F32)
        SH = A.alloc([128, D], F32)
        if GM is None:
            GM = A.alloc([128, D], F32)
        R_G, R_SH, R_GM = Res("G"), Res("SH"), Res("GM")
        bcast_load(G, modrow[l:l + 1, i_sc * D:(i_sc + 1) * D], R_G, [R_mod[l][2 * i_sc], R_mod[l][2 * i_sc + 1]])
        bcast_load(SH, modrow[l:l + 1, i_sh * D:(i_sh + 1) * D], R_SH, [R_mod[l][2 * i_sh], R_mod[l][2 * i_sh + 1]])
        bcast_load(GM, grow, R_GM, [R_none])
        sc.op("dve", [R_G, R_GM], [R_G],
              lambda e: e.scalar_tensor_tensor(out=G, in0=G, scalar=1.0, in1=GM, op0=ALU.add, op1=ALU.mult))
        return G, SH, R_G, R_SH

    class NormBufs:
        def __init__(self, n=2, ntmp=2):
            self.n = n
            self.xt = [A.alloc([128, D], F32) for _ in range(n)]
            self.R_xt = [Res(f"xt{i}") for i in range(n)]
            self.tmp = [A.alloc([128, D], F32) for _ in range(ntmp)]
            self.R_tmp = [Res(f"ntmp{i}") for i in range(ntmp)]
            self.u = [A.alloc([128, D], BF16) for _ in range(n)]
            self.R_u = [Res(f"u{i}") for i in range(n)]
            self.junk = A.alloc([128, D], BF16)
            self.R_junk = Res("junk")
            self.ss = [A.alloc([128, 2], F32) for _ in range(n)]
            self.R_ss = [Res(f"ss{i}") for i in range(n)]

    def norm_tile(nb, s_, G, SH, R_G, R_SH):
        ti = s_ % len(nb.tmp)
        xt, ss, tmp, u = nb.xt[s_], nb.ss[s_], nb.tmp[ti], nb.u[s_]
        Rx, Rss, Rt, Ru = nb.R_xt[s_], nb.R_ss[s_], nb.R_tmp[ti], nb.R_u[s_]
        sc.op("act", [Rx], [nb.R_junk, Rss],
              lambda e: e.activation(out=nb.junk, in_=xt, func=AF.Square, accum_out=ss[:, 0:1]))
        sc.op("dve", [Rss], [Rss], lambda e: e.tensor_scalar(out=ss[:, 0:1], in0=ss[:, 0:1], scalar1=1.0 / D,
                                                             scalar2=EPS, op0=ALU.mult, op1=ALU.add))
        sc.op("act", [Rss], [Rss], lambda e: e.activation(out=ss[:, 0:1], in_=ss[:, 0:1], func=AF.Ln))
        sc.op("act", [Rss], [Rss], lambda e: e.activation(out=ss[:, 0:1], in_=ss[:, 0:1], func=AF.Exp, scale=-0.5))
        sc.op("dve", [Rx, Rss, R_G], [Rt],
              lambda e: e.scalar_tensor_tensor(out=tmp, in0=xt, scalar=ss[:, 0:1], in1=G, op0=ALU.mult, op1=ALU.mult))
        sc.op("dve", [Rt, R_SH], [Ru], lambda e: e.tensor_tensor(out=u, in0=tmp, in1=SH, op=ALU.add))

    def transpose_tile(u, Ru, pb, dst3, Rdst, eng):
        def tr(e):
            for kc in range(KC):
                ins = e.transpose(psb(pb)[:, kc * 128:(kc + 1) * 128], u[:, kc * 128:(kc + 1) * 128], ident)
            return ins
        sc.op("pe", [Ru, R_cbf], [PB[pb]], tr)
        src = psb(pb).rearrange("p (a b) -> p a b", b=128)
        if eng == "act":
            sc.op("act", [PB[pb]], [Rdst], lambda e: e.copy(out=dst3, in_=src))
        else:
            sc.op("dve", [PB[pb]], [Rdst], lambda e: e.tensor_copy(out=dst3, in_=src))

    def phase_norm_attn(l):
        m = A.mark()
        G, SH, R_G, R_SH = load_mod_tiles(l, 0, 1, g_mix[l:l + 1, :])
        nb = NormBufs(2)
        for t in range(NT):
            s_ = t % 2
            src = x_in if l == 0 else y
            sc.dma(nb.xt[s_], src[t * 128:(t + 1) * 128, :], nb.R_xt[s_], [R_none if l == 0 else R_y[t]], [nb.R_xt[s_]])
            norm_tile(nb, s_, G, SH, R_G, R_SH)
            transpose_tile(nb.u[s_], nb.R_u[s_], t % 2, uT[:, :, t * 128:(t + 1) * 128], R_uT[t // 4],
                           "act" if t % 2 == 0 else "dve")
        for q in range(NQ):
            sc.dma(uts[:, :, q * 512:(q + 1) * 512], uT[:, :, q * 512:(q + 1) * 512], R_uT[q], [R_uT[q]], [R_uts[q]])
        A.release(m)
        sc.barrier()

    def load_w_cols(l, wsrc_view, c0, ncols, stg, R_stg, dst, R_dst, kcs=KC):
        sc.dma(stg[:, 0:kcs, 0:ncols], wsrc_view[:, :, c0:c0 + ncols], R_stg, [R_none], [R_stg])
        sc.op("pool", [R_stg], [R_dst], lambda e: e.tensor_copy(out=dst[:, 0:kcs, 0:ncols], in_=stg[:, 0:kcs, 0:ncols]))

    def proj_featmajor(w3, Rw, c0, M, kcs, rhs_fn, R_rhs, pb, nkc_rows=None):
        def mm(e):
            for kc in range(kcs):
                ins = e.matmul(psf(pb)[0:M, :], w3[:, kc, c0:c0 + M], rhs_fn(kc), start=(kc == 0), stop=(kc == kcs - 1))
            return ins
        sc.op("pe", [Rw] + R_rhs, [PB[pb]], mm)

    def phase_proj_qkv(l, b, colbase):
        m = A.mark()
        wv = w_in[l].rearrange("(kc p) n -> p kc n", p=128)
        stg = [A.alloc([128, KC, 384], F32) for _ in range(2)]
        R_stg = [Res(f"stg{i}") for i in range(2)]
        wb = [A.alloc([128, KC, 384], BF16) for _ in range(2)]
        R_wb = [Res(f"wb{i}") for i in range(2)]
        R_stg3 = [[Res(f"stg{i}_{j}") for j in range(3)] for i in range(2)]
        qT2 = [A.alloc([128, S], BF16) for _ in range(2)]
        kT2 = [A.alloc([128, S], BF16) for _ in range(2)]
        W = VW[b]
        vh2 = [[A.alloc([128, NT, W], BF16) for _ in range(2)] for _ in range(2)]
        R_qT2 = [Res(f"qT{i}") for i in range(2)]
        R_kT2 = [Res(f"kT{i}") for i in range(2)]
        R_vh2 = [[Res(f"vh{i}_{j}") for j in range(2)] for i in range(2)]
        if W == 128:
            for i in range(2):
                for j in range(2):
                    sc.op("pool", [], [R_vh2[i][j]], lambda e, i=i, j=j: e.memset(vh2[i][j][:, :, 64:128], 1.0))
        for hp in range(4):
            s_ = hp % 2
            qT, kT = qT2[s_], kT2[s_]
            R_qT, R_kT = R_qT2[s_], R_kT2[s_]
            for i in range(3):
                c0 = colbase + i * 512 + hp * 128
                sc.dma(stg[s_][:, :, i * 128:(i + 1) * 128], wv[:, :, c0:c0 + 128], R_stg3[s_][i], [R_none], [R_stg3[s_][i]])
            cast(wb[s_], stg[s_], R_stg3[s_], [R_wb[s_]] + R_stg3[s_])
            cnt = 0
            for i, (dstT, R_d) in enumerate(((qT, R_qT), (kT, R_kT))):
                for tc in range(NQ):
                    pb = cnt % 2
                    cnt += 1
                    proj_featmajor(wb[s_], R_wb[s_], i * 128, 128, KC,
                                   lambda kc, tc=tc: uT[:, kc, tc * 512:(tc + 1) * 512], [R_uT[tc]], pb)
                    if cnt % 2 == 0:
                        sc.op("act", [PB[pb]], [R_d], lambda e, pb=pb, dstT=dstT, tc=tc: e.copy(
                            out=dstT[:, tc * 512:(tc + 1) * 512], in_=psf(pb)))
                    else:
                        sc.op("dve", [PB[pb]], [R_d], lambda e, pb=pb, dstT=dstT, tc=tc: e.tensor_copy(
                            out=dstT[:, tc * 512:(tc + 1) * 512], in_=psf(pb)))
            for tg in range(NT // 4):
                pb = 2 + tg % 2

                def mmv(e, tg=tg, pb=pb, s_=s_):
                    for j in range(4):
                        t = tg * 4 + j
                        for kc in range(KC):
                            ins = e.matmul(psf(pb)[:, j * 128:(j + 1) * 128], uT[:, kc, t * 128:(t + 1) * 128],
                                           wb[s_][:, kc, 256:384], start=(kc == 0), stop=(kc == KC - 1))
                    return ins
                sc.op("pe", [R_wb[s_], R_uT[tg]], [PB[pb]], mmv)
                src3 = psf(pb).rearrange("p (a b) -> p a b", b=128)
                sc.op("dve", [PB[pb]], [R_vh2[s_][0]], lambda e, tg=tg, src3=src3: e.tensor_copy(
                    out=vh2[s_][0][:, tg * 4:(tg + 1) * 4, 0:64], in_=src3[:, :, 0:64]))
                sc.op("act", [PB[pb]], [R_vh2[s_][1]], lambda e, tg=tg, src3=src3: e.copy(
                    out=vh2[s_][1][:, tg * 4:(tg + 1) * 4, 0:64], in_=src3[:, :, 64:128]))
            for hh in range(2):
                h = 2 * hp + hh
                sc.dma(qs[b][h, 0:64, :], qT[hh * 64:(hh + 1) * 64, :], R_qT, [R_qT], [R_qs[b][h]], ser=False)
                sc.dma(ks[b][h, 0:64, :], kT[hh * 64:(hh + 1) * 64, :], R_kT, [R_kT], [R_ks[b][h]], ser=False)
                sc.dma(vs[b][h].rearrange("p (t d) -> p t d", d=W), vh2[s_][hh], R_vh2[s_][hh],
                       [R_vh2[s_][hh]], [R_vs[b][h]])
        A.release(m)
        sc.barrier()

    def phase_fox_F(l):
        m = A.mark()
        wv = w_in[l].rearrange("(kc p) n -> p kc n", p=128)
        stg = A.alloc([128, KC, 8], F32)
        wF = A.alloc([128, KC, 8], BF16)
        R_stg, R_wF = Res("stgF"), Res("wF")
        bF = A.alloc([128, 8], F32)
        R_bF = Res("bF")
        lf = A.alloc([128, NT, 8], F32)
        R_lf = Res("lf")
        sc.dma(stg, wv[:, :, OFF_FOXF:OFF_FOXF + 8], R_stg, [R_none], [R_stg])
        cast(wF, stg, [R_stg], [R_wF])
        bcast_load(bF, b_fox_f[l:l + 1, :], R_bF, [R_none])

        def mmf(e):
            for t in range(NT):
                for kc in range(KC):
                    ins = e.matmul(psf(0)[:, t * 8:(t + 1) * 8], uT[:, kc, t * 128:(t + 1) * 128], wF[:, kc, :],
                                   start=(kc == 0), stop=(kc == KC - 1))
            return ins
        sc.op("pe", [R_wF] + R_uT, [PB[0]], mmf)
        for t in range(NT):
            sc.op("dve", [PB[0], R_bF], [R_lf], lambda e, t=t: e.tensor_tensor(
                out=lf[:, t, :], in0=psf(0)[:, t * 8:(t + 1) * 8], in1=bF, op=ALU.add))
        lf2 = lf.rearrange("p t h -> p (t h)")
        sc.op("act", [R_lf], [R_lf], lambda e: e.activation(out=lf2, in_=lf2, func=AF.Exp, scale=-1.0))
        sc.op("act", [R_lf], [R_lf], lambda e: e.activation(out=lf2, in_=lf2, func=AF.Ln, bias=1.0, scale=1.0))
        f8 = [A.alloc([8, 512], F32) for _ in range(2)]
        r1 = [A.alloc([8, 512], F32) for _ in range(2)]
        fall = [A.alloc([8, 3, 512], BF16) for _ in range(2)]
        nfall = [A.alloc([8, 3, 512], BF16) for _ in range(2)]
        R_f8 = [Res(f"f8{i}") for i in range(2)]
        R_r1 = [Res(f"r1{i}") for i in range(2)]
        R_fall = [Res(f"fall{i}") for i in range(2)]
        R_nfall = [Res(f"nfall{i}") for i in range(2)]
        for qc in range(NQ):
            s_ = qc % 2
            pb = 1 + qc % 2

            def mmc(e, qc=qc, pb=pb):
                for j in range(4):
                    ti = qc * 4 + j
                    for tj in range(ti + 1):
                        ins = e.matmul(psf(pb)[0:8, j * 128:(j + 1) * 128], lf[:, tj, :],
                                       ntri2_f if tj == ti else nones_f, start=(tj == 0), stop=(tj == ti))
                return ins
            sc.op("pe", [R_lf, R_cst], [PB[pb]], mmc)
            f, r, fa, nfa = f8[s_], r1[s_], fall[s_], nfall[s_]
            Rf, Rr, Rfa, Rnfa = R_f8[s_], R_r1[s_], R_fall[s_], R_nfall[s_]
            sc.op("dve", [PB[pb]], [Rf], lambda e, f=f, pb=pb: e.tensor_scalar(
                out=f, in0=psf(pb)[0:8, :], scalar1=8.0, scalar2=None, op0=ALU.mult))
            sc.op("dve", [Rf], [Rfa], lambda e, f=f, fa=fa: e.tensor_copy(out=fa[:, 0, :], in_=f))
            sc.op("dve", [Rf, Rfa], [Rr], lambda e, f=f, fa=fa, r=r: e.tensor_tensor(
                out=r, in0=f, in1=fa[:, 0, :], op=ALU.subtract))
            sc.op("dve", [Rr], [Rfa], lambda e, fa=fa, r=r: e.tensor_copy(out=fa[:, 1, :], in_=r))
            sc.op("dve", [Rr, Rfa], [Rr], lambda e, fa=fa, r=r: e.tensor_tensor(
                out=r, in0=r, in1=fa[:, 1, :], op=ALU.subtract))
            sc.op("dve", [Rr], [Rfa], lambda e, fa=fa, r=r: e.tensor_copy(out=fa[:, 2, :], in_=r))
            sc.op("dve", [Rfa], [Rnfa], lambda e, fa=fa, nfa=nfa: e.tensor_scalar(
                out=nfa, in0=fa, scalar1=-1.0, scalar2=None, op0=ALU.mult))
            sc.dma(qs[0][:, 64:67, qc * 512:(qc + 1) * 512], fa, Rfa, [Rfa], [R_qF])
            sc.dma(ks[0][:, 67:70, qc * 512:(qc + 1) * 512], nfa, Rnfa, [Rnfa], [R_kF])
        A.release(m)
        sc.barrier()

    def phase_proj_mla(l):
        m = A.mark()
        b = 1
        wv = w_in[l].rearrange("(kc p) n -> p kc n", p=128)
        stg = A.alloc([128, KC, 672], F32)
        R_stg = Res("stgm")
        wl = A.alloc([128, KC, 640], BF16)
        wkr = A.alloc([128, KC, 96], BF16)
        wkrs = A.alloc([128, KC, 96], BF16)
        R_wl, R_wkr = Res("wl"), Res("wkr")
        sc.dma(stg, wv[:, :, OFF_QL:OFF_QL + 672], R_stg, [R_none], [R_stg])
        cast(wl, stg[:, :, 0:640], [R_stg], [R_wl])
        sc.op("pool", [R_stg], [R_wkr], lambda e: e.tensor_copy(out=wkr, in_=stg[:, :, 576:672]))
        sc.op("pool", [R_stg], [R_wkr], lambda e: e.tensor_copy(out=wkrs[:, :, 0:64], in_=stg[:, :, 576:640]))
        sc.op("pool", [R_stg], [R_wkr], lambda e: e.tensor_copy(out=wkrs[:, :, 64:80], in_=stg[:, :, 656:672]))
        sc.op("pool", [R_stg], [R_wkr], lambda e: e.tensor_copy(out=wkrs[:, :, 80:96], in_=stg[:, :, 640:656]))
        stq = A.alloc([128, 3, 768], F32)
        R_stq = Res("stq")
        wq = A.alloc([128, 3, 768], BF16)
        wqs = A.alloc([128, 3, 768], BF16)
        R_wq = Res("wq")
        sc.dma(stq, w_mla_uq[l].rearrange("(kc p) n -> p kc n", p=128), R_stq, [R_none], [R_stq])
        cast(wq, stq, [R_stq], [R_wq])
        stq4 = stq.rearrange("p k (h c) -> p k h c", c=96)
        wqs4 = wqs.rearrange("p k (h c) -> p k h c", c=96)
        for kc in range(3):
            sc.op("pool", [R_stq], [R_wq], lambda e, kc=kc: e.tensor_copy(out=wqs4[:, kc, :, 0:64], in_=stq4[:, kc, :, 0:64]))
            sc.op("pool", [R_stq], [R_wq], lambda e, kc=kc: e.tensor_copy(out=wqs4[:, kc, :, 64:80], in_=stq4[:, kc, :, 80:96]))
            sc.op("pool", [R_stq], [R_wq], lambda e, kc=kc: e.tensor_copy(out=wqs4[:, kc, :, 80:96], in_=stq4[:, kc, :, 64:80]))
        stkv = A.alloc([128, 2, 1024], F32)
        R_stkv = Res("stkv")
        wkv = A.alloc([128, 2, 1024], BF16)
        R_wkv = Res("wkv")
        sc.dma(stkv, w_mla_ukv[l].rearrange("(kc p) n -> p kc n", p=128), R_stkv, [R_none], [R_stkv])
        cast(wkv, stkv, [R_stkv], [R_wkv])
        wkk = A.alloc([128, 2, 512], BF16)
        wvv = A.alloc([128, 2, 512], BF16)
        stkv4 = stkv.rearrange("p k (h c) -> p k h c", c=128)
        for kc in range(2):
            sc.op("pool", [R_stkv], [R_wkv], lambda e, kc=kc: e.tensor_copy(
                out=wkk[:, kc, :].rearrange("p (h c) -> p h c", c=64), in_=stkv4[:, kc, :, 0:64]))
            sc.op("pool", [R_stkv], [R_wkv], lambda e, kc=kc: e.tensor_copy(
                out=wvv[:, kc, :].rearrange("p (h c) -> p h c", c=64), in_=stkv4[:, kc, :, 64:128]))
        gq = A.alloc([128, 384], F32)
        gkv = A.alloc([128, 256], F32)
        R_gq, R_gkv = Res("gq"), Res("gkv")
        bcast_load(gq, g_mla_q[l:l + 1, :], R_gq, [R_none])
        bcast_load(gkv, g_mla_kv[l:l + 1, :], R_gkv, [R_none])
        posi = A.alloc([96, 512], I32)
        ang = A.alloc([96, 512], F32)
        cs = A.alloc([96, 512], F32)
        ss_ = A.alloc([96, 512], F32)
        R_posi, R_ang, R_cs, R_ss = Res("posi"), Res("ang"), Res("cs"), Res("ssn")
        cqT = A.alloc([128, 3, 512], BF16)
        ckvT = A.alloc([128, 2, 512], BF16)
        R_cqT, R_ckvT = Res("cqT"), Res("ckvT")
        lat = [A.alloc([128, 640], F32) for _ in range(2)]
        R_lat = [Res(f"lat{i}") for i in range(2)]
        latb = [A.alloc([128, 640], BF16) for _ in range(2)]
        R_latb = [Res(f"latb{i}") for i in range(2)]
        st2 = [A.alloc([128, 2], F32) for _ in range(2)]
        R_st2 = [Res(f"st2{i}") for i in range(2)]
        junk = A.alloc([128, 384], BF16)
        R_junk = Res("junkm")
        krope = A.alloc([96, 512], BF16)
        R_krope = Res("krope")
        t1 = [A.alloc([96, 512], F32) for _ in range(2)]
        t2 = [A.alloc([96, 512], F32) for _ in range(2)]
        R_t1 = [Res(f"t1{i}") for i in range(2)]
        R_t2 = [Res(f"t2{i}") for i in range(2)]
        qh = [A.alloc([96, 512], BF16) for _ in range(2)]
        kh = [A.alloc([96, 512], BF16) for _ in range(2)]
        R_qh = [Res(f"qh{i}") for i in range(2)]
        R_kh = [Res(f"kh{i}") for i in range(2)]
        vm = A.alloc([128, 8, 4, 128], BF16)
        R_vsb = Res("vsbm")
        sc.op("pool", [], [R_vsb], lambda e: e.memset(vm[:, :, :, 64:128], 1.0))
        inv_c = cst[64:96, C_INV:C_INV + 1]
        sgn_c = cst[64:96, C_SGN:C_SGN + 1]
        P = slice(64, 96)
        import os as _os
        kmla = int(_os.environ.get("KMLA", "99"))

        def _bail():
            A.release(m)
            sc.barrier()
        if kmla <= 1:
            return _bail()
        for qc in range(NQ):
            cols = slice(qc * 512, (qc + 1) * 512)
            sc.dma(posi[P, :], pos[0:1, cols].partition_broadcast(32), R_posi, [R_none], [R_posi])
            sc.op("dve", [R_posi], [R_ang], lambda e: e.tensor_copy(out=ang[P, :], in_=posi[P, :]))
            sc.op("dve", [R_ang, R_cst], [R_ang], lambda e: e.tensor_scalar(
                out=ang[P, :], in0=ang[P, :], scalar1=inv_c, scalar2=None, op0=ALU.mult))
            sc.op("dve", [R_ang], [R_cs], lambda e: e.tensor_scalar(
                out=cs[P, :], in0=ang[P, :], scalar1=1.0 / TWO_PI, scalar2=None, op0=ALU.mult))
            sc.op("dve", [R_cs], [R_posi], lambda e: e.tensor_copy(out=posi[P, :], in_=cs[P, :]))
            sc.op("dve", [R_posi], [R_cs], lambda e: e.tensor_copy(out=cs[P, :], in_=posi[P, :]))
            sc.op("dve", [R_cs, R_ang], [R_ang], lambda e: e.scalar_tensor_tensor(
                out=ang[P, :], in0=cs[P, :], scalar=-CW1, in1=ang[P, :], op0=ALU.mult, op1=ALU.add))
            sc.op("dve", [R_cs, R_ang], [R_ang], lambda e: e.scalar_tensor_tensor(
                out=ang[P, :], in0=cs[P, :], scalar=-CW2, in1=ang[P, :], op0=ALU.mult, op1=ALU.add))
            sc.op("dve", [R_ang], [R_cs], lambda e: e.tensor_scalar(
                out=cs[P, :], in0=ang[P, :], scalar1=PI, scalar2=TWO_PI, op0=ALU.is_gt, op1=ALU.mult))
            sc.op("dve", [R_cs, R_ang], [R_ang], lambda e: e.tensor_tensor(
                out=ang[P, :], in0=ang[P, :], in1=cs[P, :], op=ALU.subtract))
            sc.op("dve", [R_ang], [R_cs], lambda e: e.tensor_scalar(
                out=cs[P, :], in0=ang[P, :], scalar1=-PI, scalar2=TWO_PI, op0=ALU.is_lt, op1=ALU.mult))
            sc.op("dve", [R_cs, R_ang], [R_ang], lambda e: e.tensor_tensor(
                out=ang[P, :], in0=ang[P, :], in1=cs[P, :], op=ALU.add))
            sc.op("dve", [R_ang], [R_cs], lambda e: e.tensor_scalar(
                out=cs[P, :], in0=ang[P, :], scalar1=PI / 2, scalar2=None, op0=ALU.add))
            sc.op("dve", [R_cs], [R_ss], lambda e: e.tensor_scalar(
                out=ss_[P, :], in0=cs[P, :], scalar1=PI, scalar2=TWO_PI, op0=ALU.is_gt, op1=ALU.mult))
            sc.op("dve", [R_cs, R_ss], [R_cs], lambda e: e.tensor_tensor(
                out=cs[P, :], in0=cs[P, :], in1=ss_[P, :], op=ALU.subtract))
            sc.op("act", [R_cs], [R_cs], lambda e: e.activation(out=cs[P, :], in_=cs[P, :], func=AF.Sin))
            sc.op("act", [R_ang], [R_ss], lambda e: e.activation(out=ss_[P, :], in_=ang[P, :], func=AF.Sin))
            sc.op("dve", [R_ss, R_cst], [R_ss], lambda e: e.tensor_scalar(
                out=ss_[P, :], in0=ss_[P, :], scalar1=sgn_c, scalar2=None, op0=ALU.mult))
            if kmla <= 2:
                return _bail()
            def lat_A(j):
                    t = qc * 4 + j
                    s_ = j % 2
                    tok = slice(t * 128, (t + 1) * 128)
                    pA, pB = (0, 1) if j % 2 == 0 else (6, 7)

                    def mml(e, tok=tok, pA=pA, pB=pB):
                        for kc in range(KC):
                            e.matmul(psf(pA)[:, 0:384], uT[:, kc, tok], wl[:, kc, 0:384], start=(kc == 0), stop=(kc == KC - 1))
                        for kc in range(KC):
                            ins = e.matmul(psf(pB)[:, 0:256], uT[:, kc, tok], wl[:, kc, 384:640], start=(kc == 0), stop=(kc == KC - 1))
                        return ins
                    sc.op("pe", [R_wl, R_uT[qc]], [PB[pA], PB[pB]], mml)

            def lat_B(j):
                    t = qc * 4 + j
                    s_ = j % 2
                    tok = slice(t * 128, (t + 1) * 128)
                    pA, pB = (0, 1) if j % 2 == 0 else (6, 7)
                    la, lb, st, Rla, Rlb, Rst = lat[s_], latb[s_], st2[s_], R_lat[s_], R_latb[s_], R_st2[s_]
                    sc.op("act", [PB[pA]], [R_junk, Rst], lambda e, st=st: e.activation(
                        out=junk[:, 0:384], in_=psf(pA)[:, 0:384], func=AF.Square, accum_out=st[:, 0:1]))
                    sc.op("act", [PB[pB]], [R_junk, Rst], lambda e, st=st: e.activation(
                        out=junk[:, 0:256], in_=psf(pB)[:, 0:256], func=AF.Square, accum_out=st[:, 1:2]))
                    sc.op("dve", [Rst], [Rst], lambda e, st=st: e.tensor_scalar(
                        out=st[:, 0:1], in0=st[:, 0:1], scalar1=1.0 / 384, scalar2=EPS, op0=ALU.mult, op1=ALU.add))
                    sc.op("dve", [Rst], [Rst], lambda e, st=st: e.tensor_scalar(
                        out=st[:, 1:2], in0=st[:, 1:2], scalar1=1.0 / 256, scalar2=EPS, op0=ALU.mult, op1=ALU.add))
                    sc.op("act", [Rst], [Rst], lambda e, st=st: e.activation(out=st, in_=st, func=AF.Ln))
                    sc.op("act", [Rst], [Rst], lambda e, st=st: e.activation(out=st, in_=st, func=AF.Exp, scale=-0.5))
                    sc.op("dve", [PB[pA], Rst, R_gq], [Rlb], lambda e, st=st, lb=lb: e.scalar_tensor_tensor(
                        out=lb[:, 0:384], in0=psf(pA)[:, 0:384], scalar=st[:, 0:1], in1=gq, op0=ALU.mult, op1=ALU.mult))
                    sc.op("dve", [PB[pB], Rst, R_gkv], [Rlb], lambda e, st=st, lb=lb: e.scalar_tensor_tensor(
                        out=lb[:, 384:640], in0=psf(pB)[:, 0:256], scalar=st[:, 1:2], in1=gkv, op0=ALU.mult, op1=ALU.mult))
                    pT = 2 if j % 2 == 0 else 5

                    def trl(e, lb=lb, pT=pT):
                        for kc in range(5):
                            ins = e.transpose(psb(pT)[:, kc * 128:(kc + 1) * 128], lb[:, kc * 128:(kc + 1) * 128], ident)
                        return ins
                    sc.op("pe", [Rlb, R_cbf], [PB[pT]], trl)
                    srcq = psb(pT)[:, 0:384].rearrange("p (a b) -> p a b", b=128)
                    srck = psb(pT)[:, 384:640].rearrange("p (a b) -> p a b", b=128)
                    sc.op("act", [PB[pT]], [R_cqT], lambda e, j=j, srcq=srcq: e.copy(out=cqT[:, :, j * 128:(j + 1) * 128], in_=srcq))
                    sc.op("dve", [PB[pT]], [R_ckvT], lambda e, j=j, srck=srck: e.tensor_copy(
                        out=ckvT[:, :, j * 128:(j + 1) * 128], in_=srck))

            lat_A(0)
            for j in range(4):
                if j + 1 < 4:
                    lat_A(j + 1)
                lat_B(j)
            if kmla <= 3:
                return _bail()
            pA, pB = 3, 4
            proj_featmajor(wkr, R_wkr, 0, 96, KC, lambda kc, cols=cols: uT[:, kc, cols], [R_uT[qc]], pA)
            proj_featmajor(wkrs, R_wkr, 0, 96, KC, lambda kc, cols=cols: uT[:, kc, cols], [R_uT[qc]], pB)
            sc.op("dve", [PB[pA], R_cs], [R_t1[0]], lambda e: e.tensor_tensor(
                out=t1[0][P, :], in0=psf(pA)[P, :], in1=cs[P, :], op=ALU.mult))
            sc.op("dve", [PB[pB], R_ss], [R_t2[0]], lambda e: e.tensor_tensor(
                out=t2[0][P, :], in0=psf(pB)[P, :], in1=ss_[P, :], op=ALU.mult))
            sc.op("pool", [R_t1[0], R_t2[0]], [R_krope], lambda e: e.tensor_tensor(
                out=krope[P, :], in0=t1[0][P, :], in1=t2[0][P, :], op=ALU.add))
            if kmla <= 4:
                return _bail()
            for h in range(8):
                s_ = h % 2
                pA, pB, pK = 3 + 3 * s_ - 3 * s_, 4, 5
                pA = 3 if s_ == 0 else 6
                pB = 4 if s_ == 0 else 7
                pK = 5 if s_ == 0 else 2
                proj_featmajor(wq, R_wq, h * 96, 96, 3, lambda kc: cqT[:, kc, :], [R_cqT], pA)
                proj_featmajor(wqs, R_wq, h * 96, 96, 3, lambda kc: cqT[:, kc, :], [R_cqT], pB)
                q_, k_, Rq, Rk = qh[s_], kh[s_], R_qh[s_], R_kh[s_]
                a1, a2, Ra1, Ra2 = t1[s_], t2[s_], R_t1[s_], R_t2[s_]
                sc.op("act", [PB[pA]], [Rq], lambda e, q_=q_, pA=pA: e.copy(out=q_[0:64, :], in_=psf(pA)[0:64, :]))
                sc.op("dve", [PB[pA], R_cs], [Ra1], lambda e, a1=a1, pA=pA: e.tensor_tensor(
                    out=a1[P, :], in0=psf(pA)[P, :], in1=cs[P, :], op=ALU.mult))
                sc.op("dve", [PB[pB], R_ss], [Ra2], lambda e, a2=a2, pB=pB: e.tensor_tensor(
                    out=a2[P, :], in0=psf(pB)[P, :], in1=ss_[P, :], op=ALU.mult))
                sc.op("pool", [Ra1, Ra2], [Rq], lambda e, a1=a1, a2=a2, q_=q_: e.tensor_tensor(
                    out=q_[P, :], in0=a1[P, :], in1=a2[P, :], op=ALU.add))
                sc.dma(qs[b][h, :, cols], q_[0:96, :], Rq, [Rq], [R_qs[b][h]])
                def mmk(e, h=h, pK=pK):
                    for kc in range(2):
                        ins = e.matmul(psf(pK)[0:64, :], wkk[:, kc, h * 64:(h + 1) * 64], ckvT[:, kc, :],
                                       start=(kc == 0), stop=(kc == 1))
                    return ins
                sc.op("pe", [R_wkv, R_ckvT], [PB[pK]], mmk)
                sc.op("act", [PB[pK]], [Rk], lambda e, k_=k_, pK=pK: e.copy(out=k_[0:64, :], in_=psf(pK)[0:64, :]))
                sc.dma(ks[b][h, 0:64, cols], k_[0:64, :], Rk, [Rk], [R_ks[b][h]])
                sc.dma(ks[b][h, 64:96, cols], krope[P, :], R_krope, [R_krope], [R_ks[b][h]], ser=False)
            if kmla <= 5:
                return _bail()
            for j in range(4):
                pV = j % 2

                def mmv(e, j=j, pV=pV):
                    for kc in range(2):
                        ins = e.matmul(psf(pV), ckvT[:, kc, j * 128:(j + 1) * 128],
                                       wvv[:, kc, :], start=(kc == 0), stop=(kc == 1))
                    return ins
                sc.op("pe", [R_wkv, R_ckvT], [PB[pV]], mmv)
                sc.op("dve", [PB[pV]], [R_vsb], lambda e, j=j, pV=pV: e.tensor_copy(
                    out=vm[:, :, j, 0:64], in_=psf(pV).rearrange("p (h c) -> p h c", c=64)))
            for h in range(8):
                dst = vs[b][h].rearrange("p (t d) -> p t d", d=128)[:, qc * 4:(qc + 1) * 4, :]
                sc.dma(dst, vm[:, h, :, :], R_vsb, [R_vsb], [R_vs[b][h]], ser=False)
        A.release(m)
        sc.barrier()

    bg = [None]
    MW = {}

    def bg_step():
        if bg[0] is not None:
            try:
                next(bg[0])
            except StopIteration:
                bg[0] = None

    def bg_drain():
        while bg[0] is not None:
            bg_step()

    def gen_merge_weights(l):
        wv = w_in[l].rearrange("(kc p) n -> p kc n", p=128)
        wo = [A.alloc_top([128, 4, D], BF16) for _ in range(3)]
        wg = A.alloc_top([128, KC, 3 * D], BF16)
        wout = A.alloc_top([128, KC, D], BF16)
        stg = [A.alloc_top([128, KC, 256], F32) for _ in range(3)]
        R_stg = [Res(f"bstg{i}") for i in range(3)]
        R_wo = [Res(f"bwo{i}") for i in range(3)]
        R_wg, R_wout = Res("bwg"), Res("bwout")
        MW.update(wo=wo, wg=wg, wout=wout, R_wo=R_wo, R_wg=R_wg, R_wout=R_wout, stg=stg, R_stg=R_stg)
        units = []
        for b in range(3):
            wob = w_o[b][l].rearrange("(kc p) n -> p kc n", p=128)
            for half in range(2):
                units.append((wob[:, :, half * 512:(half + 1) * 512], 4, 512, wo[b][:, :, half * 512:(half + 1) * 512], R_wo[b]))
        for n in range(12):
            units.append((wv[:, :, OFF_GATE + n * 256:OFF_GATE + (n + 1) * 256], KC, 256, wg[:, :, n * 256:(n + 1) * 256], R_wg))
        wov = w_out[l].rearrange("(kc p) n -> p kc n", p=128)
        for n in range(4):
            units.append((wov[:, :, n * 256:(n + 1) * 256], KC, 256, wout[:, :, n * 256:(n + 1) * 256], R_wout))
        pend = []
        for i, (src, a_, c_, dst, Rd) in enumerate(units):
            s_ = i % 3
            stv = stg[s_].rearrange("p k c -> p (k c)").rearrange("p (k c) -> p k c", c=c_)
            if len(pend) == 2:
                pstv, pdst, pRd, ps_ = pend.pop(0)
                sc.op("dve", [R_stg[ps_]], [pRd], lambda e, pstv=pstv, pdst=pdst: e.tensor_copy(out=pdst, in_=pstv))
            sc.dma(stv, src, R_stg[s_], [R_none], [R_stg[s_]])
            pend.append((stv, dst, Rd, s_))
            yield
        for (pstv, pdst, pRd, ps_) in pend:
            sc.op("dve", [R_stg[ps_]], [pRd], lambda e, pstv=pstv, pdst=pdst: e.tensor_copy(out=pdst, in_=pstv))
            yield

    mod_done = set()

    def gen_mod(l2):
        stg, R_stg = MW["stg"], MW["R_stg"]
        ct = A.alloc_top([128, 8], F32)
        cond = A.alloc_top([128, 8], F32)
        R_ct, R_cond = Res("bct"), Res("bcond")
        brow = [A.alloc_top([128, 256], F32) for _ in range(3)]
        mrow = [A.alloc_top([128, 256], F32) for _ in range(3)]
        R_brow = [Res(f"bbrow{i}") for i in range(3)]
        R_mrow = [Res(f"bmrow{i}") for i in range(3)]
        sc.dma(ct, c_t, R_ct, [R_none], [R_ct])
        sc.op("act", [R_ct], [R_cond], lambda e: e.activation(out=cond, in_=ct, func=AF.Silu))
        wv = w_ada[l2].rearrange("(kc p) n -> p kc n", p=128)
        pb = 7
        pend = []

        def finish(n, s_):
            def mm(e):
                for kc in range(KC):
                    ins = e.matmul(psf(pb)[0:1, 0:256], cond[:, kc:kc + 1], stg[s_][:, kc, :],
                                   start=(kc == 0), stop=(kc == KC - 1))
                return ins
            sc.op("pe", [R_cond, R_stg[s_]], [PB[pb]], mm)
            sc.op("dve", [PB[pb], R_brow[s_]], [R_mrow[s_]], lambda e: e.tensor_tensor(
                out=mrow[s_][0:1, :], in0=psf(pb)[0:1, 0:256], in1=brow[s_][0:1, :], op=ALU.add))
            sc.dma(modrow[l2:l2 + 1, n * 256:(n + 1) * 256], mrow[s_][0:1, :], R_mrow[s_], [R_mrow[s_]], [R_mod[l2][n // 2]])
        for n in range(24):
            s_ = n % 3
            if len(pend) == 2:
                finish(*pend.pop(0))
            sc.dma(stg[s_], wv[:, :, n * 256:(n + 1) * 256], R_stg[s_], [R_none], [R_stg[s_]])
            sc.dma(brow[s_][0:1, :], b_ada[l2:l2 + 1, n * 256:(n + 1) * 256], R_brow[s_], [R_none], [R_brow[s_]])
            pend.append((n, s_))
            yield
        for p in pend:
            finish(*p)
            yield
        mod_done.add(l2)

    def gen_bg(l):
        yield from gen_merge_weights(l)
        if l + 1 < depth:
            yield from gen_mod(l + 1)

    def start_bg(l):
        bg[0] = gen_bg(l)

    def phase_attn_softmax(b, scale):
        A.release(cbf_mark)
        m = A.mark()
        K = KQ[b]
        KP = 96 if b == 0 else K
        qsb = [A.alloc([128, S], BF16) for _ in range(2)]
        ksb = [A.alloc([128, S], BF16) for _ in range(2)]
        vsb = [A.alloc([128, NT, 128], BF16) for _ in range(2)]
        R_q = [Res(f"aq{i}") for i in range(2)]
        R_k = [Res(f"ak{i}") for i in range(2)]
        R_v = [Res(f"av{i}") for i in range(2)]
        R_kf = [Res(f"akf{i}") for i in range(2)]
        for i in range(2):
            if b == 0:
                sc.op("pool", [], [R_q[i]], lambda e, i=i: e.memset(qsb[i][64:96, :], 0.0))
                sc.op("pool", [], [R_k[i]], lambda e, i=i: e.memset(ksb[i][64:96, :], 0.0))
                sc.op("pool", [R_q[i]], [R_q[i]], lambda e, i=i: e.memset(qsb[i][64:70, :], 1.0))
                sc.op("pool", [R_k[i]], [R_k[i], R_kf[i]], lambda e, i=i: e.memset(ksb[i][64:70, :], 1.0))
        NPT = 4
        pt = [A.alloc([128, 512], BF16) for _ in range(NPT)]
        R_pt = [Res(f"pt{i}") for i in range(NPT)]
        rec = [A.alloc([64, 512], F32) for _ in range(3)]
        R_rec = [Res(f"rec{i}") for i in range(3)]
        yo = [A.alloc([64, 512], BF16) for _ in range(4)]
        R_yo = [Res(f"yo{i}") for i in range(4)]
        SBK = [0, 1, 2, 3]
        OB = [4, 5, 6]
        mask = masks[b]

        def load_head(h):
            s_ = h % 2
            q_, k_, v_ = qsb[s_], ksb[s_], vsb[s_]
            if b == 0:
                sc.dma(q_[0:67, :], qs[b][h, 0:67, :], R_q[s_], [R_qs[b][h], R_qF], [R_q[s_]])
                sc.dma(k_[0:64, :], ks[b][h, 0:64, :], R_k[s_], [R_ks[b][h]], [R_k[s_]])
                sc.dma(k_[67:70, :], ks[b][h, 67:70, :], R_kf[s_], [R_kF], [R_kf[s_]])
            else:
                sc.dma(q_[0:K, :], qs[b][h, :, :], R_q[s_], [R_qs[b][h]], [R_q[s_]])
                sc.dma(k_[0:K, :], ks[b][h, :, :], R_k[s_], [R_ks[b][h]], [R_k[s_]])
            sc.dma(v_, vs[b][h].rearrange("p (t d) -> p t d", d=128), R_v[s_], [R_vs[b][h]], [R_v[s_]])

        tiles = []
        chain = 0
        for h in range(8):
            for qc in range(NQ):
                nkb = 4 * qc + 4
                for kb in range(nkb):
                    j = kb - 4 * qc
                    tiles.append(dict(h=h, s=h % 2, qc=qc, kb=kb, j=j, c0=(128 * j if j > 0 else 0), first=(kb == 0),
                                      last=(kb == nkb - 1), ob=OB[chain % 3], os=chain % 3, ys=chain % 4,
                                      lasthead=(kb == nkb - 1 and qc == NQ - 1)))
                chain += 1
        T = len(tiles)

        def emit_qk(t):
            d = tiles[t]
            sbk = SBK[t % 4]
            q_, k_ = qsb[d["s"]], ksb[d["s"]]
            c0, kb, qc, j = d["c0"], d["kb"], d["qc"], d["j"]

            def f(e):
                ins = e.matmul(psf(sbk)[:, c0:512], k_[0:KP, kb * 128:(kb + 1) * 128],
                               q_[0:KP, qc * 512 + c0:(qc + 1) * 512], start=True, stop=(j < 0))
                if j >= 0:
                    ins = e.matmul(psf(sbk)[:, c0:c0 + 128], ident, mask, start=False, stop=True)
                return ins
            sc.op("pe", [R_q[d["s"]], R_k[d["s"]], R_kf[d["s"]], R_cbf], [PB[sbk]], f)

        def emit_exp(t):
            d = tiles[t]
            sbk = SBK[t % 4]
            pi = t % NPT
            c0 = d["c0"]
            sc.op("act", [PB[sbk]], [R_pt[pi]], lambda e: e.activation(
                out=pt[pi][:, c0:512], in_=psf(sbk)[:, c0:512], func=AF.Exp, scale=scale))

        def emit_pv(t):
            d = tiles[t]
            pi = t % NPT
            c0, kb, ob, os_ = d["c0"], d["kb"], d["ob"], d["os"]
            v_ = vsb[d["s"]]
            sc.op("pe", [R_v[d["s"]], R_pt[pi]], [PB[ob]], lambda e: e.matmul(
                psf(ob)[:, c0:512], v_[:, kb, :], pt[pi][:, c0:512], start=d["first"], stop=d["last"]))
            if d["last"]:
                h, qc = d["h"], d["qc"]
                sc.op("dve", [PB[ob]], [R_rec[os_]], lambda e: e.reciprocal(out=rec[os_], in_=psf(ob)[64:128, :]))
                ys = d["ys"]
                sc.op("dve", [PB[ob], R_rec[os_]], [R_yo[ys]], lambda e: e.tensor_tensor(
                    out=yo[ys], in0=psf(ob)[0:64, :], in1=rec[os_], op=ALU.mult))
                sc.dma(ybr[b, h * 64:(h + 1) * 64, qc * 512:(qc + 1) * 512], yo[ys], R_yo[ys], [R_yo[ys]],
                       [R_ybr[b][h][qc]])
            if d["lasthead"] and d["h"] + 2 < 8:
                load_head(d["h"] + 2)

        load_head(0)
        load_head(1)
        for t in range(-2, T):
            if t + 2 < T:
                emit_qk(t + 2)
            if 0 <= t + 1 < T:
                emit_exp(t + 1)
            if t >= 0:
                emit_pv(t)
            if t % 16 == 0:
                bg_step()
        A.release(persist_mark)
        sc.barrier()

    def phase_attn_sb():
        b = 2
        A.release(cbf_mark)
        m = A.mark()
        qsb = [A.alloc([64, S], BF16) for _ in range(2)]
        ksb = [A.alloc([64, S], BF16) for _ in range(2)]
        vsb = [A.alloc([128, NT, 64], BF16) for _ in range(2)]
        R_q = [Res(f"sq{i}") for i in range(2)]
        R_k = [Res(f"sk{i}") for i in range(2)]
        R_v = [Res(f"sv{i}") for i in range(2)]
        NB = 4
        eb = [A.alloc([128, 512], F32) for _ in range(NB)]
        spb = [A.alloc([128, 512], BF16) for _ in range(NB)]
        ecb = [A.alloc([128, 512], F32) for _ in range(NB)]
        ab = [A.alloc([128, 512], BF16) for _ in range(NB)]
        R_e = [Res(f"e{i}") for i in range(NB)]
        R_sp = [Res(f"sp{i}") for i in range(NB)]
        R_ec = [Res(f"ec{i}") for i in range(NB)]
        R_a = [Res(f"a{i}") for i in range(NB)]
        yo = [A.alloc([64, 512], BF16) for _ in range(4)]
        R_yo = [Res(f"syo{i}") for i in range(4)]
        ZB = [0, 1, 2]
        ACC = [3, 4]
        OB = [5, 6]
        mask = masks[2]

        def load_head(h):
            s_ = h % 2
            sc.dma(qsb[s_], qs[b][h, :, :], R_q[s_], [R_qs[b][h]], [R_q[s_]])
            sc.dma(ksb[s_], ks[b][h, :, :], R_k[s_], [R_ks[b][h]], [R_k[s_]])
            sc.dma(vsb[s_], vs[b][h].rearrange("p (t d) -> p t d", d=64), R_v[s_], [R_vs[b][h]], [R_v[s_]])

        tiles = []
        chain = 0
        for h in range(8):
            for qc in range(NQ):
                nkb = 4 * qc + 4
                for idx, kb in enumerate(range(nkb - 1, -1, -1)):
                    j = kb - 4 * qc
                    tiles.append(dict(h=h, s=h % 2, qc=qc, kb=kb, j=j, c0=(128 * j if j > 0 else 0), first=(idx == 0),
                                      last=(idx == nkb - 1), acc=ACC[chain % 2], ob=OB[chain % 2], os=chain % 4,
                                      lasthead=(idx == nkb - 1 and qc == NQ - 1)))
                chain += 1
        T = len(tiles)

        def st_qk(t):
            d = tiles[t]
            zb = ZB[t % 3]
            q_, k_ = qsb[d["s"]], ksb[d["s"]]
            c0, kb, qc, j = d["c0"], d["kb"], d["qc"], d["j"]

            def f(e):
                ins = e.matmul(psf(zb)[:, c0:512], k_[:, kb * 128:(kb + 1) * 128],
                               q_[:, qc * 512 + c0:(qc + 1) * 512], start=True, stop=(j < 0))
                if j >= 0:
                    ins = e.matmul(psf(zb)[:, c0:c0 + 128], ident, mask, start=False, stop=True)
                return ins
            sc.op("pe", [R_q[d["s"]], R_k[d["s"]], R_cbf], [PB[zb]], f)

        def st_act1(t):
            d = tiles[t]
            zb = ZB[t % 3]
            bi = t % NB
            c0 = d["c0"]
            sc.op("act", [PB[zb]], [R_e[bi]], lambda e: e.activation(
                out=eb[bi][:, c0:512], in_=psf(zb)[:, c0:512], func=AF.Exp, scale=0.125))
            sc.op("act", [R_e[bi]], [R_sp[bi]], lambda e: e.activation(
                out=spb[bi][:, c0:512], in_=eb[bi][:, c0:512], func=AF.Ln, bias=1.0, scale=1.0))

        def st_tri(t):
            d = tiles[t]
            bi = t % NB
            c0, acc = d["c0"], d["acc"]
            sc.op("pe", [R_sp[bi], R_cbf], [PB[acc]], lambda e: e.matmul(
                psf(acc)[:, c0:512], ntri, spb[bi][:, c0:512], start=d["first"], stop=True, skip_group_check=True))

        def st_expc(t):
            d = tiles[t]
            bi = t % NB
            c0, acc = d["c0"], d["acc"]
            sc.op("act", [PB[acc]], [R_ec[bi]], lambda e: e.activation(
                out=ecb[bi][:, c0:512], in_=psf(acc)[:, c0:512], func=AF.Exp))
            sc.op("pool", [R_e[bi], R_ec[bi]], [R_a[bi]], lambda e: e.tensor_tensor(
                out=ab[bi][:, c0:512], in0=eb[bi][:, c0:512], in1=ecb[bi][:, c0:512], op=ALU.mult))

        def st_u(t):
            d = tiles[t]
            if d["last"]:
                return
            bi = t % NB
            c0, acc = d["c0"], d["acc"]
            sc.op("pe", [R_sp[bi], R_cbf], [PB[acc]], lambda e: e.matmul(
                psf(acc)[:, c0:512], nu, spb[bi][:, c0:512], start=False, stop=True, skip_group_check=True))

        def st_pv(t):
            d = tiles[t]
            bi = t % NB
            c0, kb, ob, os_ = d["c0"], d["kb"], d["ob"], d["os"]
            v_ = vsb[d["s"]]
            sc.op("pe", [R_v[d["s"]], R_a[bi]], [PB[ob]], lambda e: e.matmul(
                psf(ob)[0:64, c0:512], v_[:, kb, :], ab[bi][:, c0:512], start=d["first"], stop=d["last"],
                skip_group_check=True))
            if d["last"]:
                h, qc = d["h"], d["qc"]
                sc.op("dve", [PB[ob]], [R_yo[os_]], lambda e: e.tensor_copy(out=yo[os_], in_=psf(ob)[0:64, :]))
                sc.dma(ybr[b, h * 64:(h + 1) * 64, qc * 512:(qc + 1) * 512], yo[os_], R_yo[os_], [R_yo[os_]],
                       [R_ybr[b][h][qc]])
            if d["lasthead"] and d["h"] + 2 < 8:
                load_head(d["h"] + 2)

        load_head(0)
        load_head(1)
        for t in range(-2, T + 2):
            if 0 <= t - 1 < T:
                st_u(t - 1)
            if 0 <= t < T:
                st_tri(t)
            if 0 <= t + 2 < T:
                st_qk(t + 2)
            if 0 <= t + 1 < T:
                st_act1(t + 1)
            if 0 <= t < T:
                st_expc(t)
            if 0 <= t - 2 < T:
                st_pv(t - 2)
            if t % 16 == 0:
                bg_step()
        A.release(persist_mark)
        sc.barrier()

    def phase_merge(l):
        A.release(cbf_mark)
        m = A.mark()
        if not MW:
            bg[0] = gen_bg(l)
        bg_drain()
        wo, wg, wout = MW["wo"], MW["wg"], MW["wout"]
        R_wo, R_wg, R_wout = MW["R_wo"], MW["R_wg"], MW["R_wout"]
        GT = A.alloc([128, D], F32)
        R_GT = Res("GT")
        bcast_load(GT, modrow[l:l + 1, 2 * D:3 * D], R_GT, [R_mod[l][4], R_mod[l][5]])
        yb = [A.alloc([128, 4, 512], BF16) for _ in range(3)]
        R_yb = [Res(f"yb{b}") for b in range(3)]
        uc = [A.alloc([128, KC, 512], BF16) for _ in range(2)]
        R_uc = [Res(f"uc{i}") for i in range(2)]
        mT = A.alloc([128, KC, 512], BF16)
        R_mT = Res("mT")
        sig = [A.alloc([128, 512], F32) for _ in range(2)]
        R_sig = [Res(f"sig{i}") for i in range(2)]
        accm = [A.alloc([128, 512], F32) for _ in range(2)]
        R_accm = [Res(f"accm{i}") for i in range(2)]
        tmpm = [A.alloc([128, 512], F32) for _ in range(2)]
        R_tmpm = [Res(f"tmpm{i}") for i in range(2)]
        xt = [A.alloc([128, D], F32) for _ in range(2)]
        R_xt = [Res(f"mxt{i}") for i in range(2)]
        xo = [A.alloc([128, D], F32) for _ in range(2)]
        R_xo = [Res(f"mxo{i}") for i in range(2)]
        cnt = 0
        xcnt = 0
        for qc in range(NQ):
            cols = slice(qc * 512, (qc + 1) * 512)
            us = qc % 2
            sc.dma(uc[us], uts[:, :, cols], R_uc[us], [R_uts[qc]], [R_uc[us]])
            for b in branches:
                src = ybr[b].rearrange("(f p) s -> p f s", p=128)[:, :, cols]
                sc.dma(yb[b], src, R_yb[b], [R_ybr[b][h][qc] for h in range(8)], [R_yb[b]])
            for nci in range(KC):
                a_ = nci % 2
                for b in branches:
                    p1 = 0 + (cnt % 2)
                    p2 = 2 + (cnt % 2)
                    g_ = cnt % 2
                    cnt += 1

                    def mm1(e, b=b, p1=p1, nci=nci):
                        for f in range(4):
                            ins = e.matmul(psf(p1), wo[b][:, f, nci * 128:(nci + 1) * 128], yb[b][:, f, :],
                                           start=(f == 0), stop=(f == 3))
                        return ins
                    sc.op("pe", [R_wo[b], R_yb[b]], [PB[p1]], mm1)

                    def mm2(e, b=b, p2=p2, nci=nci):
                        for kc in range(KC):
                            ins = e.matmul(psf(p2), wg[:, kc, b * D + nci * 128:b * D + (nci + 1) * 128], uc[us][:, kc, :],
                                           start=(kc == 0), stop=(kc == KC - 1))
                        return ins
                    sc.op("pe", [R_wg, R_uc[us]], [PB[p2]], mm2)
                    sc.op("act", [PB[p2]], [R_sig[g_]], lambda e, g_=g_, p2=p2: e.activation(
                        out=sig[g_], in_=psf(p2), func=AF.Sigmoid))
                    if len(branches) == 1:
                        sc.op("dve", [PB[p1], R_sig[g_]], [R_mT], lambda e, g_=g_, p1=p1, nci=nci: e.tensor_tensor(
                            out=mT[:, nci, :], in0=psf(p1), in1=sig[g_], op=ALU.mult))
                    elif b == branches[0]:
                        sc.op("dve", [PB[p1], R_sig[g_]], [R_accm[a_]], lambda e, g_=g_, p1=p1, a_=a_: e.tensor_tensor(
                            out=accm[a_], in0=psf(p1), in1=sig[g_], op=ALU.mult))
                    else:
                        sc.op("dve", [PB[p1], R_sig[g_]], [R_tmpm[g_]], lambda e, g_=g_, p1=p1: e.tensor_tensor(
                            out=tmpm[g_], in0=psf(p1), in1=sig[g_], op=ALU.mult))
                        if b != branches[-1]:
                            sc.op("pool", [R_tmpm[g_], R_accm[a_]], [R_accm[a_]], lambda e, g_=g_, a_=a_: e.tensor_tensor(
                                out=accm[a_], in0=accm[a_], in1=tmpm[g_], op=ALU.add))
                        else:
                            sc.op("pool", [R_tmpm[g_], R_accm[a_]], [R_mT], lambda e, g_=g_, a_=a_, nci=nci: e.tensor_tensor(
                                out=mT[:, nci, :], in0=accm[a_], in1=tmpm[g_], op=ALU.add))
            for j in range(4):
                t = qc * 4 + j
                xs = xcnt % 2
                xcnt += 1
                srcx = x_in if l == 0 else y
                sc.dma(xt[xs], srcx[t * 128:(t + 1) * 128, :], R_xt[xs], [R_none if l == 0 else R_y[t]], [R_xt[xs]])
                for n in range(2):
                    po = 4 + n

                    def mmo(e, j=j, n=n, po=po):
                        for kc in range(KC):
                            ins = e.matmul(psf(po), mT[:, kc, j * 128:(j + 1) * 128], wout[:, kc, n * 512:(n + 1) * 512],
                                           start=(kc == 0), stop=(kc == KC - 1))
                        return ins
                    sc.op("pe", [R_mT, R_wout], [PB[po]], mmo)
                    sc.op("dve", [PB[po], R_GT], [R_xo[xs]], lambda e, n=n, po=po, xs=xs: e.tensor_tensor(
                        out=xo[xs][:, n * 512:(n + 1) * 512], in0=psf(po), in1=GT[:, n * 512:(n + 1) * 512], op=ALU.mult))
                sc.op("dve", [R_xo[xs], R_xt[xs]], [R_xo[xs]], lambda e, xs=xs: e.tensor_tensor(
                    out=xo[xs], in0=xo[xs], in1=xt[xs], op=ALU.add))
                sc.dma(y[t * 128:(t + 1) * 128, :], xo[xs], R_xo[xs], [R_xo[xs]], [R_y[t]])
        A.release(persist_mark)
        A.htop = A.n
        MW.clear()
        sc.barrier()

    def phase_ffn(l, final):
        import os as _os
        FT = int(_os.environ.get("FFNFT", "256"))
        A.release(cbf_mark)
        wg = A.alloc([128, KC, DFF], BF16)
        wu = A.alloc([128, KC, DFF], BF16)
        wd = A.alloc([128, FC, D], BF16)
        NU = DFF // 256
        R_wg = [Res(f"fwg{i}") for i in range(NU)]
        R_wu = [Res(f"fwu{i}") for i in range(NU)]
        R_wd = [Res(f"fwd{i}") for i in range(FC // 2)]
        m = A.mark()
        stg = [A.alloc([128, KC, 256], F32) for _ in range(2)]
        R_stg = [Res(f"fstg{i}") for i in range(2)]
        ld = 0
        wgv = w_gate[l].rearrange("(kc p) n -> p kc n", p=128)
        wuv = w_up[l].rearrange("(kc p) n -> p kc n", p=128)
        for u in range(NU):
            for (wvv, dst, Rd) in ((wgv, wg, R_wg[u]), (wuv, wu, R_wu[u])):
                c = u * 256
                s_ = ld % 2
                ld += 1
                sc.dma(stg[s_], wvv[:, :, c:c + 256], R_stg[s_], [R_none], [R_stg[s_]])
                cast(dst[:, :, c:c + 256], stg[s_], [R_stg[s_]], [Rd])
        wdv = w_down[l].rearrange("(fc p) n -> p fc n", p=128)
        for f0 in range(0, FC, 2):
            s_ = ld % 2
            ld += 1
            stv = stg[s_].rearrange("p k c -> p (k c)").rearrange("p (k c) -> p k c", c=1024)
            sc.dma(stv, wdv[:, f0:f0 + 2, :], R_stg[s_], [R_none], [R_stg[s_]])
            cast(wd[:, f0:f0 + 2, :], stv, [R_stg[s_]], [R_wd[f0 // 2]])
        A.release(m)
        xo = [A.alloc([128, D], F32) for _ in range(2)]
        R_xo = [Res(f"fxo{i}") for i in range(2)]
        xr = [A.alloc([128, D], F32) for _ in range(1)]
        R_xr = [Res(f"xr{i}") for i in range(1)]
        GF = A.alloc([128, D], F32)
        R_GF = Res("GF")
        assert A.mark() == m + 4 * D * 4
        nb = NormBufs(2, 1)
        G, SH, R_G, R_SH = load_mod_tiles(l, 3, 4, g_ffn[l:l + 1, :], GM=nb.tmp[0])
        GT = A.alloc([128, D], F32)
        R_GT = Res("fGT")
        bcast_load(GT, modrow[l:l + 1, 5 * D:6 * D], R_GT, [R_mod[l][10], R_mod[l][11]])
        ufT2 = [A.alloc([128, KC, FT], BF16) for _ in range(2)]
        R_ufT2 = [Res(f"ufT{i}") for i in range(2)]
        hT = A.alloc([128, FC, FT], BF16)
        R_hT = Res("hT")
        sg = [A.alloc([128, FT], F32) for _ in range(2)]
        R_sg = [Res(f"fsg{i}") for i in range(2)]
        fss = [A.alloc([128, 2], F32) for _ in range(2)]
        R_fss = [Res(f"fss{i}") for i in range(2)]
        cnt = 0
        xcnt = 0
        NJ = FT // 128
        ysrc = y
        NCH = S // FT

        def pro_load(ch):
            for j in range(NJ):
                t = ch * NJ + j
                sc.dma(nb.xt[j], ysrc[t * 128:(t + 1) * 128, :], nb.R_xt[j], [R_y[t]], [nb.R_xt[j]])

        def pro_norm(ch):
            for j in range(NJ):
                norm_tile(nb, j, G, SH, R_G, R_SH)

        def pro_T(ch):
            for j in range(NJ):
                transpose_tile(nb.u[j], nb.R_u[j], j % 2, ufT2[ch % 2][:, :, j * 128:(j + 1) * 128], R_ufT2[ch % 2],
                               "act" if j % 2 == 0 else "dve")

        def gu(ch, f):
            nonlocal cnt
            ufT, R_ufT = ufT2[ch % 2], R_ufT2[ch % 2]
            pg = 2 + (cnt % 2)
            pu = 4 + (cnt % 2)
            g_ = cnt % 2
            cnt += 1

            def mmg(e):
                for kc in range(KC):
                    ins = e.matmul(psf(pg)[:, 0:FT], wg[:, kc, f * 128:(f + 1) * 128], ufT[:, kc, :],
                                   start=(kc == 0), stop=(kc == KC - 1))
                return ins
            sc.op("pe", [R_wg[f // 2], R_ufT], [PB[pg]], mmg)

            def mmu(e):
                for kc in range(KC):
                    ins = e.matmul(psf(pu)[:, 0:FT], wu[:, kc, f * 128:(f + 1) * 128], ufT[:, kc, :],
                                   start=(kc == 0), stop=(kc == KC - 1))
                return ins
            sc.op("pe", [R_wu[f // 2], R_ufT], [PB[pu]], mmu)
            sc.op("act", [PB[pg]], [R_sg[g_]], lambda e: e.activation(out=sg[g_], in_=psf(pg)[:, 0:FT], func=AF.Silu))
            sc.op("dve", [PB[pu], R_sg[g_]], [R_hT], lambda e: e.tensor_tensor(
                out=hT[:, f, :], in0=psf(pu)[:, 0:FT], in1=sg[g_], op=ALU.mult))

        pro_load(0)
        pro_norm(0)
        pro_T(0)
        for ch in range(NCH):
            if ch + 1 < NCH:
                pro_load(ch + 1)
            for f in range(FC // 2):
                gu(ch, f)
            if ch + 1 < NCH:
                pro_norm(ch + 1)
                pro_T(ch + 1)
            for f in range(FC // 2, FC):
                gu(ch, f)
            if final and ch == 0:
                bcast_load(GF, g_final[0:1, :], R_GF, [R_none] + R_wd)
            for j in range(NJ):
                t = ch * NJ + j
                xs = xcnt % 2
                xcnt += 1
                sc.dma(xr[0], ysrc[t * 128:(t + 1) * 128, :], R_xr[0], [R_y[t]] + R_wd, [R_xr[0]])
                for n in range(2):
                    po = 6 + n

                    def mmd(e, j=j, n=n, po=po):
                        for f in range(FC):
                            ins = e.matmul(psf(po), hT[:, f, j * 128:(j + 1) * 128], wd[:, f, n * 512:(n + 1) * 512],
                                           start=(f == 0), stop=(f == FC - 1))
                        return ins
                    sc.op("pe", [R_hT] + R_wd, [PB[po]], mmd)
                    sc.op("dve", [PB[po], R_GT], [R_xo[xs]], lambda e, n=n, po=po, xs=xs: e.tensor_tensor(
                        out=xo[xs][:, n * 512:(n + 1) * 512], in0=psf(po), in1=GT[:, n * 512:(n + 1) * 512], op=ALU.mult))
                sc.op("dve", [R_xo[xs], R_xr[0]], [R_xo[xs]], lambda e, xs=xs: e.tensor_tensor(
                    out=xo[xs], in0=xo[xs], in1=xr[0], op=ALU.add))
                if final:
                    ss = fss[xs]
                    Rss = R_fss[xs]
                    sc.op("act", [R_xo[xs]], [nb.R_junk, Rss], lambda e, xs=xs, ss=ss: e.activation(
                        out=nb.junk, in_=xo[xs], func=AF.Square, accum_out=ss[:, 0:1]))
                    sc.op("dve", [Rss], [Rss], lambda e, ss=ss: e.tensor_scalar(
                        out=ss[:, 0:1], in0=ss[:, 0:1], scalar1=1.0 / D, scalar2=EPS, op0=ALU.mult, op1=ALU.add))
                    sc.op("act", [Rss], [Rss], lambda e, ss=ss: e.activation(out=ss[:, 0:1], in_=ss[:, 0:1], func=AF.Ln))
                    sc.op("act", [Rss], [Rss], lambda e, ss=ss: e.activation(
                        out=ss[:, 0:1], in_=ss[:, 0:1], func=AF.Exp, scale=-0.5))
                    sc.op("dve", [R_xo[xs], Rss, R_GF], [R_xo[xs]], lambda e, xs=xs, ss=ss: e.scalar_tensor_tensor(
                        out=xo[xs], in0=xo[xs], scalar=ss[:, 0:1], in1=GF, op0=ALU.mult, op1=ALU.mult))
                sc.dma(y[t * 128:(t + 1) * 128, :], xo[xs], R_xo[xs], [R_xo[xs]], [R_y[t]])
        A.release(persist_mark)
        sc.barrier()

    cbf_mark = A.mark() - KC * S * 2
    assert cbf_mark >= 0

    import os as _os
    stop = int(_os.environ.get("KSTOP", "999"))
    plist = []
    for l in range(depth):
        plist.append(lambda l=l: phase_mod(l))
        plist.append(lambda l=l: phase_norm_attn(l))
        if 0 in branches:
            plist.append(lambda l=l: phase_proj_qkv(l, 0, OFF_FOXQ))
            plist.append(lambda l=l: phase_fox_F(l))
        if 1 in branches:
            plist.append(lambda l=l: phase_proj_mla(l))
        if 2 in branches:
            plist.append(lambda l=l: phase_proj_qkv(l, 2, OFF_SB))
        if 0 in branches:
            plist.append(lambda l=l: phase_attn_softmax(0, 0.125))
        if 1 in branches:
            plist.append(lambda l=l: phase_attn_softmax(1, float(96 ** -0.5)))
        if 2 in branches:
            plist.append(lambda l=l: start_bg(l))
            plist.append(lambda l=l: phase_attn_sb())
        plist.append(lambda l=l: phase_merge(l))
        plist.append(lambda l=l: phase_ffn(l, final=(l == depth - 1)))
    if _os.environ.get("ONLYFFN"):
        plist = [lambda: phase_mod(0), lambda: phase_ffn(0, final=bool(int(_os.environ.get("FFNFINAL", "1"))))]
    for i, p in enumerate(plist):
        if i >= stop:
            break
        p()
    sc.barrier(engines=("sp",))
    build.info = dict(peak=A.peak, nwait=sc.nwait, cnt=dict(sc.cnt), nsem=sc.nsem)
    return nc


_CACHE = {}


def _prep_core(inputs, bidx, S):
    c = np.asarray(inputs["c"], np.float32)[bidx]
    m = {
        "x": np.ascontiguousarray(np.asarray(inputs["x"], np.float32)[bidx, :S]),
        "c_t": np.ascontiguousarray(c.reshape(8, 128).T),
        "pos": np.ascontiguousarray(np.asarray(inputs["positions"], np.int32)[bidx, :S].reshape(1, S)),
        "g_final": np.ascontiguousarray(np.asarray(inputs["g_final"], np.float32).reshape(1, D)),
        "consts": make_consts(),
    }
    for k in ("g_mix", "w_ada", "b_ada", "w_in", "b_fox_f", "g_mla_q", "w_mla_uq", "g_mla_kv", "w_mla_ukv",
              "w_o_fox", "w_o_mla", "w_o_sb", "w_out", "g_ffn", "w_ffn_gate", "w_ffn_up", "w_ffn_down"):
        m[k] = np.ascontiguousarray(np.asarray(inputs[k], np.float32))
    return m


def kernel(**inputs):
    x = np.asarray(inputs["x"])
    B, S, _ = x.shape
    key = (S,)
    if key not in _CACHE:
        _CACHE[key] = build(S)
    nc = _CACHE[key]
    in_maps = [_prep_core(inputs, b, S) for b in range(B)]
    res = run_bass_kernel_spmd(nc, in_maps, core_ids=list(range(B)))
    out = np.stack([np.asarray(r["y"], np.float32) for r in res.results], axis=0)
    return out
```

```python
import numpy as np
import ml_dtypes
import concourse.bass as bass
import concourse.mybir as mybir
from concourse.bass_utils import run_bass_kernel_spmd

F32, BF16, I32, U8 = mybir.dt.float32, mybir.dt.bfloat16, mybir.dt.int32, mybir.dt.uint8
AF = mybir.ActivationFunctionType
ALU = mybir.AluOpType

D = 1024
KC = 8
DFF = 2816
FC = 22
DEPTH = 2
EPS = 1e-6
OFF_FOXQ, OFF_FOXF, OFF_QL, OFF_KVL, OFF_KR, OFF_SB, OFF_GATE, INW = 0, 1536, 1544, 1928, 2184, 2216, 3752, 6824
NEG = -30000.0
TWO_PI = float(2.0 * np.pi)
PI = float(np.pi)
CW1 = 6.28125
CW2 = float(2.0 * np.pi - 6.28125)

C_IDENT, C_NTRI, C_NU, C_NTRI2, C_NONES, C_MFOX, C_MMLA, C_MSB, C_INV, C_SGN, C_END = (
    0, 128, 256, 384, 512, 640, 768, 896, 1024, 1025, 1026)


def make_consts():
    c = np.zeros((128, C_END), np.float32)
    i = np.arange(128)
    c[:, C_IDENT:C_IDENT + 128] = np.eye(128)
    j, s = i[:, None], i[None, :]
    c[:, C_NTRI:C_NTRI + 128] = -1.0 * (j >= s)
    c[:, C_NU:C_NU + 128] = -1.0 * (j < s)
    c[:, C_NTRI2:C_NTRI2 + 128] = -1.0 * (j <= s)
    c[:, C_NONES:C_NONES + 128] = -1.0
    k, q = i[:, None], i[None, :]
    c[:, C_MFOX:C_MFOX + 128] = NEG * (k > q)
    c[:, C_MMLA:C_MMLA + 128] = NEG * ((k >= 64) & (q < 64))
    c[:, C_MSB:C_MSB + 128] = NEG * (k >= q)
    inv = (np.float32(10000.0) ** (-(np.arange(16, dtype=np.float32) / np.float32(16)))).astype(np.float32)
    c[64:80, C_INV] = inv
    c[80:96, C_INV] = inv
    c[64:80, C_SGN] = -1.0
    c[80:96, C_SGN] = 1.0
    return c


class Res:
    __slots__ = ("name", "w", "r", "sem", "semv", "excl")

    def __init__(self, name, excl=False):
        self.name = name
        self.excl = excl
        self.w = None
        self.r = {}
        self.sem = None
        self.semv = 0


class Sched:
    def __init__(self, nc):
        self.nc = nc
        self.eng = {"pe": nc.tensor, "act": nc.scalar, "dve": nc.vector, "pool": nc.gpsimd, "sp": nc.sync}
        self.sem = {k: nc.alloc_semaphore("s_" + k) for k in ("pe", "act", "dve", "pool")}
        self.cnt = {k: 0 for k in self.sem}
        self.seen = {k: {} for k in self.eng}
        self.dma_res = []
        self.nwait = 0
        self.sem_pool = []
        self.nsem = 0

    def _wait(self, e, tok):
        if tok is None:
            return
        sem, val, src = tok
        if src == "pe" and e == "pe":
            return
        sid = id(sem)
        if self.seen[e].get(sid, 0) >= val:
            return
        self.eng[e].wait_ge(sem, val)
        self.nwait += 1
        self.seen[e][sid] = val

    def _deps(self, e, reads, writes):
        for r in reads:
            self._wait(e, r.w)
            if r.excl:
                for tok in r.r.values():
                    if tok[2] != e:
                        self._wait(e, tok)
        for w in writes:
            self._wait(e, w.w)
            for tok in w.r.values():
                if tok[2] == e and e != "sp":
                    continue
                self._wait(e, tok)

    def _post(self, tok, reads, writes):
        sid = id(tok[0])
        for r in reads:
            r.r[sid] = tok
        for w in writes:
            w.w = tok
            w.r = {}

    def op(self, e, reads, writes, fn):
        self._deps(e, reads, writes)
        ins = fn(self.eng[e])
        self.cnt[e] += 1
        ins.then_inc(self.sem[e], 1)
        tok = (self.sem[e], self.cnt[e], e)
        self._post(tok, reads, writes)

    def dma(self, out_ap, in_ap, sb, reads, writes, q="sp", ser=True):
        wr = list(writes)
        if sb not in wr and ser:
            wr_dep = wr + [sb]
        else:
            wr_dep = wr
        self._deps(q, reads, wr_dep)
        if sb.sem is None:
            if self.sem_pool:
                sb.sem, sb.semv = self.sem_pool.pop()
            else:
                sb.sem = self.nc.alloc_semaphore(f"d{self.nsem}")
                sb.semv = 0
                self.nsem += 1
            self.dma_res.append(sb)
        ins = self.eng[q].dma_start(out=out_ap, in_=in_ap)
        sb.semv += 16
        ins.then_inc(sb.sem, 16)
        tok = (sb.sem, sb.semv, "dma")
        self._post(tok, reads, wr)

    def barrier(self, engines=("pe", "act", "dve", "pool", "sp")):
        toks = [(self.sem[k], self.cnt[k], k) for k in self.sem if self.cnt[k] > 0]
        toks += [(r.sem, r.semv, "dma") for r in self.dma_res if r.semv > 0]
        for e in engines:
            for t in toks:
                self._wait(e, t)
        if len(engines) == 5:
            for r in self.dma_res:
                self.sem_pool.append((r.sem, r.semv))
                r.sem = None
            self.dma_res = []


class Arena:
    def __init__(self, nc, nbytes):
        self.t = nc.alloc_sbuf_tensor("arena", [128, nbytes], U8)
        self.n = nbytes
        self.top = 0
        self.peak = 0
        self.htop = nbytes

    def alloc(self, shape, dt, parts=128):
        esz = {F32: 4, BF16: 2, I32: 4}[dt]
        free = int(np.prod(shape[1:]))
        nb = (free * esz + 31) // 32 * 32
        off = self.top
        assert off + nb <= self.htop, f"arena overflow {off}+{nb}>{self.htop}"
        self.top += nb
        self.peak = max(self.peak, self.top)
        v = self.t[:, off:off + free * esz].bitcast(dt)
        if len(shape) == 3:
            v = v.rearrange("p (a b) -> p a b", b=shape[2])
        elif len(shape) == 4:
            v = v.rearrange("p (a b c) -> p a b c", b=shape[2], c=shape[3])
        if shape[0] < 128:
            v = v[0:shape[0]]
        return v

    def alloc_top(self, shape, dt):
        esz = {F32: 4, BF16: 2, I32: 4}[dt]
        free = int(np.prod(shape[1:]))
        nb = (free * esz + 31) // 32 * 32
        self.htop -= nb
        off = self.htop
        assert off >= self.top, "arena top/bottom collision"
        v = self.t[:, off:off + free * esz].bitcast(dt)
        if len(shape) == 3:
            v = v.rearrange("p (a b) -> p a b", b=shape[2])
        return v

    def mark(self):
        return self.top

    def release(self, m):
        self.top = m


def build(S, depth=DEPTH, branches=(0, 1, 2)):
    NT = S // 128
    NQ = S // 512
    nc = bass.Bass("TRN2", target_bir_lowering=False)

    def din(name, shape, dt=F32):
        return nc.dram_tensor(name, list(shape), dt, kind="ExternalInput").ap()

    x_in = din("x", [S, D])
    c_t = din("c_t", [128, 8])
    pos = din("pos", [1, S], I32)
    g_mix = din("g_mix", [DEPTH, D])
    w_ada = din("w_ada", [DEPTH, D, 6 * D])
    b_ada = din("b_ada", [DEPTH, 6 * D])
    w_in = din("w_in", [DEPTH, D, INW])
    b_fox_f = din("b_fox_f", [DEPTH, 8])
    g_mla_q = din("g_mla_q", [DEPTH, 384])
    w_mla_uq = din("w_mla_uq", [DEPTH, 384, 768])
    g_mla_kv = din("g_mla_kv", [DEPTH, 256])
    w_mla_ukv = din("w_mla_ukv", [DEPTH, 256, 1024])
    w_o = [din("w_o_fox", [DEPTH, 512, D]), din("w_o_mla", [DEPTH, 512, D]), din("w_o_sb", [DEPTH, 512, D])]
    w_out = din("w_out", [DEPTH, D, D])
    g_ffn = din("g_ffn", [DEPTH, D])
    w_gate = din("w_ffn_gate", [DEPTH, D, DFF])
    w_up = din("w_ffn_up", [DEPTH, D, DFF])
    w_down = din("w_ffn_down", [DEPTH, DFF, D])
    g_final = din("g_final", [1, D])
    consts = din("consts", [128, C_END])
    y = nc.dram_tensor("y", [S, D], F32, kind="ExternalOutput").ap()

    KQ = [70, 96, 64]
    modrow = nc.dram_tensor("modrow", [DEPTH, 6 * D], F32).ap()
    qs = [nc.dram_tensor(f"qs{b}", [8, KQ[b], S], BF16).ap() for b in range(3)]
    ks = [nc.dram_tensor(f"ks{b}", [8, KQ[b], S], BF16).ap() for b in range(3)]
    VW = [128, 128, 64]
    vs = [nc.dram_tensor(f"vs{b}", [8, 128, NT * VW[b]], BF16).ap() for b in range(3)]
    ybr = nc.dram_tensor("ybr", [3, 512, S], BF16).ap()
    uts = nc.dram_tensor("uts", [128, KC, S], BF16).ap()

    sc = Sched(nc)
    A = Arena(nc, 207 * 1024)
    ps = [nc.alloc_psum_tensor(f"ps{i}", [128, 512], F32) for i in range(8)]
    PB = [Res(f"pb{i}", excl=True) for i in range(8)]

    def psf(i):
        return ps[i][:]

    def psb(i):
        return ps[i][:].bitcast(BF16)

    R_mod = [[Res(f"mod{l}_{n}") for n in range(12)] for l in range(DEPTH)]
    R_y = [Res(f"y{t}") for t in range(NT)]
    R_qs = [[Res(f"qs{b}_{h}") for h in range(8)] for b in range(3)]
    R_ks = [[Res(f"ks{b}_{h}") for h in range(8)] for b in range(3)]
    R_vs = [[Res(f"vs{b}_{h}") for h in range(8)] for b in range(3)]
    R_qF = Res("qF")
    R_kF = Res("kF")
    R_ybr = [[[Res(f"ybr{b}_{h}_{q}") for q in range(NQ)] for h in range(8)] for b in range(3)]
    R_none = Res("ext")
    R_uts = [Res(f"uts{q}") for q in range(NQ)]

    cst = A.alloc([128, C_END], F32)
    R_cst = Res("cst")
    cbf = A.alloc([128, 1024], BF16)
    R_cbf = Res("cbf")
    sc.dma(cst, consts, R_cst, [R_none], [R_cst])
    sc.op("dve", [R_cst], [R_cbf], lambda e: e.tensor_copy(out=cbf, in_=cst[:, 0:1024]))
    ident = cbf[:, C_IDENT:C_IDENT + 128]
    ntri = cbf[:, C_NTRI:C_NTRI + 128]
    nu = cbf[:, C_NU:C_NU + 128]
    masks = [cbf[:, C_MFOX:C_MFOX + 128], cbf[:, C_MMLA:C_MMLA + 128], cbf[:, C_MSB:C_MSB + 128]]
    ntri2_f = cst[:, C_NTRI2:C_NTRI2 + 128]
    nones_f = cst[:, C_NONES:C_NONES + 128]

    uT = A.alloc([128, KC, S], BF16)
    R_uT = [Res(f"uT{q}") for q in range(NQ)]
    persist_mark = A.mark()

    cast_ctr = [0]

    def cast(dst, src, reads, writes):
        cast_ctr[0] += 1
        if cast_ctr[0] % 2 == 0:
            sc.op("dve", reads, writes, lambda e: e.tensor_copy(out=dst, in_=src))
        else:
            sc.op("act", reads, writes, lambda e: e.copy(out=dst, in_=src))

    def bcast_load(dst, row_ap, Rdst, reads):
        sc.dma(dst, row_ap.partition_broadcast(128), Rdst, reads, [Rdst])

    def phase_mod(l):
        if l in mod_done:
            return
        m = A.mark()
        ct = A.alloc([128, 8], F32)
        cond = A.alloc([128, 8], F32)
        R_ct, R_cond = Res("ct"), Res("cond")
        wst = [A.alloc([128, KC, 512], F32) for _ in range(2)]
        R_wst = [Res(f"wst{i}") for i in range(2)]
        brow = [A.alloc([1, 512], F32) for _ in range(2)]
        R_brow = [Res(f"brow{i}") for i in range(2)]
        mrow = [A.alloc([1, 512], F32) for _ in range(2)]
        R_mrow = [Res(f"mrow{i}") for i in range(2)]
        sc.dma(ct, c_t, R_ct, [R_none], [R_ct])
        sc.op("act", [R_ct], [R_cond], lambda e: e.activation(out=cond, in_=ct, func=AF.Silu))
        wv = w_ada[l].rearrange("(kc p) n -> p kc n", p=128)
        for n in range(12):
            s_ = n % 2
            sc.dma(wst[s_], wv[:, :, n * 512:(n + 1) * 512], R_wst[s_], [R_none], [R_wst[s_]])
            sc.dma(brow[s_][0:1, :], b_ada[l:l + 1, n * 512:(n + 1) * 512], R_brow[s_], [R_none], [R_brow[s_]])
            pb = n % 2

            def mm(e, s_=s_, pb=pb):
                for kc in range(KC):
                    ins = e.matmul(psf(pb)[0:1, :], cond[:, kc:kc + 1], wst[s_][:, kc, :],
                                   start=(kc == 0), stop=(kc == KC - 1))
                return ins
            sc.op("pe", [R_cond, R_wst[s_]], [PB[pb]], mm)
            sc.op("dve", [PB[pb], R_brow[s_]], [R_mrow[s_]],
                  lambda e, s_=s_, pb=pb: e.tensor_tensor(out=mrow[s_][0:1, :], in0=psf(pb)[0:1, :],
                                                          in1=brow[s_][0:1, :], op=ALU.add))
            sc.dma(modrow[l:l + 1, n * 512:(n + 1) * 512], mrow[s_][0:1, :], R_mrow[s_], [R_mrow[s_]], [R_mod[l][n]])
        A.release(m)
        sc.barrier()

    def load_mod_tiles(l, i_sh, i_sc, grow):
        G = A.alloc([128, D], F32)
        SH = A.alloc([128, D], F32)
        GM = A.alloc([128, D], F32)
        R_G, R_SH, R_GM = Res("G"), Res("SH"), Res("GM")
        bcast_load(G, modrow[l:l + 1, i_sc * D:(i_sc + 1) * D], R_G, [R_mod[l][2 * i_sc], R_mod[l][2 * i_sc + 1]])
        bcast_load(SH, modrow[l:l + 1, i_sh * D:(i_sh + 1) * D], R_SH, [R_mod[l][2 * i_sh], R_mod[l][2 * i_sh + 1]])
        bcast_load(GM, grow, R_GM, [R_none])
        sc.op("dve", [R_G, R_GM], [R_G],
              lambda e: e.scalar_tensor_tensor(out=G, in0=G, scalar=1.0, in1=GM, op0=ALU.add, op1=ALU.mult))
        return G, SH, R_G, R_SH

    class NormBufs:
        def __init__(self, n=2, ntmp=2):
            self.n = n
            self.xt = [A.alloc([128, D], F32) for _ in range(n)]
            self.R_xt = [Res(f"xt{i}") for i in range(n)]
            self.tmp = [A.alloc([128, D], F32) for _ in range(ntmp)]
            self.R_tmp = [Res(f"ntmp{i}") for i in range(ntmp)]
            self.u = [A.alloc([128, D], BF16) for _ in range(n)]
            self.R_u = [Res(f"u{i}") for i in range(n)]
            self.junk = A.alloc([128, D], BF16)
            self.R_junk = Res("junk")
            self.ss = [A.alloc([128, 2], F32) for _ in range(n)]
            self.R_ss = [Res(f"ss{i}") for i in range(n)]

    def norm_tile(nb, s_, G, SH, R_G, R_SH):
        ti = s_ % len(nb.tmp)
        xt, ss, tmp, u = nb.xt[s_], nb.ss[s_], nb.tmp[ti], nb.u[s_]
        Rx, Rss, Rt, Ru = nb.R_xt[s_], nb.R_ss[s_], nb.R_tmp[ti], nb.R_u[s_]
        sc.op("act", [Rx], [nb.R_junk, Rss],
              lambda e: e.activation(out=nb.junk, in_=xt, func=AF.Square, accum_out=ss[:, 0:1]))
        sc.op("dve", [Rss], [Rss], lambda e: e.tensor_scalar(out=ss[:, 0:1], in0=ss[:, 0:1], scalar1=1.0 / D,
                                                             scalar2=EPS, op0=ALU.mult, op1=ALU.add))
        sc.op("act", [Rss], [Rss], lambda e: e.activation(out=ss[:, 0:1], in_=ss[:, 0:1], func=AF.Ln))
        sc.op("act", [Rss], [Rss], lambda e: e.activation(out=ss[:, 0:1], in_=ss[:, 0:1], func=AF.Exp, scale=-0.5))
        sc.op("dve", [Rx, Rss, R_G], [Rt],
              lambda e: e.scalar_tensor_tensor(out=tmp, in0=xt, scalar=ss[:, 0:1], in1=G, op0=ALU.mult, op1=ALU.mult))
        sc.op("dve", [Rt, R_SH], [Ru], lambda e: e.tensor_tensor(out=u, in0=tmp, in1=SH, op=ALU.add))

    def transpose_tile(u, Ru, pb, dst3, Rdst, eng):
        def tr(e):
            for kc in range(KC):
                ins = e.transpose(psb(pb)[:, kc * 128:(kc + 1) * 128], u[:, kc * 128:(kc + 1) * 128], ident)
            return ins
        sc.op("pe", [Ru, R_cbf], [PB[pb]], tr)
        src = psb(pb).rearrange("p (a b) -> p a b", b=128)
        if eng == "act":
            sc.op("act", [PB[pb]], [Rdst], lambda e: e.copy(out=dst3, in_=src))
        else:
            sc.op("dve", [PB[pb]], [Rdst], lambda e: e.tensor_copy(out=dst3, in_=src))

    def phase_norm_attn(l):
        m = A.mark()
        G, SH, R_G, R_SH = load_mod_tiles(l, 0, 1, g_mix[l:l + 1, :])
        nb = NormBufs(3, 2)
        src = x_in if l == 0 else y

        def p1(t):
            s_ = t % 3
            xt, ss = nb.xt[s_], nb.ss[s_]
            sc.dma(xt, src[t * 128:(t + 1) * 128, :], nb.R_xt[s_], [R_none if l == 0 else R_y[t]], [nb.R_xt[s_]])
            sc.op("act", [nb.R_xt[s_]], [nb.R_junk, nb.R_ss[s_]],
                  lambda e: e.activation(out=nb.junk, in_=xt, func=AF.Square, accum_out=ss[:, 0:1]))
            sc.op("dve", [nb.R_ss[s_]], [nb.R_ss[s_]], lambda e: e.tensor_scalar(
                out=ss[:, 0:1], in0=ss[:, 0:1], scalar1=1.0 / D, scalar2=EPS, op0=ALU.mult, op1=ALU.add))

        def p2(t):
            s_ = t % 3
            ti = t % 2
            xt, ss, tmp, u = nb.xt[s_], nb.ss[s_], nb.tmp[ti], nb.u[s_]
            Rx, Rss, Rt, Ru = nb.R_xt[s_], nb.R_ss[s_], nb.R_tmp[ti], nb.R_u[s_]
            sc.op("act", [Rss], [Rss], lambda e: e.activation(out=ss[:, 0:1], in_=ss[:, 0:1], func=AF.Ln))
            sc.op("act", [Rss], [Rss], lambda e: e.activation(out=ss[:, 0:1], in_=ss[:, 0:1], func=AF.Exp, scale=-0.5))
            sc.op("dve", [Rx, Rss, R_G], [Rt], lambda e: e.scalar_tensor_tensor(
                out=tmp, in0=xt, scalar=ss[:, 0:1], in1=G, op0=ALU.mult, op1=ALU.mult))
            sc.op("dve", [Rt, R_SH], [Ru], lambda e: e.tensor_tensor(out=u, in0=tmp, in1=SH, op=ALU.add))

        def p3(t):
            s_ = t % 3
            transpose_tile(nb.u[s_], nb.R_u[s_], t % 2, uT[:, :, t * 128:(t + 1) * 128], R_uT[t // 4],
                           "act" if t % 2 == 0 else "dve")

        p1(0)
        for t in range(NT):
            if t + 1 < NT:
                p1(t + 1)
            p2(t)
            p3(t)
        for q in range(NQ):
            sc.dma(uts[:, :, q * 512:(q + 1) * 512], uT[:, :, q * 512:(q + 1) * 512], R_uT[q], [R_uT[q]], [R_uts[q]])
        A.release(m)
        sc.barrier()

    def load_w_cols(l, wsrc_view, c0, ncols, stg, R_stg, dst, R_dst, kcs=KC):
        sc.dma(stg[:, 0:kcs, 0:ncols], wsrc_view[:, :, c0:c0 + ncols], R_stg, [R_none], [R_stg])
        sc.op("pool", [R_stg], [R_dst], lambda e: e.tensor_copy(out=dst[:, 0:kcs, 0:ncols], in_=stg[:, 0:kcs, 0:ncols]))

    def proj_featmajor(w3, Rw, c0, M, kcs, rhs_fn, R_rhs, pb, nkc_rows=None):
        def mm(e):
            for kc in range(kcs):
                ins = e.matmul(psf(pb)[0:M, :], w3[:, kc, c0:c0 + M], rhs_fn(kc), start=(kc == 0), stop=(kc == kcs - 1))
            return ins
        sc.op("pe", [Rw] + R_rhs, [PB[pb]], mm)

    def phase_proj_qkv(l, b, colbase):
        m = A.mark()
        wv = w_in[l].rearrange("(kc p) n -> p kc n", p=128)
        stg = [A.alloc([128, KC, 384], F32) for _ in range(2)]
        R_stg = [Res(f"stg{i}") for i in range(2)]
        wb = [A.alloc([128, KC, 384], BF16) for _ in range(2)]
        R_wb = [Res(f"wb{i}") for i in range(2)]
        R_stg3 = [[Res(f"stg{i}_{j}") for j in range(3)] for i in range(2)]
        qT2 = [A.alloc([128, S], BF16) for _ in range(2)]
        kT2 = [A.alloc([128, S], BF16) for _ in range(2)]
        W = VW[b]
        vh2 = [[A.alloc([128, NT, W], BF16) for _ in range(2)] for _ in range(2)]
        R_qT2 = [Res(f"qT{i}") for i in range(2)]
        R_kT2 = [Res(f"kT{i}") for i in range(2)]
        R_vh2 = [[Res(f"vh{i}_{j}") for j in range(2)] for i in range(2)]
        if W == 128:
            for i in range(2):
                for j in range(2):
                    sc.op("pool", [], [R_vh2[i][j]], lambda e, i=i, j=j: e.memset(vh2[i][j][:, :, 64:128], 1.0))
        for hp in range(4):
            s_ = hp % 2
            qT, kT = qT2[s_], kT2[s_]
            R_qT, R_kT = R_qT2[s_], R_kT2[s_]
            for i in range(3):
                c0 = colbase + i * 512 + hp * 128
                sc.dma(stg[s_][:, :, i * 128:(i + 1) * 128], wv[:, :, c0:c0 + 128], R_stg3[s_][i], [R_none], [R_stg3[s_][i]])
            cast(wb[s_], stg[s_], R_stg3[s_], [R_wb[s_]] + R_stg3[s_])
            cnt = 0
            for i, (dstT, R_d) in enumerate(((qT, R_qT), (kT, R_kT))):
                for tc in range(NQ):
                    pb = cnt % 2
                    cnt += 1
                    proj_featmajor(wb[s_], R_wb[s_], i * 128, 128, KC,
                                   lambda kc, tc=tc: uT[:, kc, tc * 512:(tc + 1) * 512], [R_uT[tc]], pb)
                    if cnt % 2 == 0:
                        sc.op("act", [PB[pb]], [R_d], lambda e, pb=pb, dstT=dstT, tc=tc: e.copy(
                            out=dstT[:, tc * 512:(tc + 1) * 512], in_=psf(pb)))
                    else:
                        sc.op("dve", [PB[pb]], [R_d], lambda e, pb=pb, dstT=dstT, tc=tc: e.tensor_copy(
                            out=dstT[:, tc * 512:(tc + 1) * 512], in_=psf(pb)))
            for tg in range(NT // 4):
                pb = 2 + tg % 2

                def mmv(e, tg=tg, pb=pb, s_=s_):
                    for j in range(4):
                        t = tg * 4 + j
                        for kc in range(KC):
                            ins = e.matmul(psf(pb)[:, j * 128:(j + 1) * 128], uT[:, kc, t * 128:(t + 1) * 128],
                                           wb[s_][:, kc, 256:384], start=(kc == 0), stop=(kc == KC - 1))
                    return ins
                sc.op("pe", [R_wb[s_], R_uT[tg]], [PB[pb]], mmv)
                src3 = psf(pb).rearrange("p (a b) -> p a b", b=128)
                sc.op("dve", [PB[pb]], [R_vh2[s_][0]], lambda e, tg=tg, src3=src3: e.tensor_copy(
                    out=vh2[s_][0][:, tg * 4:(tg + 1) * 4, 0:64], in_=src3[:, :, 0:64]))
                sc.op("act", [PB[pb]], [R_vh2[s_][1]], lambda e, tg=tg, src3=src3: e.copy(
                    out=vh2[s_][1][:, tg * 4:(tg + 1) * 4, 0:64], in_=src3[:, :, 64:128]))
            for hh in range(2):
                h = 2 * hp + hh
                sc.dma(qs[b][h, 0:64, :], qT[hh * 64:(hh + 1) * 64, :], R_qT, [R_qT], [R_qs[b][h]], ser=False)
                sc.dma(ks[b][h, 0:64, :], kT[hh * 64:(hh + 1) * 64, :], R_kT, [R_kT], [R_ks[b][h]], ser=False)
                sc.dma(vs[b][h].rearrange("p (t d) -> p t d", d=W), vh2[s_][hh], R_vh2[s_][hh],
                       [R_vh2[s_][hh]], [R_vs[b][h]])
        A.release(m)
        sc.barrier()

    def phase_fox_F(l):
        m = A.mark()
        wv = w_in[l].rearrange("(kc p) n -> p kc n", p=128)
        stg = A.alloc([128, KC, 8], F32)
        wF = A.alloc([128, KC, 8], BF16)
        R_stg, R_wF = Res("stgF"), Res("wF")
        bF = A.alloc([128, 8], F32)
        R_bF = Res("bF")
        lf = A.alloc([128, NT, 8], F32)
        R_lf = Res("lf")
        sc.dma(stg, wv[:, :, OFF_FOXF:OFF_FOXF + 8], R_stg, [R_none], [R_stg])
        cast(wF, stg, [R_stg], [R_wF])
        bcast_load(bF, b_fox_f[l:l + 1, :], R_bF, [R_none])

        def mmf(e):
            for t in range(NT):
                for kc in range(KC):
                    ins = e.matmul(psf(0)[:, t * 8:(t + 1) * 8], uT[:, kc, t * 128:(t + 1) * 128], wF[:, kc, :],
                                   start=(kc == 0), stop=(kc == KC - 1))
            return ins
        sc.op("pe", [R_wF] + R_uT, [PB[0]], mmf)
        for t in range(NT):
            sc.op("dve", [PB[0], R_bF], [R_lf], lambda e, t=t: e.tensor_tensor(
                out=lf[:, t, :], in0=psf(0)[:, t * 8:(t + 1) * 8], in1=bF, op=ALU.add))
        lf2 = lf.rearrange("p t h -> p (t h)")
        sc.op("act", [R_lf], [R_lf], lambda e: e.activation(out=lf2, in_=lf2, func=AF.Exp, scale=-1.0))
        sc.op("act", [R_lf], [R_lf], lambda e: e.activation(out=lf2, in_=lf2, func=AF.Ln, bias=1.0, scale=1.0))
        f8 = [A.alloc([8, 512], F32) for _ in range(2)]
        r1 = [A.alloc([8, 512], F32) for _ in range(2)]
        fall = [A.alloc([8, 3, 512], BF16) for _ in range(2)]
        nfall = [A.alloc([8, 3, 512], BF16) for _ in range(2)]
        R_f8 = [Res(f"f8{i}") for i in range(2)]
        R_r1 = [Res(f"r1{i}") for i in range(2)]
        R_fall = [Res(f"fall{i}") for i in range(2)]
        R_nfall = [Res(f"nfall{i}") for i in range(2)]
        for qc in range(NQ):
            s_ = qc % 2
            pb = 1 + qc % 2

            def mmc(e, qc=qc, pb=pb):
                for j in range(4):
                    ti = qc * 4 + j
                    for tj in range(ti + 1):
                        ins = e.matmul(psf(pb)[0:8, j * 128:(j + 1) * 128], lf[:, tj, :],
                                       ntri2_f if tj == ti else nones_f, start=(tj == 0), stop=(tj == ti))
                return ins
            sc.op("pe", [R_lf, R_cst], [PB[pb]], mmc)
            f, r, fa, nfa = f8[s_], r1[s_], fall[s_], nfall[s_]
            Rf, Rr, Rfa, Rnfa = R_f8[s_], R_r1[s_], R_fall[s_], R_nfall[s_]
            sc.op("dve", [PB[pb]], [Rf], lambda e, f=f, pb=pb: e.tensor_scalar(
                out=f, in0=psf(pb)[0:8, :], scalar1=8.0, scalar2=None, op0=ALU.mult))
            sc.op("dve", [Rf], [Rfa], lambda e, f=f, fa=fa: e.tensor_copy(out=fa[:, 0, :], in_=f))
            sc.op("dve", [Rf, Rfa], [Rr], lambda e, f=f, fa=fa, r=r: e.tensor_tensor(
                out=r, in0=f, in1=fa[:, 0, :], op=ALU.subtract))
            sc.op("dve", [Rr], [Rfa], lambda e, fa=fa, r=r: e.tensor_copy(out=fa[:, 1, :], in_=r))
            sc.op("dve", [Rr, Rfa], [Rr], lambda e, fa=fa, r=r: e.tensor_tensor(
                out=r, in0=r, in1=fa[:, 1, :], op=ALU.subtract))
            sc.op("dve", [Rr], [Rfa], lambda e, fa=fa, r=r: e.tensor_copy(out=fa[:, 2, :], in_=r))
            sc.op("dve", [Rfa], [Rnfa], lambda e, fa=fa, nfa=nfa: e.tensor_scalar(
                out=nfa, in0=fa, scalar1=-1.0, scalar2=None, op0=ALU.mult))
            sc.dma(qs[0][:, 64:67, qc * 512:(qc + 1) * 512], fa, Rfa, [Rfa], [R_qF])
            sc.dma(ks[0][:, 67:70, qc * 512:(qc + 1) * 512], nfa, Rnfa, [Rnfa], [R_kF])
        A.release(m)
        sc.barrier()

    def phase_proj_mla(l):
        m = A.mark()
        b = 1
        wv = w_in[l].rearrange("(kc p) n -> p kc n", p=128)
        stg = A.alloc([128, KC, 672], F32)
        R_stg = Res("stgm")
        wl = A.alloc([128, KC, 640], BF16)
        wkr = A.alloc([128, KC, 96], BF16)
        wkrs = A.alloc([128, KC, 96], BF16)
        R_wl, R_wkr = Res("wl"), Res("wkr")
        sc.dma(stg, wv[:, :, OFF_QL:OFF_QL + 672], R_stg, [R_none], [R_stg])
        cast(wl, stg[:, :, 0:640], [R_stg], [R_wl])
        sc.op("pool", [R_stg], [R_wkr], lambda e: e.tensor_copy(out=wkr, in_=stg[:, :, 576:672]))
        sc.op("pool", [R_stg], [R_wkr], lambda e: e.tensor_copy(out=wkrs[:, :, 0:64], in_=stg[:, :, 576:640]))
        sc.op("pool", [R_stg], [R_wkr], lambda e: e.tensor_copy(out=wkrs[:, :, 64:80], in_=stg[:, :, 656:672]))
        sc.op("pool", [R_stg], [R_wkr], lambda e: e.tensor_copy(out=wkrs[:, :, 80:96], in_=stg[:, :, 640:656]))
        stq = A.alloc([128, 3, 768], F32)
        R_stq = Res("stq")
        wq = A.alloc([128, 3, 768], BF16)
        wqs = A.alloc([128, 3, 768], BF16)
        R_wq = Res("wq")
        sc.dma(stq, w_mla_uq[l].rearrange("(kc p) n -> p kc n", p=128), R_stq, [R_none], [R_stq])
        cast(wq, stq, [R_stq], [R_wq])
        stq4 = stq.rearrange("p k (h c) -> p k h c", c=96)
        wqs4 = wqs.rearrange("p k (h c) -> p k h c", c=96)
        for kc in range(3):
            sc.op("pool", [R_stq], [R_wq], lambda e, kc=kc: e.tensor_copy(out=wqs4[:, kc, :, 0:64], in_=stq4[:, kc, :, 0:64]))
            sc.op("pool", [R_stq], [R_wq], lambda e, kc=kc: e.tensor_copy(out=wqs4[:, kc, :, 64:80], in_=stq4[:, kc, :, 80:96]))
            sc.op("pool", [R_stq], [R_wq], lambda e, kc=kc: e.tensor_copy(out=wqs4[:, kc, :, 80:96], in_=stq4[:, kc, :, 64:80]))
        stkv = A.alloc([128, 2, 1024], F32)
        R_stkv = Res("stkv")
        wkv = A.alloc([128, 2, 1024], BF16)
        R_wkv = Res("wkv")
        sc.dma(stkv, w_mla_ukv[l].rearrange("(kc p) n -> p kc n", p=128), R_stkv, [R_none], [R_stkv])
        cast(wkv, stkv, [R_stkv], [R_wkv])
        wkk = A.alloc([128, 2, 512], BF16)
        wvv = A.alloc([128, 2, 512], BF16)
        stkv4 = stkv.rearrange("p k (h c) -> p k h c", c=128)
        for kc in range(2):
            sc.op("pool", [R_stkv], [R_wkv], lambda e, kc=kc: e.tensor_copy(
                out=wkk[:, kc, :].rearrange("p (h c) -> p h c", c=64), in_=stkv4[:, kc, :, 0:64]))
            sc.op("pool", [R_stkv], [R_wkv], lambda e, kc=kc: e.tensor_copy(
                out=wvv[:, kc, :].rearrange("p (h c) -> p h c", c=64), in_=stkv4[:, kc, :, 64:128]))
        gq = A.alloc([128, 384], F32)
        gkv = A.alloc([128, 256], F32)
        R_gq, R_gkv = Res("gq"), Res("gkv")
        bcast_load(gq, g_mla_q[l:l + 1, :], R_gq, [R_none])
        bcast_load(gkv, g_mla_kv[l:l + 1, :], R_gkv, [R_none])
        posi = A.alloc([96, 512], I32)
        ang = A.alloc([96, 512], F32)
        cs = A.alloc([96, 512], F32)
        ss_ = A.alloc([96, 512], F32)
        R_posi, R_ang, R_cs, R_ss = Res("posi"), Res("ang"), Res("cs"), Res("ssn")
        cqT = A.alloc([128, 3, 512], BF16)
        ckvT = A.alloc([128, 2, 512], BF16)
        R_cqT, R_ckvT = Res("cqT"), Res("ckvT")
        lat = [A.alloc([128, 640], F32) for _ in range(2)]
        R_lat = [Res(f"lat{i}") for i in range(2)]
        latb = [A.alloc([128, 640], BF16) for _ in range(2)]
        R_latb = [Res(f"latb{i}") for i in range(2)]
        st2 = [A.alloc([128, 2], F32) for _ in range(2)]
        R_st2 = [Res(f"st2{i}") for i in range(2)]
        junk = A.alloc([128, 384], BF16)
        R_junk = Res("junkm")
        krope = A.alloc([96, 512], BF16)
        R_krope = Res("krope")
        t1 = [A.alloc([96, 512], F32) for _ in range(2)]
        t2 = [A.alloc([96, 512], F32) for _ in range(2)]
        R_t1 = [Res(f"t1{i}") for i in range(2)]
        R_t2 = [Res(f"t2{i}") for i in range(2)]
        qh = [A.alloc([96, 512], BF16) for _ in range(2)]
        kh = [A.alloc([96, 512], BF16) for _ in range(2)]
        R_qh = [Res(f"qh{i}") for i in range(2)]
        R_kh = [Res(f"kh{i}") for i in range(2)]
        vm = A.alloc([128, 8, 4, 128], BF16)
        R_vsb = Res("vsbm")
        sc.op("pool", [], [R_vsb], lambda e: e.memset(vm[:, :, :, 64:128], 1.0))
        inv_c = cst[64:96, C_INV:C_INV + 1]
        sgn_c = cst[64:96, C_SGN:C_SGN + 1]
        P = slice(64, 96)
        import os as _os
        kmla = int(_os.environ.get("KMLA", "99"))

        def _bail():
            A.release(m)
            sc.barrier()
        if kmla <= 1:
            return _bail()
        for qc in range(NQ):
            cols = slice(qc * 512, (qc + 1) * 512)
            sc.dma(posi[P, :], pos[0:1, cols].partition_broadcast(32), R_posi, [R_none], [R_posi])
            sc.op("dve", [R_posi], [R_ang], lambda e: e.tensor_copy(out=ang[P, :], in_=posi[P, :]))
            sc.op("dve", [R_ang, R_cst], [R_ang], lambda e: e.tensor_scalar(
                out=ang[P, :], in0=ang[P, :], scalar1=inv_c, scalar2=None, op0=ALU.mult))
            sc.op("dve", [R_ang], [R_cs], lambda e: e.tensor_scalar(
                out=cs[P, :], in0=ang[P, :], scalar1=1.0 / TWO_PI, scalar2=None, op0=ALU.mult))
            sc.op("dve", [R_cs], [R_posi], lambda e: e.tensor_copy(out=posi[P, :], in_=cs[P, :]))
            sc.op("dve", [R_posi], [R_cs], lambda e: e.tensor_copy(out=cs[P, :], in_=posi[P, :]))
            sc.op("dve", [R_cs, R_ang], [R_ang], lambda e: e.scalar_tensor_tensor(
                out=ang[P, :], in0=cs[P, :], scalar=-CW1, in1=ang[P, :], op0=ALU.mult, op1=ALU.add))
            sc.op("dve", [R_cs, R_ang], [R_ang], lambda e: e.scalar_tensor_tensor(
                out=ang[P, :], in0=cs[P, :], scalar=-CW2, in1=ang[P, :], op0=ALU.mult, op1=ALU.add))
            sc.op("dve", [R_ang], [R_cs], lambda e: e.tensor_scalar(
                out=cs[P, :], in0=ang[P, :], scalar1=PI, scalar2=TWO_PI, op0=ALU.is_gt, op1=ALU.mult))
            sc.op("dve", [R_cs, R_ang], [R_ang], lambda e: e.tensor_tensor(
                out=ang[P, :], in0=ang[P, :], in1=cs[P, :], op=ALU.subtract))
            sc.op("dve", [R_ang], [R_cs], lambda e: e.tensor_scalar(
                out=cs[P, :], in0=ang[P, :], scalar1=-PI, scalar2=TWO_PI, op0=ALU.is_lt, op1=ALU.mult))
            sc.op("dve", [R_cs, R_ang], [R_ang], lambda e: e.tensor_tensor(
                out=ang[P, :], in0=ang[P, :], in1=cs[P, :], op=ALU.add))
            sc.op("dve", [R_ang], [R_cs], lambda e: e.tensor_scalar(
                out=cs[P, :], in0=ang[P, :], scalar1=PI / 2, scalar2=None, op0=ALU.add))
            sc.op("dve", [R_cs], [R_ss], lambda e: e.tensor_scalar(
                out=ss_[P, :], in0=cs[P, :], scalar1=PI, scalar2=TWO_PI, op0=ALU.is_gt, op1=ALU.mult))
            sc.op("dve", [R_cs, R_ss], [R_cs], lambda e: e.tensor_tensor(
                out=cs[P, :], in0=cs[P, :], in1=ss_[P, :], op=ALU.subtract))
            sc.op("act", [R_cs], [R_cs], lambda e: e.activation(out=cs[P, :], in_=cs[P, :], func=AF.Sin))
            sc.op("act", [R_ang], [R_ss], lambda e: e.activation(out=ss_[P, :], in_=ang[P, :], func=AF.Sin))
            sc.op("dve", [R_ss, R_cst], [R_ss], lambda e: e.tensor_scalar(
                out=ss_[P, :], in0=ss_[P, :], scalar1=sgn_c, scalar2=None, op0=ALU.mult))
            if kmla <= 2:
                return _bail()
            def lat_A(j):
                    t = qc * 4 + j
                    s_ = j % 2
                    tok = slice(t * 128, (t + 1) * 128)
                    pA, pB = (0, 1) if j % 2 == 0 else (6, 7)

                    def mml(e, tok=tok, pA=pA, pB=pB):
                        for kc in range(KC):
                            e.matmul(psf(pA)[:, 0:384], uT[:, kc, tok], wl[:, kc, 0:384], start=(kc == 0), stop=(kc == KC - 1))
                        for kc in range(KC):
                            ins = e.matmul(psf(pB)[:, 0:256], uT[:, kc, tok], wl[:, kc, 384:640], start=(kc == 0), stop=(kc == KC - 1))
                        return ins
                    sc.op("pe", [R_wl, R_uT[qc]], [PB[pA], PB[pB]], mml)

            def lat_B(j):
                    t = qc * 4 + j
                    s_ = j % 2
                    tok = slice(t * 128, (t + 1) * 128)
                    pA, pB = (0, 1) if j % 2 == 0 else (6, 7)
                    la, lb, st, Rla, Rlb, Rst = lat[s_], latb[s_], st2[s_], R_lat[s_], R_latb[s_], R_st2[s_]
                    sc.op("act", [PB[pA]], [R_junk, Rst], lambda e, st=st: e.activation(
                        out=junk[:, 0:384], in_=psf(pA)[:, 0:384], func=AF.Square, accum_out=st[:, 0:1]))
                    sc.op("act", [PB[pB]], [R_junk, Rst], lambda e, st=st: e.activation(
                        out=junk[:, 0:256], in_=psf(pB)[:, 0:256], func=AF.Square, accum_out=st[:, 1:2]))
                    sc.op("dve", [Rst], [Rst], lambda e, st=st: e.tensor_scalar(
                        out=st[:, 0:1], in0=st[:, 0:1], scalar1=1.0 / 384, scalar2=EPS, op0=ALU.mult, op1=ALU.add))
                    sc.op("dve", [Rst], [Rst], lambda e, st=st: e.tensor_scalar(
                        out=st[:, 1:2], in0=st[:, 1:2], scalar1=1.0 / 256, scalar2=EPS, op0=ALU.mult, op1=ALU.add))
                    sc.op("act", [Rst], [Rst], lambda e, st=st: e.activation(out=st, in_=st, func=AF.Ln))
                    sc.op("act", [Rst], [Rst], lambda e, st=st: e.activation(out=st, in_=st, func=AF.Exp, scale=-0.5))
                    sc.op("dve", [PB[pA], Rst, R_gq], [Rlb], lambda e, st=st, lb=lb: e.scalar_tensor_tensor(
                        out=lb[:, 0:384], in0=psf(pA)[:, 0:384], scalar=st[:, 0:1], in1=gq, op0=ALU.mult, op1=ALU.mult))
                    sc.op("dve", [PB[pB], Rst, R_gkv], [Rlb], lambda e, st=st, lb=lb: e.scalar_tensor_tensor(
                        out=lb[:, 384:640], in0=psf(pB)[:, 0:256], scalar=st[:, 1:2], in1=gkv, op0=ALU.mult, op1=ALU.mult))
                    pT = 2 if j % 2 == 0 else 5

                    def trl(e, lb=lb, pT=pT):
                        for kc in range(5):
                            ins = e.transpose(psb(pT)[:, kc * 128:(kc + 1) * 128], lb[:, kc * 128:(kc + 1) * 128], ident)
                        return ins
                    sc.op("pe", [Rlb, R_cbf], [PB[pT]], trl)
                    srcq = psb(pT)[:, 0:384].rearrange("p (a b) -> p a b", b=128)
                    srck = psb(pT)[:, 384:640].rearrange("p (a b) -> p a b", b=128)
                    sc.op("act", [PB[pT]], [R_cqT], lambda e, j=j, srcq=srcq: e.copy(out=cqT[:, :, j * 128:(j + 1) * 128], in_=srcq))
                    sc.op("dve", [PB[pT]], [R_ckvT], lambda e, j=j, srck=srck: e.tensor_copy(
                        out=ckvT[:, :, j * 128:(j + 1) * 128], in_=srck))

            lat_A(0)
            for j in range(4):
                if j + 1 < 4:
                    lat_A(j + 1)
                lat_B(j)
            if kmla <= 3:
                return _bail()
            pA, pB = 3, 4
            proj_featmajor(wkr, R_wkr, 0, 96, KC, lambda kc, cols=cols: uT[:, kc, cols], [R_uT[qc]], pA)
            proj_featmajor(wkrs, R_wkr, 0, 96, KC, lambda kc, cols=cols: uT[:, kc, cols], [R_uT[qc]], pB)
            sc.op("dve", [PB[pA], R_cs], [R_t1[0]], lambda e: e.tensor_tensor(
                out=t1[0][P, :], in0=psf(pA)[P, :], in1=cs[P, :], op=ALU.mult))
            sc.op("dve", [PB[pB], R_ss], [R_t2[0]], lambda e: e.tensor_tensor(
                out=t2[0][P, :], in0=psf(pB)[P, :], in1=ss_[P, :], op=ALU.mult))
            sc.op("pool", [R_t1[0], R_t2[0]], [R_krope], lambda e: e.tensor_tensor(
                out=krope[P, :], in0=t1[0][P, :], in1=t2[0][P, :], op=ALU.add))
            if kmla <= 4:
                return _bail()
            for h in range(8):
                s_ = h % 2
                pA, pB, pK = 3 + 3 * s_ - 3 * s_, 4, 5
                pA = 3 if s_ == 0 else 6
                pB = 4 if s_ == 0 else 7
                pK = 5 if s_ == 0 else 2
                proj_featmajor(wq, R_wq, h * 96, 96, 3, lambda kc: cqT[:, kc, :], [R_cqT], pA)
                proj_featmajor(wqs, R_wq, h * 96, 96, 3, lambda kc: cqT[:, kc, :], [R_cqT], pB)
                q_, k_, Rq, Rk = qh[s_], kh[s_], R_qh[s_], R_kh[s_]
                a1, a2, Ra1, Ra2 = t1[s_], t2[s_], R_t1[s_], R_t2[s_]
                sc.op("act", [PB[pA]], [Rq], lambda e, q_=q_, pA=pA: e.copy(out=q_[0:64, :], in_=psf(pA)[0:64, :]))
                sc.op("dve", [PB[pA], R_cs], [Ra1], lambda e, a1=a1, pA=pA: e.tensor_tensor(
                    out=a1[P, :], in0=psf(pA)[P, :], in1=cs[P, :], op=ALU.mult))
                sc.op("dve", [PB[pB], R_ss], [Ra2], lambda e, a2=a2, pB=pB: e.tensor_tensor(
                    out=a2[P, :], in0=psf(pB)[P, :], in1=ss_[P, :], op=ALU.mult))
                sc.op("pool", [Ra1, Ra2], [Rq], lambda e, a1=a1, a2=a2, q_=q_: e.tensor_tensor(
                    out=q_[P, :], in0=a1[P, :], in1=a2[P, :], op=ALU.add))
                sc.dma(qs[b][h, :, cols], q_[0:96, :], Rq, [Rq], [R_qs[b][h]])
                def mmk(e, h=h, pK=pK):
                    for kc in range(2):
                        ins = e.matmul(psf(pK)[0:64, :], wkk[:, kc, h * 64:(h + 1) * 64], ckvT[:, kc, :],
                                       start=(kc == 0), stop=(kc == 1))
                    return ins
                sc.op("pe", [R_wkv, R_ckvT], [PB[pK]], mmk)
                sc.op("act", [PB[pK]], [Rk], lambda e, k_=k_, pK=pK: e.copy(out=k_[0:64, :], in_=psf(pK)[0:64, :]))
                sc.dma(ks[b][h, 0:64, cols], k_[0:64, :], Rk, [Rk], [R_ks[b][h]])
                sc.dma(ks[b][h, 64:96, cols], krope[P, :], R_krope, [R_krope], [R_ks[b][h]], ser=False)
            if kmla <= 5:
                return _bail()
            for j in range(4):
                pV = j % 2

                def mmv(e, j=j, pV=pV):
                    for kc in range(2):
                        ins = e.matmul(psf(pV), ckvT[:, kc, j * 128:(j + 1) * 128],
                                       wvv[:, kc, :], start=(kc == 0), stop=(kc == 1))
                    return ins
                sc.op("pe", [R_wkv, R_ckvT], [PB[pV]], mmv)
                sc.op("dve", [PB[pV]], [R_vsb], lambda e, j=j, pV=pV: e.tensor_copy(
                    out=vm[:, :, j, 0:64], in_=psf(pV).rearrange("p (h c) -> p h c", c=64)))
            for h in range(8):
                dst = vs[b][h].rearrange("p (t d) -> p t d", d=128)[:, qc * 4:(qc + 1) * 4, :]
                sc.dma(dst, vm[:, h, :, :], R_vsb, [R_vsb], [R_vs[b][h]], ser=False)
        A.release(m)
        sc.barrier()

    bg = [None]
    MW = {}

    def bg_step():
        if bg[0] is not None:
            try:
                next(bg[0])
            except StopIteration:
                bg[0] = None

    def bg_drain():
        while bg[0] is not None:
            bg_step()

    def gen_merge_weights(l):
        wv = w_in[l].rearrange("(kc p) n -> p kc n", p=128)
        wo = [A.alloc_top([128, 4, D], BF16) for _ in range(3)]
        wg = A.alloc_top([128, KC, 3 * D], BF16)
        wout = A.alloc_top([128, KC, D], BF16)
        stg = [A.alloc_top([128, KC, 256], F32) for _ in range(3)]
        R_stg = [Res(f"bstg{i}") for i in range(3)]
        R_wo = [Res(f"bwo{i}") for i in range(3)]
        R_wg, R_wout = Res("bwg"), Res("bwout")
        MW.update(wo=wo, wg=wg, wout=wout, R_wo=R_wo, R_wg=R_wg, R_wout=R_wout, stg=stg, R_stg=R_stg)
        units = []
        for b in range(3):
            wob = w_o[b][l].rearrange("(kc p) n -> p kc n", p=128)
            for half in range(2):
                units.append((wob[:, :, half * 512:(half + 1) * 512], 4, 512, wo[b][:, :, half * 512:(half + 1) * 512], R_wo[b]))
        for n in range(12):
            units.append((wv[:, :, OFF_GATE + n * 256:OFF_GATE + (n + 1) * 256], KC, 256, wg[:, :, n * 256:(n + 1) * 256], R_wg))
        wov = w_out[l].rearrange("(kc p) n -> p kc n", p=128)
        for n in range(4):
            units.append((wov[:, :, n * 256:(n + 1) * 256], KC, 256, wout[:, :, n * 256:(n + 1) * 256], R_wout))
        pend = []
        for i, (src, a_, c_, dst, Rd) in enumerate(units):
            s_ = i % 3
            stv = stg[s_].rearrange("p k c -> p (k c)").rearrange("p (k c) -> p k c", c=c_)
            if len(pend) == 2:
                pstv, pdst, pRd, ps_ = pend.pop(0)
                sc.op("dve", [R_stg[ps_]], [pRd], lambda e, pstv=pstv, pdst=pdst: e.tensor_copy(out=pdst, in_=pstv))
            sc.dma(stv, src, R_stg[s_], [R_none], [R_stg[s_]])
            pend.append((stv, dst, Rd, s_))
            yield
        for (pstv, pdst, pRd, ps_) in pend:
            sc.op("dve", [R_stg[ps_]], [pRd], lambda e, pstv=pstv, pdst=pdst: e.tensor_copy(out=pdst, in_=pstv))
            yield

    mod_done = set()

    def gen_mod(l2):
        stg, R_stg = MW["stg"], MW["R_stg"]
        ct = A.alloc_top([128, 8], F32)
        cond = A.alloc_top([128, 8], F32)
        R_ct, R_cond = Res("bct"), Res("bcond")
        brow = [A.alloc_top([128, 256], F32) for _ in range(3)]
        mrow = [A.alloc_top([128, 256], F32) for _ in range(3)]
        R_brow = [Res(f"bbrow{i}") for i in range(3)]
        R_mrow = [Res(f"bmrow{i}") for i in range(3)]
        sc.dma(ct, c_t, R_ct, [R_none], [R_ct])
        sc.op("act", [R_ct], [R_cond], lambda e: e.activation(out=cond, in_=ct, func=AF.Silu))
        wv = w_ada[l2].rearrange("(kc p) n -> p kc n", p=128)
        pb = 7
        pend = []

        def finish(n, s_):
            def mm(e):
                for kc in range(KC):
                    ins = e.matmul(psf(pb)[0:1, 0:256], cond[:, kc:kc + 1], stg[s_][:, kc, :],
                                   start=(kc == 0), stop=(kc == KC - 1))
                return ins
            sc.op("pe", [R_cond, R_stg[s_]], [PB[pb]], mm)
            sc.op("dve", [PB[pb], R_brow[s_]], [R_mrow[s_]], lambda e: e.tensor_tensor(
                out=mrow[s_][0:1, :], in0=psf(pb)[0:1, 0:256], in1=brow[s_][0:1, :], op=ALU.add))
            sc.dma(modrow[l2:l2 + 1, n * 256:(n + 1) * 256], mrow[s_][0:1, :], R_mrow[s_], [R_mrow[s_]], [R_mod[l2][n // 2]])
        for n in range(24):
            s_ = n % 3
            if len(pend) == 2:
                finish(*pend.pop(0))
            sc.dma(stg[s_], wv[:, :, n * 256:(n + 1) * 256], R_stg[s_], [R_none], [R_stg[s_]])
            sc.dma(brow[s_][0:1, :], b_ada[l2:l2 + 1, n * 256:(n + 1) * 256], R_brow[s_], [R_none], [R_brow[s_]])
            pend.append((n, s_))
            yield
        for p in pend:
            finish(*p)
            yield
        mod_done.add(l2)

    def gen_bg(l):
        yield from gen_merge_weights(l)
        if l + 1 < depth:
            yield from gen_mod(l + 1)

    def start_bg(l):
        bg[0] = gen_bg(l)

    def phase_attn_softmax(b, scale):
        A.release(cbf_mark)
        m = A.mark()
        K = KQ[b]
        KP = 96 if b == 0 else K
        qsb = [A.alloc([128, S], BF16) for _ in range(2)]
        ksb = [A.alloc([128, S], BF16) for _ in range(2)]
        vsb = [A.alloc([128, NT, 128], BF16) for _ in range(2)]
        R_q = [Res(f"aq{i}") for i in range(2)]
        R_k = [Res(f"ak{i}") for i in range(2)]
        R_v = [Res(f"av{i}") for i in range(2)]
        R_kf = [Res(f"akf{i}") for i in range(2)]
        for i in range(2):
            if b == 0:
                sc.op("pool", [], [R_q[i]], lambda e, i=i: e.memset(qsb[i][64:96, :], 0.0))
                sc.op("pool", [], [R_k[i]], lambda e, i=i: e.memset(ksb[i][64:96, :], 0.0))
                sc.op("pool", [R_q[i]], [R_q[i]], lambda e, i=i: e.memset(qsb[i][64:70, :], 1.0))
                sc.op("pool", [R_k[i]], [R_k[i], R_kf[i]], lambda e, i=i: e.memset(ksb[i][64:70, :], 1.0))
        NPT = 4
        pt = [A.alloc([128, 512], BF16) for _ in range(NPT)]
        R_pt = [Res(f"pt{i}") for i in range(NPT)]
        rec = [A.alloc([64, 512], F32) for _ in range(3)]
        R_rec = [Res(f"rec{i}") for i in range(3)]
        yo = [A.alloc([64, 512], BF16) for _ in range(4)]
        R_yo = [Res(f"yo{i}") for i in range(4)]
        SBK = [0, 1, 2, 3]
        OB = [4, 5, 6]
        mask = masks[b]

        def load_head(h):
            s_ = h % 2
            q_, k_, v_ = qsb[s_], ksb[s_], vsb[s_]
            if b == 0:
                sc.dma(q_[0:67, :], qs[b][h, 0:67, :], R_q[s_], [R_qs[b][h], R_qF], [R_q[s_]])
                sc.dma(k_[0:64, :], ks[b][h, 0:64, :], R_k[s_], [R_ks[b][h]], [R_k[s_]])
                sc.dma(k_[67:70, :], ks[b][h, 67:70, :], R_kf[s_], [R_kF], [R_kf[s_]])
            else:
                sc.dma(q_[0:K, :], qs[b][h, :, :], R_q[s_], [R_qs[b][h]], [R_q[s_]])
                sc.dma(k_[0:K, :], ks[b][h, :, :], R_k[s_], [R_ks[b][h]], [R_k[s_]])
            sc.dma(v_, vs[b][h].rearrange("p (t d) -> p t d", d=128), R_v[s_], [R_vs[b][h]], [R_v[s_]])

        tiles = []
        chain = 0
        for h in range(8):
            for qc in range(NQ):
                nkb = 4 * qc + 4
                for kb in range(nkb):
                    j = kb - 4 * qc
                    tiles.append(dict(h=h, s=h % 2, qc=qc, kb=kb, j=j, c0=(128 * j if j > 0 else 0), first=(kb == 0),
                                      last=(kb == nkb - 1), ob=OB[chain % 3], os=chain % 3, ys=chain % 4,
                                      lasthead=(kb == nkb - 1 and qc == NQ - 1)))
                chain += 1
        T = len(tiles)

        def emit_qk(t):
            d = tiles[t]
            sbk = SBK[t % 4]
            q_, k_ = qsb[d["s"]], ksb[d["s"]]
            c0, kb, qc, j = d["c0"], d["kb"], d["qc"], d["j"]

            def f(e):
                ins = e.matmul(psf(sbk)[:, c0:512], k_[0:KP, kb * 128:(kb + 1) * 128],
                               q_[0:KP, qc * 512 + c0:(qc + 1) * 512], start=True, stop=(j < 0))
                if j >= 0:
                    ins = e.matmul(psf(sbk)[:, c0:c0 + 128], ident, mask, start=False, stop=True)
                return ins
            sc.op("pe", [R_q[d["s"]], R_k[d["s"]], R_kf[d["s"]], R_cbf], [PB[sbk]], f)

        def emit_exp(t):
            d = tiles[t]
            sbk = SBK[t % 4]
            pi = t % NPT
            c0 = d["c0"]
            sc.op("act", [PB[sbk]], [R_pt[pi]], lambda e: e.activation(
                out=pt[pi][:, c0:512], in_=psf(sbk)[:, c0:512], func=AF.Exp, scale=scale))

        def emit_pv(t):
            d = tiles[t]
            pi = t % NPT
            c0, kb, ob, os_ = d["c0"], d["kb"], d["ob"], d["os"]
            v_ = vsb[d["s"]]
            sc.op("pe", [R_v[d["s"]], R_pt[pi]], [PB[ob]], lambda e: e.matmul(
                psf(ob)[:, c0:512], v_[:, kb, :], pt[pi][:, c0:512], start=d["first"], stop=d["last"]))
            if d["last"]:
                h, qc = d["h"], d["qc"]
                sc.op("dve", [PB[ob]], [R_rec[os_]], lambda e: e.reciprocal(out=rec[os_], in_=psf(ob)[64:128, :]))
                ys = d["ys"]
                sc.op("dve", [PB[ob], R_rec[os_]], [R_yo[ys]], lambda e: e.tensor_tensor(
                    out=yo[ys], in0=psf(ob)[0:64, :], in1=rec[os_], op=ALU.mult))
                sc.dma(ybr[b, h * 64:(h + 1) * 64, qc * 512:(qc + 1) * 512], yo[ys], R_yo[ys], [R_yo[ys]],
                       [R_ybr[b][h][qc]])
            if d["lasthead"] and d["h"] + 2 < 8:
                load_head(d["h"] + 2)

        load_head(0)
        load_head(1)
        for t in range(-2, T):
            if t + 2 < T:
                emit_qk(t + 2)
            if 0 <= t + 1 < T:
                emit_exp(t + 1)
            if t >= 0:
                emit_pv(t)
            if t % 16 == 0:
                bg_step()
        A.release(persist_mark)
        sc.barrier()

    def phase_attn_sb():
        b = 2
        A.release(cbf_mark)
        m = A.mark()
        qsb = [A.alloc([64, S], BF16) for _ in range(2)]
        ksb = [A.alloc([64, S], BF16) for _ in range(2)]
        vsb = [A.alloc([128, NT, 64], BF16) for _ in range(2)]
        R_q = [Res(f"sq{i}") for i in range(2)]
        R_k = [Res(f"sk{i}") for i in range(2)]
        R_v = [Res(f"sv{i}") for i in range(2)]
        NB = 4
        eb = [A.alloc([128, 512], F32) for _ in range(NB)]
        spb = [A.alloc([128, 512], BF16) for _ in range(NB)]
        ecb = [A.alloc([128, 512], F32) for _ in range(NB)]
        ab = [A.alloc([128, 512], BF16) for _ in range(NB)]
        R_e = [Res(f"e{i}") for i in range(NB)]
        R_sp = [Res(f"sp{i}") for i in range(NB)]
        R_ec = [Res(f"ec{i}") for i in range(NB)]
        R_a = [Res(f"a{i}") for i in range(NB)]
        yo = [A.alloc([64, 512], BF16) for _ in range(4)]
        R_yo = [Res(f"syo{i}") for i in range(4)]
        ZB = [0, 1, 2]
        ACC = [3, 4]
        OB = [5, 6]
        mask = masks[2]

        def load_head(h):
            s_ = h % 2
            sc.dma(qsb[s_], qs[b][h, :, :], R_q[s_], [R_qs[b][h]], [R_q[s_]])
            sc.dma(ksb[s_], ks[b][h, :, :], R_k[s_], [R_ks[b][h]], [R_k[s_]])
            sc.dma(vsb[s_], vs[b][h].rearrange("p (t d) -> p t d", d=64), R_v[s_], [R_vs[b][h]], [R_v[s_]])

        tiles = []
        chain = 0
        for h in range(8):
            for qc in range(NQ):
                nkb = 4 * qc + 4
                for idx, kb in enumerate(range(nkb - 1, -1, -1)):
                    j = kb - 4 * qc
                    tiles.append(dict(h=h, s=h % 2, qc=qc, kb=kb, j=j, c0=(128 * j if j > 0 else 0), first=(idx == 0),
                                      last=(idx == nkb - 1), acc=ACC[chain % 2], ob=OB[chain % 2], os=chain % 4,
                                      lasthead=(idx == nkb - 1 and qc == NQ - 1)))
                chain += 1
        T = len(tiles)

        def st_qk(t):
            d = tiles[t]
            zb = ZB[t % 3]
            q_, k_ = qsb[d["s"]], ksb[d["s"]]
            c0, kb, qc, j = d["c0"], d["kb"], d["qc"], d["j"]

            def f(e):
                ins = e.matmul(psf(zb)[:, c0:512], k_[:, kb * 128:(kb + 1) * 128],
                               q_[:, qc * 512 + c0:(qc + 1) * 512], start=True, stop=(j < 0))
                if j >= 0:
                    ins = e.matmul(psf(zb)[:, c0:c0 + 128], ident, mask, start=False, stop=True)
                return ins
            sc.op("pe", [R_q[d["s"]], R_k[d["s"]], R_cbf], [PB[zb]], f)

        def st_act1(t):
            d = tiles[t]
            zb = ZB[t % 3]
            bi = t % NB
            c0 = d["c0"]
            sc.op("act", [PB[zb]], [R_e[bi]], lambda e: e.activation(
                out=eb[bi][:, c0:512], in_=psf(zb)[:, c0:512], func=AF.Exp, scale=0.125))
            sc.op("act", [R_e[bi]], [R_sp[bi]], lambda e: e.activation(
                out=spb[bi][:, c0:512], in_=eb[bi][:, c0:512], func=AF.Ln, bias=1.0, scale=1.0))

        def st_tri(t):
            d = tiles[t]
            bi = t % NB
            c0, acc = d["c0"], d["acc"]
            sc.op("pe", [R_sp[bi], R_cbf], [PB[acc]], lambda e: e.matmul(
                psf(acc)[:, c0:512], ntri, spb[bi][:, c0:512], start=d["first"], stop=True, skip_group_check=True))

        def st_expc(t):
            d = tiles[t]
            bi = t % NB
            c0, acc = d["c0"], d["acc"]
            sc.op("act", [PB[acc]], [R_ec[bi]], lambda e: e.activation(
                out=ecb[bi][:, c0:512], in_=psf(acc)[:, c0:512], func=AF.Exp))
            sc.op("pool", [R_e[bi], R_ec[bi]], [R_a[bi]], lambda e: e.tensor_tensor(
                out=ab[bi][:, c0:512], in0=eb[bi][:, c0:512], in1=ecb[bi][:, c0:512], op=ALU.mult))

        def st_u(t):
            d = tiles[t]
            if d["last"]:
                return
            bi = t % NB
            c0, acc = d["c0"], d["acc"]
            sc.op("pe", [R_sp[bi], R_cbf], [PB[acc]], lambda e: e.matmul(
                psf(acc)[:, c0:512], nu, spb[bi][:, c0:512], start=False, stop=True, skip_group_check=True))

        def st_pv(t):
            d = tiles[t]
            bi = t % NB
            c0, kb, ob, os_ = d["c0"], d["kb"], d["ob"], d["os"]
            v_ = vsb[d["s"]]
            sc.op("pe", [R_v[d["s"]], R_a[bi]], [PB[ob]], lambda e: e.matmul(
                psf(ob)[0:64, c0:512], v_[:, kb, :], ab[bi][:, c0:512], start=d["first"], stop=d["last"],
                skip_group_check=True))
            if d["last"]:
                h, qc = d["h"], d["qc"]
                sc.op("dve", [PB[ob]], [R_yo[os_]], lambda e: e.tensor_copy(out=yo[os_], in_=psf(ob)[0:64, :]))
                sc.dma(ybr[b, h * 64:(h + 1) * 64, qc * 512:(qc + 1) * 512], yo[os_], R_yo[os_], [R_yo[os_]],
                       [R_ybr[b][h][qc]])
            if d["lasthead"] and d["h"] + 2 < 8:
                load_head(d["h"] + 2)

        load_head(0)
        load_head(1)
        for t in range(-2, T + 2):
            if 0 <= t - 1 < T:
                st_u(t - 1)
            if 0 <= t < T:
                st_tri(t)
            if 0 <= t + 2 < T:
                st_qk(t + 2)
            if 0 <= t + 1 < T:
                st_act1(t + 1)
            if 0 <= t < T:
                st_expc(t)
            if 0 <= t - 2 < T:
                st_pv(t - 2)
            if t % 16 == 0:
                bg_step()
        A.release(persist_mark)
        sc.barrier()

    def phase_merge(l):
        A.release(cbf_mark)
        m = A.mark()
        if not MW:
            bg[0] = gen_bg(l)
        bg_drain()
        wo, wg, wout = MW["wo"], MW["wg"], MW["wout"]
        R_wo, R_wg, R_wout = MW["R_wo"], MW["R_wg"], MW["R_wout"]
        GT = A.alloc([128, D], F32)
        R_GT = Res("GT")
        bcast_load(GT, modrow[l:l + 1, 2 * D:3 * D], R_GT, [R_mod[l][4], R_mod[l][5]])
        yb = [A.alloc([128, 4, 512], BF16) for _ in range(3)]
        R_yb = [Res(f"yb{b}") for b in range(3)]
        uc = [A.alloc([128, KC, 512], BF16) for _ in range(2)]
        R_uc = [Res(f"uc{i}") for i in range(2)]
        mT = A.alloc([128, KC, 512], BF16)
        R_mT = Res("mT")
        sig = [A.alloc([128, 512], F32) for _ in range(2)]
        R_sig = [Res(f"sig{i}") for i in range(2)]
        accm = [A.alloc([128, 512], F32) for _ in range(2)]
        R_accm = [Res(f"accm{i}") for i in range(2)]
        tmpm = [A.alloc([128, 512], F32) for _ in range(2)]
        R_tmpm = [Res(f"tmpm{i}") for i in range(2)]
        xt = [A.alloc([128, D], F32) for _ in range(2)]
        R_xt = [Res(f"mxt{i}") for i in range(2)]
        xo = [A.alloc([128, D], F32) for _ in range(2)]
        R_xo = [Res(f"mxo{i}") for i in range(2)]
        cnt = 0
        xcnt = 0
        for qc in range(NQ):
            cols = slice(qc * 512, (qc + 1) * 512)
            us = qc % 2
            sc.dma(uc[us], uts[:, :, cols], R_uc[us], [R_uts[qc]], [R_uc[us]])
            for b in branches:
                src = ybr[b].rearrange("(f p) s -> p f s", p=128)[:, :, cols]
                sc.dma(yb[b], src, R_yb[b], [R_ybr[b][h][qc] for h in range(8)], [R_yb[b]])
            for nci in range(KC):
                a_ = nci % 2
                for b in branches:
                    p1 = 0 + (cnt % 2)
                    p2 = 2 + (cnt % 2)
                    g_ = cnt % 2
                    cnt += 1

                    def mm1(e, b=b, p1=p1, nci=nci):
                        for f in range(4):
                            ins = e.matmul(psf(p1), wo[b][:, f, nci * 128:(nci + 1) * 128], yb[b][:, f, :],
                                           start=(f == 0), stop=(f == 3))
                        return ins
                    sc.op("pe", [R_wo[b], R_yb[b]], [PB[p1]], mm1)

                    def mm2(e, b=b, p2=p2, nci=nci):
                        for kc in range(KC):
                            ins = e.matmul(psf(p2), wg[:, kc, b * D + nci * 128:b * D + (nci + 1) * 128], uc[us][:, kc, :],
                                           start=(kc == 0), stop=(kc == KC - 1))
                        return ins
                    sc.op("pe", [R_wg, R_uc[us]], [PB[p2]], mm2)
                    sc.op("act", [PB[p2]], [R_sig[g_]], lambda e, g_=g_, p2=p2: e.activation(
                        out=sig[g_], in_=psf(p2), func=AF.Sigmoid))
                    if len(branches) == 1:
                        sc.op("dve", [PB[p1], R_sig[g_]], [R_mT], lambda e, g_=g_, p1=p1, nci=nci: e.tensor_tensor(
                            out=mT[:, nci, :], in0=psf(p1), in1=sig[g_], op=ALU.mult))
                    elif b == branches[0]:
                        sc.op("dve", [PB[p1], R_sig[g_]], [R_accm[a_]], lambda e, g_=g_, p1=p1, a_=a_: e.tensor_tensor(
                            out=accm[a_], in0=psf(p1), in1=sig[g_], op=ALU.mult))
                    else:
                        sc.op("dve", [PB[p1], R_sig[g_]], [R_tmpm[g_]], lambda e, g_=g_, p1=p1: e.tensor_tensor(
                            out=tmpm[g_], in0=psf(p1), in1=sig[g_], op=ALU.mult))
                        if b != branches[-1]:
                            sc.op("pool", [R_tmpm[g_], R_accm[a_]], [R_accm[a_]], lambda e, g_=g_, a_=a_: e.tensor_tensor(
                                out=accm[a_], in0=accm[a_], in1=tmpm[g_], op=ALU.add))
                        else:
                            sc.op("pool", [R_tmpm[g_], R_accm[a_]], [R_mT], lambda e, g_=g_, a_=a_, nci=nci: e.tensor_tensor(
                                out=mT[:, nci, :], in0=accm[a_], in1=tmpm[g_], op=ALU.add))
            for j in range(4):
                t = qc * 4 + j
                xs = xcnt % 2
                xcnt += 1
                srcx = x_in if l == 0 else y
                sc.dma(xt[xs], srcx[t * 128:(t + 1) * 128, :], R_xt[xs], [R_none if l == 0 else R_y[t]], [R_xt[xs]])
                for n in range(2):
                    po = 4 + n

                    def mmo(e, j=j, n=n, po=po):
                        for kc in range(KC):
                            ins = e.matmul(psf(po), mT[:, kc, j * 128:(j + 1) * 128], wout[:, kc, n * 512:(n + 1) * 512],
                                           start=(kc == 0), stop=(kc == KC - 1))
                        return ins
                    sc.op("pe", [R_mT, R_wout], [PB[po]], mmo)
                    sc.op("dve", [PB[po], R_GT], [R_xo[xs]], lambda e, n=n, po=po, xs=xs: e.tensor_tensor(
                        out=xo[xs][:, n * 512:(n + 1) * 512], in0=psf(po), in1=GT[:, n * 512:(n + 1) * 512], op=ALU.mult))
                sc.op("dve", [R_xo[xs], R_xt[xs]], [R_xo[xs]], lambda e, xs=xs: e.tensor_tensor(
                    out=xo[xs], in0=xo[xs], in1=xt[xs], op=ALU.add))
                sc.dma(y[t * 128:(t + 1) * 128, :], xo[xs], R_xo[xs], [R_xo[xs]], [R_y[t]])
        A.release(persist_mark)
        A.htop = A.n
        MW.clear()
        sc.barrier()

    def phase_ffn(l, final):
        import os as _os
        FT = int(_os.environ.get("FFNFT", "256"))
        A.release(cbf_mark)
        wg = A.alloc([128, KC, DFF], BF16)
        wu = A.alloc([128, KC, DFF], BF16)
        wd = A.alloc([128, FC, D], BF16)
        R_wg, R_wu, R_wd = Res("fwg"), Res("fwu"), Res("fwd")
        m = A.mark()
        stg = [A.alloc([128, KC, 256], F32) for _ in range(2)]
        R_stg = [Res(f"fstg{i}") for i in range(2)]
        ld = 0
        for (wsrc, dst, Rd) in ((w_gate, wg, R_wg), (w_up, wu, R_wu)):
            wvv = wsrc[l].rearrange("(kc p) n -> p kc n", p=128)
            for c in range(0, DFF, 256):
                s_ = ld % 2
                ld += 1
                sc.dma(stg[s_], wvv[:, :, c:c + 256], R_stg[s_], [R_none], [R_stg[s_]])
                cast(dst[:, :, c:c + 256], stg[s_], [R_stg[s_]], [Rd])
        wdv = w_down[l].rearrange("(fc p) n -> p fc n", p=128)
        for f0 in range(0, FC, 2):
            s_ = ld % 2
            ld += 1
            stv = stg[s_].rearrange("p k c -> p (k c)").rearrange("p (k c) -> p k c", c=1024)
            sc.dma(stv, wdv[:, f0:f0 + 2, :], R_stg[s_], [R_none], [R_stg[s_]])
            cast(wd[:, f0:f0 + 2, :], stv, [R_stg[s_]], [R_wd])
        sc.barrier()
        A.release(m)
        import os as _os
        kffn = int(_os.environ.get("KFFN", "9"))
        if kffn <= 1:
            A.release(persist_mark)
            return
        G, SH, R_G, R_SH = load_mod_tiles(l, 3, 4, g_ffn[l:l + 1, :])
        sc.barrier()
        A.release(A.mark() - D * 4)
        GT = A.alloc([128, D], F32)
        R_GT = Res("fGT")
        bcast_load(GT, modrow[l:l + 1, 5 * D:6 * D], R_GT, [R_mod[l][10], R_mod[l][11]])
        if final:
            GF = A.alloc([128, D], F32)
            R_GF = Res("GF")
            bcast_load(GF, g_final[0:1, :], R_GF, [R_none])
        nb = NormBufs(2, 1)
        xr = [A.alloc([128, D], F32) for _ in range(1)]
        R_xr = [Res(f"xr{i}") for i in range(1)]
        ufT2 = [A.alloc([128, KC, FT], BF16) for _ in range(2)]
        R_ufT2 = [Res(f"ufT{i}") for i in range(2)]
        hT = A.alloc([128, FC, FT], BF16)
        R_hT = Res("hT")
        sg = [A.alloc([128, FT], F32) for _ in range(2)]
        R_sg = [Res(f"fsg{i}") for i in range(2)]
        xo = [A.alloc([128, D], F32) for _ in range(2)]
        R_xo = [Res(f"fxo{i}") for i in range(2)]
        fss = [A.alloc([128, 2], F32) for _ in range(2)]
        R_fss = [Res(f"fss{i}") for i in range(2)]
        cnt = 0
        xcnt = 0
        NJ = FT // 128
        if kffn <= 2:
            sc.barrier()
            A.release(persist_mark)
            return
        ysrc = y
        NCH = S // FT

        def pro_load(ch):
            for j in range(NJ):
                t = ch * NJ + j
                sc.dma(nb.xt[j], ysrc[t * 128:(t + 1) * 128, :], nb.R_xt[j], [R_y[t]], [nb.R_xt[j]])

        def pro_norm(ch):
            for j in range(NJ):
                norm_tile(nb, j, G, SH, R_G, R_SH)

        def pro_T(ch):
            for j in range(NJ):
                transpose_tile(nb.u[j], nb.R_u[j], j % 2, ufT2[ch % 2][:, :, j * 128:(j + 1) * 128], R_ufT2[ch % 2],
                               "act" if j % 2 == 0 else "dve")

        def gu(ch, f):
            nonlocal cnt
            ufT, R_ufT = ufT2[ch % 2], R_ufT2[ch % 2]
            pg = 2 + (cnt % 2)
            pu = 4 + (cnt % 2)
            g_ = cnt % 2
            cnt += 1

            def mmg(e):
                for kc in range(KC):
                    ins = e.matmul(psf(pg)[:, 0:FT], wg[:, kc, f * 128:(f + 1) * 128], ufT[:, kc, :],
                                   start=(kc == 0), stop=(kc == KC - 1))
                return ins
            sc.op("pe", [R_wg, R_ufT], [PB[pg]], mmg)

            def mmu(e):
                for kc in range(KC):
                    ins = e.matmul(psf(pu)[:, 0:FT], wu[:, kc, f * 128:(f + 1) * 128], ufT[:, kc, :],
                                   start=(kc == 0), stop=(kc == KC - 1))
                return ins
            sc.op("pe", [R_wu, R_ufT], [PB[pu]], mmu)
            sc.op("act", [PB[pg]], [R_sg[g_]], lambda e: e.activation(out=sg[g_], in_=psf(pg)[:, 0:FT], func=AF.Silu))
            sc.op("dve", [PB[pu], R_sg[g_]], [R_hT], lambda e: e.tensor_tensor(
                out=hT[:, f, :], in0=psf(pu)[:, 0:FT], in1=sg[g_], op=ALU.mult))

        pro_load(0)
        pro_norm(0)
        pro_T(0)
        for ch in range(NCH):
            if ch + 1 < NCH:
                pro_load(ch + 1)
            for f in range(FC // 2):
                gu(ch, f)
            if ch + 1 < NCH:
                pro_norm(ch + 1)
                pro_T(ch + 1)
            for f in range(FC // 2, FC):
                gu(ch, f)
            for j in range(NJ):
                t = ch * NJ + j
                xs = xcnt % 2
                xcnt += 1
                sc.dma(xr[0], ysrc[t * 128:(t + 1) * 128, :], R_xr[0], [R_y[t]], [R_xr[0]])
                for n in range(2):
                    po = 6 + n

                    def mmd(e, j=j, n=n, po=po):
                        for f in range(FC):
                            ins = e.matmul(psf(po), hT[:, f, j * 128:(j + 1) * 128], wd[:, f, n * 512:(n + 1) * 512],
                                           start=(f == 0), stop=(f == FC - 1))
                        return ins
                    sc.op("pe", [R_hT, R_wd], [PB[po]], mmd)
                    sc.op("dve", [PB[po], R_GT], [R_xo[xs]], lambda e, n=n, po=po, xs=xs: e.tensor_tensor(
                        out=xo[xs][:, n * 512:(n + 1) * 512], in0=psf(po), in1=GT[:, n * 512:(n + 1) * 512], op=ALU.mult))
                sc.op("dve", [R_xo[xs], R_xr[0]], [R_xo[xs]], lambda e, xs=xs: e.tensor_tensor(
                    out=xo[xs], in0=xo[xs], in1=xr[0], op=ALU.add))
                if final:
                    ss = fss[xs]
                    Rss = R_fss[xs]
                    sc.op("act", [R_xo[xs]], [nb.R_junk, Rss], lambda e, xs=xs, ss=ss: e.activation(
                        out=nb.junk, in_=xo[xs], func=AF.Square, accum_out=ss[:, 0:1]))
                    sc.op("dve", [Rss], [Rss], lambda e, ss=ss: e.tensor_scalar(
                        out=ss[:, 0:1], in0=ss[:, 0:1], scalar1=1.0 / D, scalar2=EPS, op0=ALU.mult, op1=ALU.add))
                    sc.op("act", [Rss], [Rss], lambda e, ss=ss: e.activation(out=ss[:, 0:1], in_=ss[:, 0:1], func=AF.Ln))
                    sc.op("act", [Rss], [Rss], lambda e, ss=ss: e.activation(
                        out=ss[:, 0:1], in_=ss[:, 0:1], func=AF.Exp, scale=-0.5))
                    sc.op("dve", [R_xo[xs], Rss, R_GF], [R_xo[xs]], lambda e, xs=xs, ss=ss: e.scalar_tensor_tensor(
                        out=xo[xs], in0=xo[xs], scalar=ss[:, 0:1], in1=GF, op0=ALU.mult, op1=ALU.mult))
                sc.dma(y[t * 128:(t + 1) * 128, :], xo[xs], R_xo[xs], [R_xo[xs]], [R_y[t]])
        A.release(persist_mark)
        sc.barrier()

    cbf_mark = A.mark() - KC * S * 2
    assert cbf_mark >= 0

    import os as _os
    stop = int(_os.environ.get("KSTOP", "999"))
    plist = []
    for l in range(depth):
        plist.append(lambda l=l: phase_mod(l))
        plist.append(lambda l=l: phase_norm_attn(l))
        if 0 in branches:
            plist.append(lambda l=l: phase_proj_qkv(l, 0, OFF_FOXQ))
            plist.append(lambda l=l: phase_fox_F(l))
        if 1 in branches:
            plist.append(lambda l=l: phase_proj_mla(l))
        if 2 in branches:
            plist.append(lambda l=l: phase_proj_qkv(l, 2, OFF_SB))
        if 0 in branches:
            plist.append(lambda l=l: phase_attn_softmax(0, 0.125))
        if 1 in branches:
            plist.append(lambda l=l: phase_attn_softmax(1, float(96 ** -0.5)))
        if 2 in branches:
            plist.append(lambda l=l: start_bg(l))
            plist.append(lambda l=l: phase_attn_sb())
        plist.append(lambda l=l: phase_merge(l))
        plist.append(lambda l=l: phase_ffn(l, final=(l == depth - 1)))
    if _os.environ.get("ONLYFFN"):
        plist = [lambda: phase_mod(0), lambda: phase_ffn(0, final=bool(int(_os.environ.get("FFNFINAL", "1"))))]
    for i, p in enumerate(plist):
        if i >= stop:
            break
        p()
    sc.barrier(engines=("sp",))
    build.info = dict(peak=A.peak, nwait=sc.nwait, cnt=dict(sc.cnt), nsem=sc.nsem)
    return nc


_CACHE = {}


def _prep_core(inputs, bidx, S):
    c = np.asarray(inputs["c"], np.float32)[bidx]
    m = {
        "x": np.ascontiguousarray(np.asarray(inputs["x"], np.float32)[bidx, :S]),
        "c_t": np.ascontiguousarray(c.reshape(8, 128).T),
        "pos": np.ascontiguousarray(np.asarray(inputs["positions"], np.int32)[bidx, :S].reshape(1, S)),
        "g_final": np.ascontiguousarray(np.asarray(inputs["g_final"], np.float32).reshape(1, D)),
        "consts": make_consts(),
    }
    for k in ("g_mix", "w_ada", "b_ada", "w_in", "b_fox_f", "g_mla_q", "w_mla_uq", "g_mla_kv", "w_mla_ukv",
              "w_o_fox", "w_o_mla", "w_o_sb", "w_out", "g_ffn", "w_ffn_gate", "w_ffn_up", "w_ffn_down"):
        m[k] = np.ascontiguousarray(np.asarray(inputs[k], np.float32))
    return m


def kernel(**inputs):
    x = np.asarray(inputs["x"])
    B, S, _ = x.shape
    key = (S,)
    if key not in _CACHE:
        _CACHE[key] = build(S)
    nc = _CACHE[key]
    in_maps = [_prep_core(inputs, b, S) for b in range(B)]
    res = run_bass_kernel_spmd(nc, in_maps, core_ids=list(range(B)))
    out = np.stack([np.asarray(r["y"], np.float32) for r in res.results], axis=0)
    return out
```
